# Optimizing a Trainium2 kernel written in Bass

```python
import jax, jax.numpy as jnp
from jax import lax
import numpy as np

D_MODEL = 1024
BATCH = 8
SEQ = 4096
DEPTH = 1

POOL_GROUPS = 4
POOL_GROUP_DIM = 128
POOL_DIM = POOL_GROUPS * POOL_GROUP_DIM
POOL_WINDOWS = (2, 4, 8, 16)
DN_HEADS = 8
DN_HEAD_DIM = 128
DN_DIM = DN_HEADS * DN_HEAD_DIM
SHORT_CONV = 5
CHUNK = 64
FFN_DIM = 2816
FFN_CONV = 3
NORM_EPS = 1e-6
L2_EPS = 1e-6

IN_SIZES = (POOL_DIM, DN_DIM, DN_DIM, DN_DIM, DN_DIM,
            DN_HEADS, DN_HEADS, DN_HEADS, DN_HEADS, D_MODEL, D_MODEL)
IN_DIM = sum(IN_SIZES)
SPLIT_POINTS = tuple(int(s) for s in np.cumsum(IN_SIZES)[:-1])

kernel_name = "hybrid_pool_deltanet_convffn_encoder"


def rmsnorm(x, w):
    xf = x.astype(jnp.float32)
    y = xf * lax.rsqrt(jnp.mean(xf * xf, axis=-1, keepdims=True) + NORM_EPS)
    return (y * w.astype(jnp.float32)).astype(x.dtype)


def dwconv_centered(x, w, b=None):
    K = w.shape[0]
    S = x.shape[1]
    pad = K // 2
    xp = jnp.pad(x, ((0, 0), (pad, pad), (0, 0)))
    y = w[0] * xp[:, 0:S]
    for t in range(1, K):
        y = y + w[t] * xp[:, t:t + S]
    if b is not None:
        y = y + b
    return y


def multiscale_pool(p):
    B, S = p.shape[0], p.shape[1]
    pf = p.astype(jnp.float32).reshape(B, S, POOL_GROUPS, POOL_GROUP_DIM)
    c = jnp.concatenate([jnp.zeros((B, 1, POOL_GROUPS, POOL_GROUP_DIM), jnp.float32),
                         jnp.cumsum(pf, axis=1)], axis=1)
    pos = jnp.arange(S)
    outs = []
    for g, w in enumerate(POOL_WINDOWS):
        start = jnp.clip(pos - w // 2, 0, S)
        end = jnp.clip(pos + w - w // 2, 0, S)
        cg = c[:, :, g]
        wsum = jnp.take(cg, end, axis=1) - jnp.take(cg, start, axis=1)
        cnt = (end - start).astype(jnp.float32)[None, :, None]
        outs.append(wsum / cnt - pf[:, :, g])
    return jnp.stack(outs, axis=2)


def l2norm(t):
    return t * lax.rsqrt(jnp.sum(t * t, axis=-1, keepdims=True) + L2_EPS)


def gated_delta_rule(q, k, v, g, beta):
    B, S, H, dk = q.shape
    dv = v.shape[-1]
    N = S // CHUNK

    def chunks(t):
        t = jnp.swapaxes(t, 1, 2)
        return t.reshape((B, H, N, CHUNK) + t.shape[3:])

    q, k, v, g, beta = chunks(q), chunks(k), chunks(v), chunks(g), chunks(beta)
    G = jnp.cumsum(g, axis=-1)
    idx = jnp.arange(CHUNK)
    incl = idx[:, None] >= idx[None, :]
    strict = idx[:, None] > idx[None, :]
    diff = G[..., :, None] - G[..., None, :]
    decay = jnp.where(incl, jnp.exp(jnp.minimum(diff, 0.0)), 0.0)
    kb = k * beta[..., None]
    vb = v * beta[..., None]
    L = jnp.where(strict, jnp.einsum('bhnid,bhnjd->bhnij', kb, k) * decay, 0.0)
    eye = jnp.eye(CHUNK, dtype=L.dtype)
    rhs = jnp.concatenate([kb * jnp.exp(G)[..., None], vb], axis=-1)
    sol = lax.linalg.triangular_solve(L + eye, rhs, left_side=True, lower=True,
                                      unit_diagonal=True)
    w_c, u_c = sol[..., :dk], sol[..., dk:]
    attn = jnp.einsum('bhnid,bhnjd->bhnij', q, k) * decay
    qg = q * jnp.exp(G)[..., None]
    G_last = G[..., -1:]
    kd = k * jnp.exp(G_last - G)[..., None]
    gl = jnp.exp(G_last[..., 0])
    xs = (jnp.moveaxis(qg, 2, 0), jnp.moveaxis(attn, 2, 0), jnp.moveaxis(w_c, 2, 0),
          jnp.moveaxis(u_c, 2, 0), jnp.moveaxis(kd, 2, 0), jnp.moveaxis(gl, 2, 0))

    def step(state, inp):
        qg_n, attn_n, w_n, u_n, kd_n, gl_n = inp
        v_new = u_n - jnp.einsum('bhcd,bhde->bhce', w_n, state)
        o = (jnp.einsum('bhcd,bhde->bhce', qg_n, state)
             + jnp.einsum('bhij,bhje->bhie', attn_n, v_new))
        state = state * gl_n[..., None, None] + jnp.einsum('bhcd,bhce->bhde', kd_n, v_new)
        return state, o

    s0 = jnp.zeros((B, H, dk, dv), jnp.float32)
    _, o = lax.scan(step, s0, xs)
    o = jnp.moveaxis(o, 0, 2).reshape(B, H, S, dv)
    return jnp.swapaxes(o, 1, 2)


def setup_inputs(seed: int = 0) -> dict:
    key = jax.random.key(seed)
    ks = jax.random.split(key, 20)
    f32 = jnp.float32
    nrm = lambda k, shape, scale: jax.random.normal(k, shape, f32) * scale
    dt = jnp.exp(jax.random.uniform(ks[7], (DEPTH, 2, DN_HEADS), f32,
                                    minval=float(np.log(1e-3)), maxval=float(np.log(1e-1))))
    return {
        "x": nrm(ks[0], (BATCH, SEQ, D_MODEL), 1.0),
        "norm1_w": 1.0 + nrm(ks[1], (DEPTH, D_MODEL), 0.02),
        "w_in": nrm(ks[2], (DEPTH, D_MODEL, IN_DIM), D_MODEL ** -0.5),
        "pool_w": nrm(ks[3], (DEPTH, POOL_GROUPS, POOL_GROUP_DIM, POOL_GROUP_DIM), POOL_GROUP_DIM ** -0.5),
        "pool_scale": 1.0 + nrm(ks[4], (DEPTH, POOL_DIM), 0.02),
        "pool_out": nrm(ks[5], (DEPTH, POOL_DIM, D_MODEL), POOL_DIM ** -0.5),
        "qkv_conv_w": nrm(ks[6], (DEPTH, SHORT_CONV, 3 * DN_DIM), SHORT_CONV ** -0.5),
        "a_log": jnp.log(jax.random.uniform(ks[8], (DEPTH, 2, DN_HEADS), f32, minval=1.0, maxval=16.0)),
        "dt_bias": dt + jnp.log(-jnp.expm1(-dt)),
        "dn_norm_w": 1.0 + nrm(ks[9], (DEPTH, DN_HEAD_DIM), 0.02),
        "dn_out": nrm(ks[10], (DEPTH, DN_DIM, D_MODEL), DN_DIM ** -0.5),
        "w_o": nrm(ks[11], (DEPTH, D_MODEL, D_MODEL), D_MODEL ** -0.5),
        "norm2_w": 1.0 + nrm(ks[12], (DEPTH, D_MODEL), 0.02),
        "ffn_up": nrm(ks[13], (DEPTH, D_MODEL, 2 * FFN_DIM), D_MODEL ** -0.5),
        "ffn_conv_w": nrm(ks[14], (DEPTH, FFN_CONV, 2 * FFN_DIM), FFN_CONV ** -0.5),
        "ffn_conv_b": nrm(ks[15], (DEPTH, 2 * FFN_DIM), 0.02),
        "ffn_down": nrm(ks[16], (DEPTH, FFN_DIM, D_MODEL), FFN_DIM ** -0.5),
        "final_norm_w": 1.0 + nrm(ks[17], (D_MODEL,), 0.02),
    }


def reference(x, norm1_w, w_in, pool_w, pool_scale, pool_out, qkv_conv_w, a_log, dt_bias,
              dn_norm_w, dn_out, w_o, norm2_w, ffn_up, ffn_conv_w, ffn_conv_b, ffn_down,
              final_norm_w):
    B, S, _ = x.shape
    dt_x = x.dtype
    f32 = jnp.float32
    for l in range(DEPTH):
        h = rmsnorm(x, norm1_w[l])
        proj = h @ w_in[l]
        p, q, k, v, z, bf, bb, af, ab, g_pool, g_dn = jnp.split(proj, SPLIT_POINTS, axis=-1)

        pm = multiscale_pool(p)
        pm = jnp.einsum('bsgc,gcd->bsgd', pm, pool_w[l].astype(f32)).reshape(B, S, POOL_DIM)
        pm = (pm * pool_scale[l].astype(f32)).astype(dt_x)
        y_pool = pm @ pool_out[l]

        qkv = jax.nn.silu(dwconv_centered(jnp.concatenate([q, k, v], axis=-1), qkv_conv_w[l]))
        qc, kc, vc = jnp.split(qkv.astype(f32), 3, axis=-1)
        qc = l2norm(qc.reshape(B, S, DN_HEADS, DN_HEAD_DIM)) * (DN_HEAD_DIM ** -0.5)
        kc = l2norm(kc.reshape(B, S, DN_HEADS, DN_HEAD_DIM))
        vc = vc.reshape(B, S, DN_HEADS, DN_HEAD_DIM)
        a_l = a_log[l].astype(f32)
        dtb = dt_bias[l].astype(f32)
        beta_f = jax.nn.sigmoid(bf.astype(f32))
        beta_b = jax.nn.sigmoid(bb.astype(f32))
        g_f = -jnp.exp(a_l[0]) * jax.nn.softplus(af.astype(f32) + dtb[0])
        g_b = -jnp.exp(a_l[1]) * jax.nn.softplus(ab.astype(f32) + dtb[1])
        o_f = gated_delta_rule(qc, kc, vc, g_f, beta_f)
        flip = lambda t: jnp.flip(t, axis=1)
        o_b = flip(gated_delta_rule(flip(qc), flip(kc), flip(vc), flip(g_b), flip(beta_b)))
        o = o_f + o_b
        o = o * lax.rsqrt(jnp.mean(o * o, axis=-1, keepdims=True) + NORM_EPS)
        o = o * dn_norm_w[l].astype(f32) * jax.nn.silu(z.astype(f32).reshape(B, S, DN_HEADS, DN_HEAD_DIM))
        y_dn = o.reshape(B, S, DN_DIM).astype(dt_x) @ dn_out[l]

        merged = jax.nn.sigmoid(g_pool) * y_pool + jax.nn.sigmoid(g_dn) * y_dn
        x = x + merged @ w_o[l]

        h2 = rmsnorm(x, norm2_w[l])
        u = dwconv_centered(h2 @ ffn_up[l], ffn_conv_w[l], ffn_conv_b[l])
        gate, val = jnp.split(u, 2, axis=-1)
        x = x + (jax.nn.silu(gate) * val) @ ffn_down[l]
    return rmsnorm(x, final_norm_w)
```

```python
import numpy as np
from contextlib import ExitStack
import concourse.bass as bass
import concourse.mybir as mybir
from concourse.bass_utils import run_bass_kernel_spmd

F32 = mybir.dt.float32
BF16 = mybir.dt.bfloat16
AF = mybir.ActivationFunctionType
ALU = mybir.AluOpType
AX = mybir.AxisListType


class Buf:
    __slots__ = ("name", "w", "r", "excl")

    def __init__(self, name, excl=False):
        self.name = name
        self.excl = excl
        self.w = None
        self.r = {}


class Prog:
    ENG = ("pe", "act", "dve", "pool", "sp")

    def __init__(self, nc, es):
        self.nc = nc
        self.es = es
        self.h = {"pe": nc.tensor, "act": nc.scalar, "dve": nc.vector, "pool": nc.gpsimd, "sp": nc.sync}
        self.sem = {e: es.enter_context(nc.semaphore("s_" + e)) for e in self.ENG}
        self.cnt = {e: 0 for e in self.ENG}
        self.waited = {e: {} for e in self.ENG}
        self.q = {e: [] for e in self.ENG}
        self.dma_sems = []
        self.n_inst = 0
        self.n_wait = 0

    def new_dma_sem(self, name):
        s = self.es.enter_context(self.nc.semaphore(name))
        d = {"sem": s, "val": 0, "name": name}
        self.dma_sems.append(d)
        return d

    def _need(self, eng, deps, key, sem, val):
        if eng == "pe" and key == "pe":
            return
        if val > self.waited[eng].get(key, 0):
            prev = deps.get(key)
            if prev is None or prev[1] < val:
                deps[key] = (sem, val)

    def _collect(self, eng, reads, writes):
        deps = {}
        for b in reads:
            if b.w is not None:
                self._need(eng, deps, *b.w)
            if b.excl:
                for k, (sem, v) in b.r.items():
                    if k != eng:
                        self._need(eng, deps, k, sem, v)
        for b in writes:
            if b.w is not None:
                self._need(eng, deps, *b.w)
            for k, (sem, v) in b.r.items():
                self._need(eng, deps, k, sem, v)
        return deps

    def _emit_waits(self, eng, deps):
        h = self.h[eng]
        for key, (sem, val) in deps.items():
            self.waited[eng][key] = val
            self.n_wait += 1
            self.q[eng].append(lambda h=h, sem=sem, val=val: h.wait_ge(sem, val))

    def _mark(self, key, sem, val, reads, writes):
        for b in reads:
            p = b.r.get(key)
            if p is None or p[1] < val:
                b.r[key] = (sem, val)
        for b in writes:
            b.w = (key, sem, val)
            b.r = {}

    def op(self, eng, fn, reads=(), writes=()):
        self._emit_waits(eng, self._collect(eng, reads, writes))
        self.cnt[eng] += 1
        idx = self.cnt[eng]
        sem = self.sem[eng]
        h = self.h[eng]
        self.q[eng].append(lambda: fn(h).then_inc(sem, 1))
        self._mark(eng, sem, idx, reads, writes)
        self.n_inst += 1
        return idx

    def dma(self, eng, dsem, out, in_, reads=(), writes=()):
        self._emit_waits(eng, self._collect(eng, reads, writes))
        dsem["val"] += 16
        val = dsem["val"]
        h = self.h[eng]
        s = dsem["sem"]
        self.q[eng].append(lambda: h.dma_start(out=out, in_=in_).then_inc(s, 16))
        self._mark("dma:" + dsem["name"], s, val, reads, writes)
        self.n_inst += 1

    def wait_all_dma(self, eng):
        h = self.h[eng]
        for d in self.dma_sems:
            if d["val"] > 0:
                self.q[eng].append(lambda h=h, s=d["sem"], v=d["val"]: h.wait_ge(s, v))

    def barrier(self):
        for e in self.ENG:
            h = self.h[e]
            for e2 in self.ENG:
                if e2 != e and self.cnt[e2] > self.waited[e].get(e2, 0):
                    v = self.cnt[e2]
                    self.waited[e][e2] = v
                    self.q[e].append(lambda h=h, s=self.sem[e2], v=v: h.wait_ge(s, v))
            if self.cnt[e] > self.waited[e].get(e, 0):
                v = self.cnt[e]
                self.waited[e][e] = v
                self.q[e].append(lambda h=h, s=self.sem[e], v=v: h.wait_ge(s, v))
            for d in self.dma_sems:
                k = "dma:" + d["name"]
                if d["val"] > self.waited[e].get(k, 0):
                    v = d["val"]
                    self.waited[e][k] = v
                    self.q[e].append(lambda h=h, s=d["sem"], v=v: h.wait_ge(s, v))

    def flush(self):
        block = self.es.enter_context(self.nc.Block())
        q = self.q

        @block.tensor
        def _(t):
            for f in q["pe"]:
                f()

        @block.scalar
        def _(t):
            for f in q["act"]:
                f()

        @block.vector
        def _(t):
            for f in q["dve"]:
                f()

        @block.gpsimd
        def _(t):
            for f in q["pool"]:
                f()

        @block.sync
        def _(t):
            for f in q["sp"]:
                f()

    def mm(self, out, lhsT, rhs, start=True, stop=True):
        return self.op("pe", lambda t: t.matmul(out.ap, lhsT=lhsT.ap, rhs=rhs.ap, start=start, stop=stop),
                       reads=lhsT.b + rhs.b + ([] if start else out.b), writes=out.b)

    def tr(self, out, in_, ident):
        return self.op("pe", lambda t: t.transpose(out=out.ap, in_=in_.ap, identity=ident.ap),
                       reads=in_.b + ident.b, writes=out.b)

    def act(self, out, in_, func, bias=None, scale=1.0, accum=None, eng="act"):
        rd = list(in_.b)
        kw = {}
        if bias is not None:
            if isinstance(bias, T):
                rd += bias.b
                kw["bias"] = bias.ap
            else:
                kw["bias"] = bias
        if isinstance(scale, T):
            rd += scale.b
            kw["scale"] = scale.ap
        else:
            kw["scale"] = scale
        wr = list(out.b)
        if accum is not None:
            wr += accum.b
            kw["accum_out"] = accum.ap
        return self.op("act", lambda a: a.activation(out=out.ap, in_=in_.ap, func=func, **kw), reads=rd, writes=wr)

    def tt(self, eng, out, a, b, op):
        return self.op(eng, lambda v: v.tensor_tensor(out=out.ap, in0=a.ap, in1=b.ap, op=op),
                       reads=a.b + b.b, writes=out.b)

    def ts(self, eng, out, a, s1, op0, s2=None, op1=None):
        rd = list(a.b)
        v1 = s1
        if isinstance(s1, T):
            rd += s1.b
            v1 = s1.ap
        v2 = s2
        if isinstance(s2, T):
            rd += s2.b
            v2 = s2.ap
        if op1 is None:
            return self.op(eng, lambda v: v.tensor_scalar(out=out.ap, in0=a.ap, scalar1=v1, scalar2=None, op0=op0),
                           reads=rd, writes=out.b)
        return self.op(eng, lambda v: v.tensor_scalar(out=out.ap, in0=a.ap, scalar1=v1, scalar2=v2, op0=op0, op1=op1),
                       reads=rd, writes=out.b)

    def stt(self, eng, out, a, s, b, op0, op1):
        rd = a.b + b.b
        sv = s
        if isinstance(s, T):
            rd = rd + s.b
            sv = s.ap
        return self.op(eng, lambda v: v.scalar_tensor_tensor(out=out.ap, in0=a.ap, scalar=sv, in1=b.ap, op0=op0, op1=op1),
                       reads=rd, writes=out.b)

    def copy(self, eng, out, in_):
        if eng == "act":
            return self.op("act", lambda a: a.copy(out=out.ap, in_=in_.ap), reads=in_.b, writes=out.b)
        return self.op(eng, lambda v: v.tensor_copy(out=out.ap, in_=in_.ap), reads=in_.b, writes=out.b)

    def memset(self, eng, out, val):
        return self.op(eng, lambda v: v.memset(out.ap, val), writes=out.b)

    def ld(self, eng, dsem, out, in_ap):
        return self.dma(eng, dsem, out.ap, in_ap, writes=out.b)

    def st(self, eng, dsem, out_ap, in_):
        return self.dma(eng, dsem, out_ap, in_.ap, reads=in_.b)


class T:
    __slots__ = ("ap", "b")

    def __init__(self, ap, b):
        self.ap = ap
        self.b = b if isinstance(b, list) else [b]

    def __getitem__(self, k):
        return T(self.ap[k], self.b)

    def wb(self, b):
        return T(self.ap, b)


class Arena:
    def __init__(self, ap, nelem):
        self.ap = ap
        self.n = nelem
        self.off = 0

    def mark(self):
        return self.off

    def reset(self, m):
        self.off = m

    def alloc(self, shape, dt, name):
        n = int(np.prod(shape[1:]))
        units = n * (2 if dt == F32 else 1)
        if self.off % 2:
            self.off += 1
        a = self.ap[:, self.off:self.off + units]
        self.off += units
        assert self.off <= self.n, ("SBUF arena overflow", name, self.off, self.n)
        if dt == F32:
            a = a.bitcast(F32)
        if len(shape) == 3:
            a = a.rearrange("p (a b) -> p a b", a=shape[1])
        elif len(shape) == 4:
            a = a.rearrange("p (a b c) -> p a b c", a=shape[1], b=shape[2])
        return T(a, Buf(name))


S = 4096
D = 1024
NT = S // 128
NKT = D // 128
NH = 8
IN_DIM = 6688
OFF_P, OFF_Q, OFF_K, OFF_V, OFF_Z, OFF_BG, OFF_GP, OFF_GD = 0, 512, 1536, 2560, 3584, 4608, 4640, 5664
FFN = 2816
NFT = FFN // 128
POOL_WINDOWS = (2, 4, 8, 16)
EPS = 1e-6

MASKS = ("ident", "ones", "up_s", "lo_s", "up_i", "lo_i", "blk", "off_lo", "off_up")
BANDS = ("C", "L", "R", "CF", "CL")


def host_constants():
    p = np.arange(128)[:, None]
    f = np.arange(128)[None, :]
    m = {}
    m["ident"] = (p == f)
    m["ones"] = np.ones((128, 128), bool)
    m["up_s"] = f > p
    m["lo_s"] = f < p
    m["up_i"] = f >= p
    m["lo_i"] = f <= p
    m["blk"] = (p < 64) == (f < 64)
    m["off_lo"] = (p >= 64) & (f < 64)
    m["off_up"] = (p < 64) & (f >= 64)
    cols = [m[k].astype(np.float32) for k in MASKS]
    for w in POOL_WINDOWS:
        lo, hi = w // 2, w - 1 - w // 2
        mats = {k: np.zeros((128, 128), np.float32) for k in BANDS}
        for i in range(128):
            for tp in range(i - lo, i + hi + 1):
                if 0 <= tp < 128:
                    mats["C"][tp, i] += 1.0 / w
                elif tp < 0:
                    mats["L"][tp + 128, i] += 1.0 / w
                else:
                    mats["R"][tp - 128, i] += 1.0 / w
                a0 = max(i - lo, 0)
                cntf = (i + hi) - a0 + 1
                if 0 <= tp < 128:
                    mats["CF"][tp, i] += 1.0 / cntf
                b1 = min(i + hi, 127)
                cntl = b1 - (i - lo) + 1
                if 0 <= tp < 128:
                    mats["CL"][tp, i] += 1.0 / cntl
        for i in range(128):
            a0 = max(i - lo, 0)
            cnt = (i + hi) - a0 + 1
            for tp in range(a0, min(i + hi, 127) + 1):
                mats["CF"][tp, i] = 1.0 / cnt
            b1 = min(i + hi, 127)
            cnt = b1 - (i - lo) + 1
            for tp in range(max(i - lo, 0), b1 + 1):
                mats["CL"][tp, i] = 1.0 / cnt
        for k in ("C", "CF", "CL"):
            mats[k] -= np.eye(128, dtype=np.float32)
        cols += [mats[k] for k in BANDS]
    return np.ascontiguousarray(np.concatenate(cols, axis=1))


NCST = (len(MASKS) + 4 * len(BANDS)) * 128


class Ring:
    def __init__(self, items):
        self.items = items
        self.i = 0

    def next(self):
        it = self.items[self.i % len(self.items)]
        self.i += 1
        return it


def build_nc(dbg=(), nheads=NH, phase3=True, stop=None):
    nc = bass.Bass("TRN2", target_bir_lowering=False)

    def din(name, shape, dtype=F32):
        return nc.dram_tensor(name, list(shape), dtype, kind="ExternalInput").ap()

    x = din("x", [S, D])
    w_in = din("w_in", [D, IN_DIM])
    pool_w = din("pool_w", [4, 128, 128])
    pool_out = din("pool_out", [512, D])
    dn_out = din("dn_out", [D, D])
    w_o = din("w_o", [D, D])
    ffn_up = din("ffn_up", [D, 2 * FFN])
    ffn_down = din("ffn_down", [FFN, D])
    cw_d = din("cw", [128, 24 * 5])
    fcw_d = din("fcw", [128, 44 * 3])
    fcb_d = din("fcb", [128, 44])
    psc_d = din("psc", [128, 4])
    nw1_d = din("nw1", [128, D])
    nw2_d = din("nw2", [128, D])
    nwf_d = din("nwf", [128, D])
    dnw_d = din("dnw", [128, 128])
    alog_d = din("alog", [128, 16])
    dtb_d = din("dtb", [128, 16])
    cst_d = din("cst", [128, NCST])
    out = nc.dram_tensor("out", [S, D], F32, kind="ExternalOutput").ap()
    onT_d = nc.dram_tensor("onT_d", [NH, 128, S], BF16).ap()
    x1_d = nc.dram_tensor("x1_d", [S, D], F32).ap()
    dbg_out = {}
    for name, shape in dbg:
        dbg_out[name] = nc.dram_tensor("dbg_" + name, list(shape), F32, kind="ExternalOutput").ap()

    es = ExitStack()
    with es:
        E = es.enter_context
        P = Prog(nc, es)
        NU = 106400
        arena_t = E(nc.sbuf_tensor("arena", [128, NU], BF16))
        A = Arena(arena_t[:, :], NU)
        pbig_t = [E(nc.psum_tensor("pbig%d" % i, [128, 512], F32)) for i in range(2)]
        psm_t = [E(nc.psum_tensor("psm%d" % i, [128, 512], F32)) for i in range(4)]
        pbf_t = [E(nc.psum_tensor("pbf%d" % i, [128, 1024], BF16)) for i in range(2)]

        def psum_sets(tag, sm_on_big=False):
            bigb = [Buf("pbig%s%d" % (tag, i), True) for i in range(len(pbig_t))]
            big = Ring([T(t[:, :], bigb[i]) for i, t in enumerate(pbig_t)])
            smb = [Buf("psm%s%d" % (tag, i), True) for i in range(len(psm_t))]
            sm_t, sm_bufs = list(psm_t), list(smb)
            if sm_on_big:
                sm_t, sm_bufs = sm_t + list(pbig_t), sm_bufs + bigb
            sm = Ring([T(t[:, j * 128:(j + 1) * 128], sm_bufs[i]) for j in range(4) for i, t in enumerate(sm_t)])
            sm2 = Ring([(T(t[:, j * 256:(j + 1) * 256], sm_bufs[i]), T(t[:, j * 256:j * 256 + 128], sm_bufs[i]),
                         T(t[:, j * 256 + 128:(j + 1) * 256], sm_bufs[i])) for j in range(2) for i, t in enumerate(sm_t)])
            psum_sets.sm2 = sm2
            bfbuf = [Buf("pbf%s%d" % (tag, i), True) for i in range(len(pbf_t))]
            bfs = Ring([T(t[:, j * 128:(j + 1) * 128], bfbuf[i]) for j in range(8) for i, t in enumerate(pbf_t)])
            bfb = Ring([T(t[:, :], bfbuf[i]) for i, t in enumerate(pbf_t)])
            return big, sm, bfb, bfs

        dsem_i = [0]

        def dsem(name):
            dsem_i[0] += 1
            return P.new_dma_sem("%s_%d" % (name, dsem_i[0]))

        d_misc = dsem("misc")
        d_miscp = dsem("miscp")
        d_dbg = dsem("dbg")

        hT = A.alloc([128, NKT, S], BF16, "hT")
        hT_b = [Buf("hT%d" % i) for i in range(8)]

        def hT_blk(k, blk):
            return T(hT.ap[:, k, blk * 512:(blk + 1) * 512], hT_b[blk])

        def hT_tile(k, t):
            return T(hT.ap[:, k, t * 128:(t + 1) * 128], hT_b[t // 4])

        cstb = A.alloc([128, len(MASKS) * 128], BF16, "cstb")
        cstf = A.alloc([128, 4 * 128], F32, "cstf")
        P.ld("pool", d_miscp, cstb, cst_d[:, 0:len(MASKS) * 128])
        mb = {k: cstb[:, i * 128:(i + 1) * 128] for i, k in enumerate(MASKS)}
        for j, k in enumerate(("ident", "ones", "up_i", "lo_i")):
            i = MASKS.index(k)
            P.ld("sp", d_misc, cstf[:, j * 128:(j + 1) * 128], cst_d[:, i * 128:(i + 1) * 128])
        ident_f, ones_f, upi_f, loi_f = [cstf[:, j * 128:(j + 1) * 128] for j in range(4)]
        cw = A.alloc([128, 24, 5], F32, "cw")
        P.ld("sp", d_misc, cw, cw_d.rearrange("p (a b) -> p a b", a=24))
        fcw = A.alloc([128, 44, 3], F32, "fcw")
        P.ld("sp", d_misc, fcw, fcw_d.rearrange("p (a b) -> p a b", a=44))
        fcb = A.alloc([128, 44], F32, "fcb")
        P.ld("sp", d_misc, fcb, fcb_d[:, :])
        psc = A.alloc([128, 4], F32, "psc")
        P.ld("sp", d_misc, psc, psc_d[:, :])
        dnw = A.alloc([128, 128], F32, "dnw")
        P.ld("sp", d_misc, dnw, dnw_d[:, :])
        alog = A.alloc([128, 16], F32, "alog")
        P.ld("sp", d_misc, alog, alog_d[:, :])
        dtb = A.alloc([128, 16], F32, "dtb")
        P.ld("sp", d_misc, dtb, dtb_d[:, :])
        epsc = A.alloc([128, 1], F32, "epsc")
        P.memset("dve", epsc, EPS)
        mhalf = A.alloc([128, 1], F32, "mhalf")
        P.memset("dve", mhalf, -0.5)
        base_mark = A.mark()

        def dump(name, src):
            if name in dbg_out:
                P.st("sp", d_dbg, dbg_out[name], src)

        def rmsnorm_to_bf16(xt, nwB, hb, junk, st):
            P.act(junk, xt, AF.Square, accum=st[:, 0:1])
            P.act(st[:, 1:2], st[:, 0:1], AF.Sqrt, bias=epsc[:, 0:1], scale=1.0 / D)
            P.op("dve", lambda v: v.reciprocal(out=st.ap[:, 2:3], in_=st.ap[:, 1:2]), reads=st.b, writes=st.b)
            P.stt("dve", hb, xt, st[:, 2:3], nwB, ALU.mult, ALU.mult)

        def transpose_to_hT(hb, t, pbank, eng):
            pv = T(pbank.ap.rearrange("p (a b) -> p a b", a=8), pbank.b)
            for k in range(NKT):
                P.tr(pv[:, k, :], hb[:, k * 128:(k + 1) * 128], mb["ident"])
            dst = T(hT.ap[:, :, t * 128:(t + 1) * 128], hT_b[t // 4])
            P.copy(eng, dst, pv)

        big, sm, bfb, bfs = psum_sets("p1")
        nwB = A.alloc([128, D], F32, "nwB")
        P.ld("sp", d_misc, nwB, nw1_d[:, :])
        xts = Ring([A.alloc([128, D], F32, "xt%d" % i) for i in range(3)])
        xds = Ring([dsem("x") for i in range(3)])
        hbs = Ring([A.alloc([128, D], BF16, "hb%d" % i) for i in range(2)])
        junk = A.alloc([128, D], BF16, "junk")
        sts = Ring([A.alloc([128, 4], F32, "st%d" % i) for i in range(3)])
        for t in range(NT):
            xt = xts.next()
            P.ld("sp", xds.next(), xt, x[t * 128:(t + 1) * 128, :])
            hb = hbs.next()
            rmsnorm_to_bf16(xt, nwB, hb, junk, sts.next())
            transpose_to_hT(hb, t, bfb.next(), "act" if t % 2 else "dve")
        if "hT" in dbg_out:
            tmpf = A.alloc([128, S], F32, "dbg_hT")
            for k in range(NKT):
                P.copy("dve", tmpf, T(hT.ap[:, k, :], hT_b))
                P.st("sp", d_dbg, dbg_out["hT"][k], tmpf)
        if stop == 1:
            P.barrier()
            P.wait_all_dma("sp")
            P.flush()
            return nc
        P.barrier()
        A.reset(base_mark)

        big, sm, bfb, bfs = psum_sets("p20")
        GSH = [128, 2, NT, NH]
        beta = A.alloc(GSH, F32, "beta")
        Gc = A.alloc(GSH, F32, "Gc")
        GB = A.alloc(GSH, F32, "GB")
        eG = A.alloc(GSH, F32, "eG")
        bG = A.alloc(GSH, F32, "bG")
        eGd = A.alloc(GSH, F32, "eGd")
        gl = A.alloc(GSH, F32, "gl")
        invb = A.alloc(GSH, F32, "invb")
        p2_mark = A.mark()
        raw = A.alloc([128, NT, 32], F32, "raw")
        gg = A.alloc(GSH, F32, "gg")
        lnb = A.alloc(GSH, F32, "lnb")
        Gtot = A.alloc(GSH, F32, "Gtot")
        negA = A.alloc([128, 16], F32, "negA")
        wbg = A.alloc([128, NKT, 32], BF16, "wbg")
        P.ld("pool", d_miscp, wbg, w_in[:, OFF_BG:OFF_BG + 32].rearrange("(k p) c -> p k c", p=128))
        for t in range(NT):
            ps = sm.next()
            for k in range(NKT):
                P.mm(ps[:, 0:32], hT_tile(k, t), wbg[:, k, :], start=(k == 0), stop=(k == NKT - 1))
            P.copy("act" if t % 2 else "dve", raw[:, t, :], ps[:, 0:32])
        raw5 = T(raw.ap.rearrange("p t (k d h) -> p k d t h", k=2, d=2), raw.b)
        P.act(beta, raw5[:, 0], AF.Sigmoid)
        P.act(lnb, beta, AF.Ln)
        P.op("dve", lambda v: v.reciprocal(out=invb.ap, in_=beta.ap), reads=beta.b, writes=invb.b)
        P.act(negA, alog, AF.Exp)
        P.ts("dve", negA, negA, -1.0, ALU.mult)
        dtb4 = T(dtb.ap.rearrange("p (d h) -> p d h", d=2).unsqueeze(2).broadcast_to(GSH), dtb.b)
        negA4 = T(negA.ap.rearrange("p (d h) -> p d h", d=2).unsqueeze(2).broadcast_to(GSH), negA.b)
        P.tt("dve", gg, raw5[:, 1], dtb4, ALU.add)
        P.act(gg, gg, AF.Exp)
        P.act(gg, gg, AF.Ln, bias=1.0)
        P.tt("dve", gg, gg, negA4, ALU.mult)
        ggf = T(gg.ap.rearrange("p d t h -> p d (t h)"), gg.b)
        pc = big.next()
        P.mm(pc[:, 0:256], upi_f, ggf[:, 0], True, True)
        P.mm(pc[:, 256:512], loi_f, ggf[:, 1], True, True)
        P.copy("dve", T(Gc.ap.rearrange("p d t h -> p (d t h)"), Gc.b), pc)
        pt_ = big.next()
        P.mm(pt_, ones_f, T(gg.ap.rearrange("p d t h -> p (d t h)"), gg.b), True, True)
        P.copy("act", T(Gtot.ap.rearrange("p d t h -> p (d t h)"), Gtot.b), pt_)
        P.tt("dve", GB, Gc, lnb, ALU.add)
        P.act(eG, Gc, AF.Exp)
        P.tt("dve", bG, beta, eG, ALU.mult)
        P.tt("dve", eGd, Gtot, Gc, ALU.subtract)
        P.act(eGd, eGd, AF.Exp)
        P.act(gl, Gtot, AF.Exp)
        dump("raw", T(raw.ap.rearrange("p t c -> p (t c)"), raw.b))
        for nm, tt_ in (("beta", beta), ("Gc", Gc), ("gg", gg)):
            dump(nm, T(tt_.ap.rearrange("p d t h -> p (d t h)"), tt_.b))
        if stop == 2:
            P.barrier()
            P.wait_all_dma("sp")
            P.flush()
            return nc
        hT_d = nc.dram_tensor("hT_d", [8, 128, NKT, 512], BF16).ap()
        d_hst = dsem("hst")
        for blk in range(8):
            P.dma("sp", d_hst, hT_d[blk], hT.ap[:, :, blk * 512:(blk + 1) * 512], reads=[hT_b[blk]])
        P.barrier()
        A.reset(p2_mark)
        A2 = Arena(arena_t[:, 0:NKT * S], NKT * S)

        big, sm, bfb, bfs = psum_sets("p2", sm_on_big=True)
        PRE0 = 2
        NS = 4
        qkv_all = [A2.alloc([128, 3 * S], BF16, "qkv_all%d" % i) for i in range(2)]
        qkv_b = [[[Buf("%s%d_%d" % (nm, si, i)) for i in range(8)] for nm in ("qT", "kT", "vT")] for si in range(2)]
        zs_l = [A2.alloc([128, NT, 128], BF16, "zs%d" % i) for i in range(2)]
        zs_b = [[Buf("zs%d_%d" % (si, i)) for i in range(NT)] for si in range(2)]
        o_dir = [A.alloc([128, NT, 128], BF16, "o_f"), A.alloc([128, NT, 128], BF16, "o_b")]
        o_b = [[Buf("o%d_%d" % (d, i)) for i in range(NT)] for d in range(2)]
        d_w = [dsem("w%d" % i) for i in range(4)]
        d_on = dsem("on")
        Sst = [A.alloc([128, 128], F32, "S%d" % d) for d in range(2)]
        Sbf = [A.alloc([128, 128], BF16, "Sbf%d" % d) for d in range(2)]
        ssn = A.alloc([128, NT], F32, "ssn")
        junk128 = A.alloc([128, 128], BF16, "junk128")
        pre = A.alloc([128, S + 4], F32, "pre")
        pre_b = [Buf("pre%d" % i) for i in range(8)]
        accs = Ring([A.alloc([128, 512], F32, "acc%d" % i) for i in range(1)])
        sqs = Ring([A.alloc([128, 512], BF16, "sq%d" % i) for i in range(2)])
        rinvs = Ring([A.alloc([128, 512], F32, "rinv%d" % i) for i in range(1)])
        wqkvz = [A.alloc([128, NKT, 128], BF16, "w%s" % nm) for nm in "qkvz"]
        hring = Ring([(A.alloc([128, NKT, 512], BF16, "hblk%d" % i), dsem("hblk%d" % i)) for i in range(2)])

        def mk_shared(i):
            d = {}
            for nm in ("kbg", "kd", "vb", "KK", "QK"):
                d[nm] = A.alloc([128, 128], BF16, "%s_%d" % (nm, i))
            return d

        def mk_dir(i):
            d = {}
            for nm in ("u", "tmpo"):
                d[nm] = A.alloc([128, 128], F32, "%s_%d" % (nm, i))
            for nm in ("E2", "N", "Nd", "attnT", "Ld", "Lo", "MTa", "MTb", "Xd", "nT1",
                       "XT", "wT", "vnew"):
                d[nm] = A.alloc([128, 128], BF16, "%s_%d" % (nm, i))
            for ab in "ab":
                pair = A.alloc([128, 256], BF16, "MQ%s_%d" % (ab, i))
                d["M" + ab] = T(pair.ap[:, 0:128], Buf("M%s_%d" % (ab, i)))
                d["Q" + ab] = T(pair.ap[:, 128:256], Buf("Q%s_%d" % (ab, i)))
                d["MQ" + ab] = T(pair.ap, d["M" + ab].b + d["Q" + ab].b)
            return d

        shared_slots = [[mk_shared(d * NS + i) for i in range(NS)] for d in range(2)]
        dir_slots = [[mk_dir(d * NS + i) for i in range(NS)] for d in range(2)]
        cur = {"si": 0}

        def col(arr, d, t, h):
            return T(arr.ap[:, d, t, h:h + 1], arr.b)

        def qkv_T(si, Xi, lo, hi, bufs):
            return T(qkv_all[si].ap[:, Xi * S + lo: Xi * S + hi], bufs)

        def tile_of(Xi, t):
            si = cur["si"]
            return qkv_T(si, Xi, t * 128, (t + 1) * 128, [qkv_b[si][Xi][t // 4]])

        hstate = {"pending": None}

        def issue_hblk(blk):
            slot, ds_ = hring.next()
            P.dma("sp", ds_, slot.ap, hT_d[blk], writes=slot.b)
            hstate["pending"] = (blk, slot)

        def load_hblk(blk, nxt=None):
            if hstate["pending"] is None or hstate["pending"][0] != blk:
                issue_hblk(blk)
            slot = hstate["pending"][1]
            hstate["pending"] = None
            if nxt is not None:
                issue_hblk(nxt)
            return slot

        def conv_block(h, Xi, blk, si):
            acc = accs.next()
            rb = [pre_b[b_] for b_ in (blk - 1, blk, blk + 1) if 0 <= b_ < 8]
            for s_ in range(5):
                src = T(pre.ap[:, blk * 512 + s_: blk * 512 + s_ + 512], rb)
                wcol = T(cw.ap[:, Xi * 8 + h, s_:s_ + 1], cw.b)
                if s_ == 0:
                    P.ts("dve", acc, src, wcol, ALU.mult)
                else:
                    P.stt("dve", acc, src, wcol, acc, ALU.mult, ALU.add)
                if s_ % 2 == 1:
                    yield
            dst = qkv_T(si, Xi, blk * 512, (blk + 1) * 512, [qkv_b[si][Xi][blk]])
            rinv = rinvs.next()
            P.act(rinv, acc, AF.Exp, scale=-1.0)
            P.act(rinv, rinv, AF.Ln, bias=1.0)
            P.act(rinv, rinv, AF.Exp, scale=-1.0)
            yield
            if Xi == 2:
                P.tt("dve", dst, acc, rinv, ALU.mult)
                yield
            else:
                P.tt("dve", acc, acc, rinv, ALU.mult)
                yield
                sq = sqs.next()
                P.tt("pool", sq, acc, acc, ALU.mult)
                yield
                ps = big.next()
                P.mm(ps, mb["ones"], sq, True, True)
                P.act(rinv, ps, AF.Ln, bias=epsc[:, 0:1], scale=1.0)
                yield
                P.act(rinv, rinv, AF.Exp, scale=-0.5)
                yield
                if Xi == 0:
                    P.stt("dve", dst, acc, float(128 ** -0.5), rinv, ALU.mult, ALU.mult)
                else:
                    P.tt("dve", dst, acc, rinv, ALU.mult)
                yield

        def proj_gens(h, si):
            st = {"a": 0, "b": 0}

            def gen_a():
                for Xi, woff in enumerate((OFF_Q, OFF_K, OFF_V, OFF_Z)):
                    P.ld("pool", d_w[Xi], wqkvz[Xi],
                         w_in[:, woff + h * 128: woff + (h + 1) * 128].rearrange("(k p) c -> p k c", p=128))
                yield
                for Xi in range(3):
                    wX = wqkvz[Xi]
                    for blk in range(8):
                        i = 8 * Xi + blk
                        need = 0 if Xi == 0 else min(i - 6, 8 * Xi)
                        while st["b"] < need:
                            yield (lambda need=need: st["b"] >= need)
                        hb_ = load_hblk(blk, (blk + 1) % 8)
                        ps = big.next()
                        for k in range(NKT):
                            P.mm(ps, wX[:, k, :], hb_[:, k, :], start=(k == 0), stop=(k == NKT - 1))
                        P.copy("act", T(pre.ap[:, PRE0 + blk * 512: PRE0 + (blk + 1) * 512], [pre_b[blk]]), ps)
                        st["a"] = i + 1
                        yield
                while st["b"] < 24:
                    yield (lambda: st["b"] >= 24)
                wz = wqkvz[3]
                zflat = zs_l[si].ap.rearrange("p t e -> p (t e)")
                for blk in range(8):
                    hb_ = load_hblk(blk, blk + 1 if blk < 7 else None)
                    ps = big.next()
                    for k in range(NKT):
                        P.mm(ps, wz[:, k, :], hb_[:, k, :], start=(k == 0), stop=(k == NKT - 1))
                    acc = accs.next()
                    rinv = rinvs.next()
                    P.act(rinv, ps, AF.Exp, scale=-1.0)
                    P.act(acc, ps, AF.Copy)
                    yield
                    P.act(rinv, rinv, AF.Ln, bias=1.0)
                    P.act(rinv, rinv, AF.Exp, scale=-1.0)
                    yield
                    P.tt("dve", T(zflat[:, blk * 512:(blk + 1) * 512], zs_b[si][blk * 4:(blk + 1) * 4]), acc, rinv,
                         ALU.mult)
                    yield

            def gen_b():
                for Xi in range(3):
                    for blk in range(8):
                        i = 8 * Xi + blk
                        need = min(i + 2, 8 * Xi + 8)
                        while st["a"] < need:
                            yield (lambda need=need: st["a"] >= need)
                        yield from conv_block(h, Xi, blk, si)
                        st["b"] = i + 1

            return [gen_a(), gen_b()]

        def prep_tile(h, t, d, sh):
            ud = "up" if d == 0 else "lo"
            kt, qt, vt = tile_of(1, t), tile_of(0, t), tile_of(2, t)
            pk = bfs.next()
            P.tr(pk, kt, mb["ident"])
            P.act(sh["kbg"], pk, AF.Copy, scale=col(bG, d, t, h))
            P.act(sh["kd"], pk, AF.Copy, scale=col(eGd, d, t, h))
            yield
            pv = bfs.next()
            P.tr(pv, vt, mb["ident"])
            P.act(sh["vb"], pv, AF.Copy, scale=col(beta, d, t, h))
            yield
            pboth, pqk, pkk = psum_sets.sm2.next()
            si_ = cur["si"]
            qk_rhs = T(qkv_all[si_].ap[:, 0:2 * S].rearrange("p (x s) -> p x s", x=2)[:, :, t * 128:(t + 1) * 128],
                       [qkv_b[si_][0][t // 4], qkv_b[si_][1][t // 4]])
            P.mm(T(pboth.ap.rearrange("p (x s) -> p x s", x=2), pboth.b), kt, qk_rhs, True, True)
            P.tt("dve", sh["QK"], pqk, mb[ud + "_i"], ALU.mult)
            P.tt("dve", sh["KK"], pkk, mb[ud + "_s"], ALU.mult)
            yield

        def local_gen(h, t, d, sh, ds):
            dg_, te_ = ds["tmpo"], ds["u"]
            P.ts("dve", dg_, ident_f, col(GB, d, t, h), ALU.mult)
            yield
            prow = sm.next()
            P.mm(prow, ones_f, dg_, True, True)
            P.ts("dve", te_, prow, col(Gc, d, t, h), ALU.subtract, 0.0, ALU.min)
            yield
            P.act(ds["E2"], te_, AF.Exp)
            yield
            P.tt("pool", ds["N"], sh["KK"], ds["E2"], ALU.mult)
            yield
            P.tt("pool", ds["attnT"], sh["QK"], ds["E2"], ALU.mult)
            P.tt("pool", ds["Nd"], ds["N"], mb["blk"], ALU.mult)
            yield
            pl = bfs.next()
            P.tr(pl, ds["N"], mb["ident"])
            P.tt("dve", ds["Ld"], pl, mb["blk"], ALU.mult)
            P.tt("dve", ds["Lo"], pl, mb["off_lo" if d == 0 else "off_up"], ALU.mult)
            yield
            P.tt("pool", ds["Qa"], mb["ident"], ds["Nd"], ALU.subtract)
            yield
            M, MT, Q = ds["Nd"], ds["Ld"], ds["Qa"]
            pa = sm.next()
            P.mm(pa, MT, M, True, True)
            P.copy("act", ds["Ma"], pa)
            yield
            pb = sm.next()
            P.mm(pb, M, MT, True, True)
            P.copy("act", ds["MTa"], pb)
            yield
            M, MT = ds["Ma"], ds["MTa"]
            for lvl in range(1, 6):
                nM, nMT = (ds["Mb"], ds["MTb"]) if lvl % 2 else (ds["Ma"], ds["MTa"])
                nQ = ds["Qb"] if lvl % 2 else ds["Qa"]
                if lvl < 5:
                    pboth, p_m, p_q = psum_sets.sm2.next()
                    rhs = ds["MQa"] if (lvl % 2 == 1) else ds["MQb"]
                    P.mm(pboth, MT, rhs, True, True)
                    P.copy("act", nM, p_m)
                    P.tt("dve", nQ, Q, p_q, ALU.add)
                    yield
                    pb = sm.next()
                    P.mm(pb, M, MT, True, True)
                    P.copy("act" if lvl % 2 else "dve", nMT, pb)
                    yield
                else:
                    pq = sm.next()
                    P.mm(pq, MT, Q, True, True)
                    P.tt("dve", nQ, Q, pq, ALU.add)
                    yield
                M, MT, Q = nM, nMT, nQ
            Y = Q
            px = bfs.next()
            P.tr(px, Y, mb["ident"])
            P.copy("act", ds["Xd"], px)
            yield
            p1 = sm.next()
            P.mm(p1, ds["Lo"], Y, True, True)
            P.act(ds["nT1"], p1, AF.Copy, scale=-1.0)
            yield
            p2 = sm.next()
            P.mm(p2, ds["Xd"], ds["nT1"], True, True)
            P.tt("dve", ds["XT"], Y, p2, ALU.add)
            yield
            pw = sm.next()
            P.mm(pw, sh["kbg"], ds["XT"], True, True)
            P.copy("act", ds["wT"], pw)
            yield
            pu = sm.next()
            P.mm(pu, ds["XT"], sh["vb"], True, True)
            P.copy("act", ds["u"], pu)
            yield

        def scan_gen(h, t, d, sh, ds):
            qt = tile_of(0, t)
            pa = sm.next()
            P.mm(pa, ds["wT"], Sbf[d], True, True)
            po1 = sm.next()
            P.mm(po1, qt, Sbf[d], True, True)
            P.tt("dve", ds["vnew"], ds["u"], pa, ALU.subtract)
            P.act(ds["tmpo"], po1, AF.Copy, scale=col(eG, d, t, h))
            yield
            psn = sm.next()
            P.mm(psn, sh["kd"], ds["vnew"], True, True)
            P.stt("dve", Sst[d], Sst[d], T(gl.ap[:, d, t, h:h + 1], gl.b), psn, ALU.mult, ALU.add)
            yield
            P.copy("act", Sbf[d], Sst[d])
            yield
            po2 = sm.next()
            P.mm(po2, ds["attnT"], ds["vnew"], True, True)
            P.stt("dve", o_dir[d][:, t, :].wb(o_b[d][t]), po2, col(invb, d, t, h), ds["tmpo"], ALU.mult, ALU.add)
            yield

        def run_sched(gens):
            active = [(g, None) for g in gens]
            while active:
                progressed = False
                nxt = []
                for g, cond in active:
                    if cond is not None and not cond():
                        nxt.append((g, cond))
                        continue
                    try:
                        c = next(g)
                        nxt.append((g, c))
                    except StopIteration:
                        pass
                    progressed = True
                active = nxt
                assert progressed or not active, "scheduler deadlock"

        def chain_gens(h):
            gens = []
            for d in range(2):
                order = list(range(NT)) if d == 0 else list(range(NT - 1, -1, -1))
                state = {"local_done": [False] * NT, "scan_done": 0}

                def worker(w, d=d, order=order, state=state):
                    for idx in range(w, NT, NS):
                        t = order[idx]
                        while state["scan_done"] < idx - NS + 1:
                            yield (lambda idx=idx: state["scan_done"] >= idx - NS + 1)
                        sh = shared_slots[d][idx % NS]
                        ds = dir_slots[d][idx % NS]
                        yield from prep_tile(h, t, d, sh)
                        yield from local_gen(h, t, d, sh, ds)
                        state["local_done"][idx] = True

                def scanner(d=d, order=order, state=state):
                    for idx, t in enumerate(order):
                        while not state["local_done"][idx]:
                            yield (lambda idx=idx: state["local_done"][idx])
                        yield from scan_gen(h, t, d, shared_slots[d][idx % NS], dir_slots[d][idx % NS])
                        state["scan_done"] = idx + 1

                gens += [worker(w) for w in range(NS)] + [scanner()]
            return gens

        def norm_stage(h, si):
            kvb = qkv_b[si][1] + qkv_b[si][2]
            osum = T(qkv_all[si].ap[:, S:3 * S].bitcast(F32).rearrange("p (t e) -> p t e", t=NT), kvb)
            onT_sb = T(qkv_all[si].ap[:, 0:S], qkv_b[si][0])
            zst = T(zs_l[si].ap, zs_b[si])
            P.tt("dve", osum, o_dir[0].wb(o_b[0]), o_dir[1].wb(o_b[1]), ALU.add)
            for t in range(NT):
                P.act(junk128, osum[:, t, :], AF.Square, accum=ssn[:, t:t + 1])
            P.act(ssn, ssn, AF.Ln, bias=epsc[:, 0:1], scale=1.0 / 128)
            P.act(ssn, ssn, AF.Exp, scale=-0.5)
            P.tt("dve", osum, osum, T(ssn.ap.unsqueeze(2).broadcast_to([128, NT, 128]), ssn.b), ALU.mult)
            P.tt("pool", osum, osum, T(dnw.ap.unsqueeze(1).broadcast_to([128, NT, 128]), dnw.b), ALU.mult)
            if h == 0:
                dump("on0", T(osum.ap.rearrange("p t e -> p (t e)"), osum.b))
            on_tok = o_dir[0].wb(o_b[0])
            P.copy("act", on_tok, osum)
            zflat = T(zs_l[si].ap.rearrange("p t e -> p (t e)"), zs_b[si])
            for t8 in range(4):
                pbk = bfb.next()
                pv = T(pbk.ap.rearrange("p (a b) -> p a b", a=8), pbk.b)
                for j in range(8):
                    P.tr(pv[:, j, :], on_tok[:, t8 * 8 + j, :], mb["ident"])
                P.tt("dve", T(onT_sb.ap[:, t8 * 1024:(t8 + 1) * 1024], onT_sb.b), pbk,
                     zflat[:, t8 * 1024:(t8 + 1) * 1024], ALU.mult)
            P.st("sp", d_on, onT_d[h], onT_sb)

        P.memset("pool", T(pre.ap[:, 0:PRE0], [pre_b[0]]), 0.0)
        P.memset("pool", T(pre.ap[:, PRE0 + S:PRE0 + S + 2], [pre_b[7]]), 0.0)
        run_sched(proj_gens(0, 0))
        for h in range(nheads):
            si = h % 2
            cur["si"] = si
            for d in range(2):
                P.memset("dve", Sst[d], 0.0)
                P.memset("pool", Sbf[d], 0.0)
            gens = chain_gens(h)
            if h + 1 < nheads:
                gens += proj_gens(h + 1, 1 - si)
            run_sched(gens)
            norm_stage(h, si)
        P.barrier()
        d_hld = dsem("hld")
        for blk in range(8):
            P.dma("sp", d_hld, hT.ap[:, :, blk * 512:(blk + 1) * 512], hT_d[blk], writes=[hT_b[blk]])
        P.barrier()
        A.reset(base_mark)
        if phase3:
            build_phase3(nc, P, A, locals())
        P.barrier()
        P.wait_all_dma("sp")
        P.flush()
    return nc


def build_phase3(nc, P, A, L):
    hT, hT_b, hT_blk, hT_tile = L["hT"], L["hT_b"], L["hT_blk"], L["hT_tile"]
    mb, cst_d, epsc, dsem, psum_sets = L["mb"], L["cst_d"], L["epsc"], L["dsem"], L["psum_sets"]
    x, out, w_in, x1_d, onT_d = L["x"], L["out"], L["w_in"], L["x1_d"], L["onT_d"]
    psc, fcw, fcb = L["psc"], L["fcw"], L["fcb"]
    base_mark = L["base_mark"]
    rmsnorm_to_bf16, dump = L["rmsnorm_to_bf16"], L["dump"]
    NB = len(MASKS) * 128

    big, sm, bfb, bfs = psum_sets("p3a")
    d_c = dsem("c3a")
    nwB = A.alloc([128, D], F32, "nwB2")
    P.ld("sp", d_c, nwB, L["nw2_d"][:, :])
    bands = A.alloc([128, 20, 128], BF16, "bands")
    P.ld("pool", dsem("bands"), bands, cst_d[:, NB:NB + 20 * 128].rearrange("p (a b) -> p a b", a=20))
    wp = A.alloc([128, NKT, 512], BF16, "wp")
    P.ld("pool", dsem("wp"), wp, w_in[:, OFF_P:OFF_P + 512].rearrange("(k p) c -> p k c", p=128))
    poolw = A.alloc([128, 4, 128], BF16, "poolw")
    P.ld("pool", dsem("poolw"), poolw, L["pool_w"].rearrange("g c d -> c g d"))
    wo_sb = A.alloc([128, NKT, D], BF16, "wo_sb")
    d_wo = dsem("wo")
    for k in range(NKT):
        P.ld("pool", d_wo, wo_sb[:, k, :], L["w_o"][k * 128:(k + 1) * 128, :])
    onTb = A.alloc([128, NH, 512], BF16, "onTb")
    d_onb = dsem("onb")
    p_tok = A.alloc([128, 8, 512], BF16, "p_tok")
    ptok_b = [Buf("ptok%d" % i) for i in range(8)]
    pm0T = A.alloc([128, 4, 512], BF16, "pm0T")
    pmT = A.alloc([128, 4, 512], BF16, "pmT")
    W_gp = A.alloc([128, NKT, D], BF16, "W_gp")
    W_gd = A.alloc([128, NKT, D], BF16, "W_gd")
    W_dn = A.alloc([128, NKT, D], BF16, "W_dn")
    W_po = A.alloc([128, 4, D], BF16, "W_po")
    d_W = [dsem("W%d" % i) for i in range(4)]
    for k in range(NKT):
        P.ld("pool", d_W[0], W_gp[:, k, :], w_in[k * 128:(k + 1) * 128, OFF_GP:OFF_GP + D])
        P.ld("pool", d_W[1], W_gd[:, k, :], w_in[k * 128:(k + 1) * 128, OFF_GD:OFF_GD + D])
        P.ld("pool", d_W[2], W_dn[:, k, :], L["dn_out"][k * 128:(k + 1) * 128, :])
        if k < 4:
            P.ld("pool", d_W[3], W_po[:, k, :], L["pool_out"][k * 128:(k + 1) * 128, :])
    tmpa = Ring([A.alloc([128, 512], F32, "tmpa%d" % i) for i in range(1)])
    tmpb = Ring([A.alloc([128, 512], F32, "tmpb%d" % i) for i in range(1)])
    mT = A.alloc([128, NKT, 512], BF16, "mT")
    xts = Ring([(A.alloc([128, D], F32, "xt3_%d" % i), dsem("xt3_%d" % i)) for i in range(1)])
    x1s = Ring([(A.alloc([128, D], F32, "x1_%d" % i), dsem("x1s_%d" % i)) for i in range(1)])
    hbs = Ring([A.alloc([128, D], BF16, "hb3_%d" % i) for i in range(1)])
    junk = None
    sts = Ring([A.alloc([128, 4], F32, "st3_%d" % i) for i in range(2)])
    ptok_done = set()

    def ensure_ptok(t):
        if t in ptok_done:
            return
        ptok_done.add(t)
        ps = big.next()
        for k in range(NKT):
            P.mm(ps, hT_tile(k, t), wp[:, k, :], start=(k == 0), stop=(k == NKT - 1))
        P.copy("act", T(p_tok.ap[:, t % 8, :], [ptok_b[t % 8]]), ps)

    def ptok(t, g):
        return T(p_tok.ap[:, t % 8, g * 128:(g + 1) * 128], [ptok_b[t % 8]])

    def band(g, kind):
        return bands[:, g * 5 + BANDS.index(kind), :]

    for b in range(8):
        t0 = b * 4
        P.ld("sp", d_onb, onTb, onT_d[:, :, b * 512:(b + 1) * 512].rearrange("h p s -> p h s"))
        for t in range(max(t0 - 1, 0), min(t0 + 5, NT)):
            ensure_ptok(t)
        for tt in range(4):
            t = t0 + tt
            for g in range(4):
                ps = sm.next()
                ck = "CF" if t == 0 else ("CL" if t == NT - 1 else "C")
                terms = [(ptok(t, g), band(g, ck))]
                if t > 0:
                    terms.append((ptok(t - 1, g), band(g, "L")))
                if t < NT - 1:
                    terms.append((ptok(t + 1, g), band(g, "R")))
                for i, (l_, r_) in enumerate(terms):
                    P.mm(ps, l_, r_, start=(i == 0), stop=(i == len(terms) - 1))
                P.copy("act" if g % 2 else "dve", pm0T[:, g, tt * 128:(tt + 1) * 128], ps)
        for g in range(4):
            ps = big.next()
            P.mm(ps, poolw[:, g, :], pm0T[:, g, :], True, True)
            P.ts("dve", pmT[:, g, :], ps, T(psc.ap[:, g:g + 1], psc.b), ALU.mult)
        for n in range(NKT):
            nsl = slice(n * 128, (n + 1) * 128)
            wgp, wpo, wgd, wdn = W_gp[:, :, nsl], W_po[:, :, nsl], W_gd[:, :, nsl], W_dn[:, :, nsl]
            ta, tb = tmpa.next(), tmpb.next()
            ps = big.next()
            for k in range(NKT):
                P.mm(ps, wgp[:, k, :], hT_blk(k, b), start=(k == 0), stop=(k == NKT - 1))
            P.act(ta, ps, AF.Sigmoid)
            ps = big.next()
            for k in range(4):
                P.mm(ps, wpo[:, k, :], pmT[:, k, :], start=(k == 0), stop=(k == 3))
            P.tt("dve", ta, ta, ps, ALU.mult)
            ps = big.next()
            for k in range(NKT):
                P.mm(ps, wgd[:, k, :], hT_blk(k, b), start=(k == 0), stop=(k == NKT - 1))
            P.act(tb, ps, AF.Sigmoid)
            ps = big.next()
            for k in range(NKT):
                P.mm(ps, wdn[:, k, :], onTb[:, k, :], start=(k == 0), stop=(k == NKT - 1))
            P.tt("dve", tb, tb, ps, ALU.mult)
            P.tt("dve", mT[:, n, :], ta, tb, ALU.add)
        def wo_mm(tt):
            pss = []
            for half in range(2):
                ps = big.next()
                for k in range(NKT):
                    P.mm(ps, mT[:, k, tt * 128:(tt + 1) * 128], wo_sb[:, k, half * 512:(half + 1) * 512],
                         start=(k == 0), stop=(k == NKT - 1))
                pss.append(ps)
            return pss

        def wo_add(tt, pss):
            t = t0 + tt
            xt, dx = xts.next()
            P.ld("sp", dx, xt, x[t * 128:(t + 1) * 128, :])
            x1, dx1 = x1s.next()
            for half in range(2):
                P.tt("dve", x1[:, half * 512:(half + 1) * 512], xt[:, half * 512:(half + 1) * 512], pss[half], ALU.add)
            P.st("sp", dx1, x1_d[t * 128:(t + 1) * 128, :], x1)
            return x1

        pss = wo_mm(0)
        x1 = wo_add(0, pss)
        for tt in range(4):
            t = t0 + tt
            if tt + 1 < 4:
                pss = wo_mm(tt + 1)
            hb = hbs.next()
            rmsnorm_to_bf16(x1, nwB, hb, hb, sts.next())
            pbank = bfb.next()
            pv = T(pbank.ap.rearrange("p (a b) -> p a b", a=8), pbank.b)
            for k in range(NKT):
                P.tr(pv[:, k, :], hb[:, k * 128:(k + 1) * 128], mb["ident"])
            P.copy("act", T(hT.ap[:, :, t * 128:(t + 1) * 128], hT_b[t // 4]), pv)
            if tt + 1 < 4:
                x1 = wo_add(tt + 1, pss)
    P.barrier()
    A.reset(base_mark)

    big, sm, bfb, bfs = psum_sets("p3b")
    nwB = A.alloc([128, D], F32, "nwBf")
    P.ld("sp", dsem("nwf"), nwB, L["nwf_d"][:, :])
    actT = A.alloc([128, NFT, 1024], BF16, "actT")
    act_b = [Buf("act%d" % j) for j in range(NFT)]
    wdn_sb = A.alloc([128, NFT, D], BF16, "wdn_sb")
    d_wd = dsem("wdn")
    for j in range(NFT):
        P.ld("pool", d_wd, wdn_sb[:, j, :], L["ffn_down"][j * 128:(j + 1) * 128, :])
    ust = [[A.alloc([128, 1026], F32, "ust%d_%d" % (i, r)) for r in range(2)] for i in range(2)]
    wups = [(A.alloc([128, NKT, 128], BF16, "wup%d" % i), dsem("wup%d" % i)) for i in range(3)]
    chunk_list = [(B_, j_, gv_) for B_ in range(4) for j_ in range(NFT) for gv_ in range(2)]
    loaded = [0]

    def prefetch_upto(i):
        while loaded[0] <= min(i, len(chunk_list) - 1):
            c = loaded[0]
            _, j_, gv_ = chunk_list[c]
            colw_ = gv_ * FFN + j_ * 128
            wt_, ds__ = wups[c % 3]
            P.ld("pool", ds__, wt_, L["ffn_up"][:, colw_:colw_ + 128].rearrange("(k p) c -> p k c", p=128))
            loaded[0] += 1

    cvt = [[[A.alloc([128, 512], F32, "cvt%d_%d_%d" % (i, hf, r)) for r in range(2)] for hf in range(2)] for i in range(2)]
    x1ts = Ring([(A.alloc([128, D], F32, "x1t%d" % i), dsem("x1t%d" % i)) for i in range(1)])
    d_xo = dsem("xo")
    junk = T(ust[0][0].ap[:, 0:D], ust[0][0].b)
    st = A.alloc([128, 4], F32, "stf")
    def ffn_proj(B, j):
        b0 = B * 1024
        for gv in range(2):
            ci = (B * NFT + j) * 2 + gv
            prefetch_upto(ci + 2)
            wt, ds_ = wups[ci % 3]
            u = ust[gv][j % 2]
            fi = gv * NFT + j
            for half in range(2):
                ps = big.next()
                blk = B * 2 + half
                for k in range(NKT):
                    P.mm(ps, wt[:, k, :], hT_blk(k, blk), start=(k == 0), stop=(k == NKT - 1))
                P.copy("act", u[:, 1 + half * 512: 1 + (half + 1) * 512], ps)
                P.act(cvt[gv][half][j % 2], ps, AF.Identity, bias=T(fcb.ap[:, fi:fi + 1], fcb.b),
                      scale=T(fcw.ap[:, fi, 1:2], fcw.b))
            for side, tok, c in ((0, b0 - 1, 0), (1, b0 + 1024, 1025)):
                if tok < 0 or tok >= S:
                    P.memset("pool", u[:, c:c + 1], 0.0)
                else:
                    ps = sm.next()
                    for k in range(NKT):
                        P.mm(ps[:, 0:1], wt[:, k, :], T(hT.ap[:, k, tok:tok + 1], hT_b[tok // 512]),
                             start=(k == 0), stop=(k == NKT - 1))
                    P.copy("act", u[:, c:c + 1], ps[:, 0:1])

    def ffn_post(B, j):
        for half in range(2):
            c0 = 1 + half * 512
            for gv in range(2):
                u = ust[gv][j % 2]
                fi = gv * NFT + j
                wc = lambda s_: T(fcw.ap[:, fi, s_:s_ + 1], fcw.b)
                cv = cvt[gv][half][j % 2]
                P.stt("dve", cv, u[:, c0 - 1:c0 + 511], wc(0), cv, ALU.mult, ALU.add)
                P.stt("dve", cv, u[:, c0 + 1:c0 + 513], wc(2), cv, ALU.mult, ALU.add)
            g_, v_ = cvt[0][half][j % 2], cvt[1][half][j % 2]
            P.act(g_, g_, AF.Silu)
            P.tt("pool", T(actT.ap[:, j, half * 512:(half + 1) * 512], [act_b[j]]), g_, v_, ALU.mult)

    for B in range(4):
        ffn_proj(B, 0)
        for j in range(NFT):
            if j + 1 < NFT:
                ffn_proj(B, j + 1)
            ffn_post(B, j)
        for tt in range(8):
            t = B * 8 + tt
            x1t, d_x1t = x1ts.next()
            P.ld("sp", d_x1t, x1t, x1_d[t * 128:(t + 1) * 128, :])
            for half in range(2):
                ps = big.next()
                for j in range(NFT):
                    P.mm(ps, T(actT.ap[:, j, tt * 128:(tt + 1) * 128], [act_b[j]]), wdn_sb[:, j, half * 512:(half + 1) * 512],
                         start=(j == 0), stop=(j == NFT - 1))
                P.tt("dve", x1t[:, half * 512:(half + 1) * 512], x1t[:, half * 512:(half + 1) * 512], ps, ALU.add)
            P.act(junk, x1t, AF.Square, accum=st[:, 0:1])
            P.act(st[:, 1:2], st[:, 0:1], AF.Sqrt, bias=epsc[:, 0:1], scale=1.0 / D)
            P.op("dve", lambda v: v.reciprocal(out=st.ap[:, 2:3], in_=st.ap[:, 1:2]), reads=st.b, writes=st.b)
            xo = T(ust[1][tt % 2].ap[:, 0:D], ust[1][tt % 2].b)
            P.stt("dve", xo, x1t, st[:, 2:3], nwB, ALU.mult, ALU.mult)
            P.st("sp", d_xo, out[t * 128:(t + 1) * 128, :], xo)


_NC_CACHE = {}


def _host_inputs(inp, b, cst):
    f = lambda n: np.ascontiguousarray(np.asarray(inp[n], dtype=np.float32))
    d = {}
    d["x"] = f("x")[b]
    d["w_in"] = f("w_in")[0]
    d["pool_w"] = f("pool_w")[0]
    d["pool_out"] = f("pool_out")[0]
    d["dn_out"] = f("dn_out")[0]
    d["w_o"] = f("w_o")[0]
    d["ffn_up"] = f("ffn_up")[0]
    d["ffn_down"] = f("ffn_down")[0]
    d["cw"] = np.ascontiguousarray(f("qkv_conv_w")[0].reshape(5, 24, 128).transpose(2, 1, 0).reshape(128, 120))
    d["fcw"] = np.ascontiguousarray(f("ffn_conv_w")[0].reshape(3, 44, 128).transpose(2, 1, 0).reshape(128, 132))
    d["fcb"] = np.ascontiguousarray(f("ffn_conv_b")[0].reshape(44, 128).T)
    d["psc"] = np.ascontiguousarray(f("pool_scale")[0].reshape(4, 128).T)
    d["nw1"] = np.ascontiguousarray(np.broadcast_to(f("norm1_w")[0], (128, 1024)))
    d["nw2"] = np.ascontiguousarray(np.broadcast_to(f("norm2_w")[0], (128, 1024)))
    d["nwf"] = np.ascontiguousarray(np.broadcast_to(f("final_norm_w"), (128, 1024)))
    d["dnw"] = np.ascontiguousarray(np.broadcast_to(f("dn_norm_w")[0], (128, 128)))
    d["alog"] = np.ascontiguousarray(np.broadcast_to(f("a_log")[0].reshape(16), (128, 16)))
    d["dtb"] = np.ascontiguousarray(np.broadcast_to(f("dt_bias")[0].reshape(16), (128, 16)))
    d["cst"] = cst
    return d


def kernel(**inputs):
    if "nc" not in _NC_CACHE:
        _NC_CACHE["nc"] = build_nc()
    nc = _NC_CACHE["nc"]
    cst = host_constants()
    nb = np.asarray(inputs["x"]).shape[0]
    in_maps = [_host_inputs(inputs, b, cst) for b in range(nb)]
    res = run_bass_kernel_spmd(nc, in_maps, core_ids=list(range(nb)))
    return np.stack([np.asarray(r["out"], dtype=np.float32) for r in res.results], axis=0)
```

```python
import numpy as np
from contextlib import ExitStack
import concourse.bass as bass
import concourse.mybir as mybir
from concourse.bass_utils import run_bass_kernel_spmd

F32 = mybir.dt.float32
BF16 = mybir.dt.bfloat16
AF = mybir.ActivationFunctionType
ALU = mybir.AluOpType
AX = mybir.AxisListType


class Buf:
    __slots__ = ("name", "w", "r", "excl")

    def __init__(self, name, excl=False):
        self.name = name
        self.excl = excl
        self.w = None
        self.r = {}


class Prog:
    ENG = ("pe", "act", "dve", "pool", "sp")

    def __init__(self, nc, es):
        self.nc = nc
        self.es = es
        self.h = {"pe": nc.tensor, "act": nc.scalar, "dve": nc.vector, "pool": nc.gpsimd, "sp": nc.sync}
        self.sem = {e: es.enter_context(nc.semaphore("s_" + e)) for e in self.ENG}
        self.cnt = {e: 0 for e in self.ENG}
        self.waited = {e: {} for e in self.ENG}
        self.q = {e: [] for e in self.ENG}
        self.dma_sems = []
        self.n_inst = 0
        self.n_wait = 0

    def new_dma_sem(self, name):
        s = self.es.enter_context(self.nc.semaphore(name))
        d = {"sem": s, "val": 0, "name": name}
        self.dma_sems.append(d)
        return d

    def _need(self, eng, deps, key, sem, val):
        if eng == "pe" and key == "pe":
            return
        if val > self.waited[eng].get(key, 0):
            prev = deps.get(key)
            if prev is None or prev[1] < val:
                deps[key] = (sem, val)

    def _collect(self, eng, reads, writes):
        deps = {}
        for b in reads:
            if b.w is not None:
                self._need(eng, deps, *b.w)
            if b.excl:
                for k, (sem, v) in b.r.items():
                    if k != eng:
                        self._need(eng, deps, k, sem, v)
        for b in writes:
            if b.w is not None:
                self._need(eng, deps, *b.w)
            for k, (sem, v) in b.r.items():
                self._need(eng, deps, k, sem, v)
        return deps

    def _emit_waits(self, eng, deps):
        h = self.h[eng]
        for key, (sem, val) in deps.items():
            self.waited[eng][key] = val
            self.n_wait += 1
            self.q[eng].append(lambda h=h, sem=sem, val=val: h.wait_ge(sem, val))

    def _mark(self, key, sem, val, reads, writes):
        for b in reads:
            p = b.r.get(key)
            if p is None or p[1] < val:
                b.r[key] = (sem, val)
        for b in writes:
            b.w = (key, sem, val)
            b.r = {}

    def op(self, eng, fn, reads=(), writes=()):
        self._emit_waits(eng, self._collect(eng, reads, writes))
        self.cnt[eng] += 1
        idx = self.cnt[eng]
        sem = self.sem[eng]
        h = self.h[eng]
        self.q[eng].append(lambda: fn(h).then_inc(sem, 1))
        self._mark(eng, sem, idx, reads, writes)
        self.n_inst += 1
        return idx

    def dma(self, eng, dsem, out, in_, reads=(), writes=()):
        self._emit_waits(eng, self._collect(eng, reads, writes))
        dsem["val"] += 16
        val = dsem["val"]
        h = self.h[eng]
        s = dsem["sem"]
        self.q[eng].append(lambda: h.dma_start(out=out, in_=in_).then_inc(s, 16))
        self._mark("dma:" + dsem["name"], s, val, reads, writes)
        self.n_inst += 1

    def wait_all_dma(self, eng):
        h = self.h[eng]
        for d in self.dma_sems:
            if d["val"] > 0:
                self.q[eng].append(lambda h=h, s=d["sem"], v=d["val"]: h.wait_ge(s, v))

    def barrier(self):
        for e in self.ENG:
            h = self.h[e]
            for e2 in self.ENG:
                if e2 != e and self.cnt[e2] > self.waited[e].get(e2, 0):
                    v = self.cnt[e2]
                    self.waited[e][e2] = v
                    self.q[e].append(lambda h=h, s=self.sem[e2], v=v: h.wait_ge(s, v))
            if self.cnt[e] > self.waited[e].get(e, 0):
                v = self.cnt[e]
                self.waited[e][e] = v
                self.q[e].append(lambda h=h, s=self.sem[e], v=v: h.wait_ge(s, v))
            for d in self.dma_sems:
                k = "dma:" + d["name"]
                if d["val"] > self.waited[e].get(k, 0):
                    v = d["val"]
                    self.waited[e][k] = v
                    self.q[e].append(lambda h=h, s=d["sem"], v=v: h.wait_ge(s, v))

    def flush(self):
        block = self.es.enter_context(self.nc.Block())
        q = self.q

        @block.tensor
        def _(t):
            for f in q["pe"]:
                f()

        @block.scalar
        def _(t):
            for f in q["act"]:
                f()

        @block.vector
        def _(t):
            for f in q["dve"]:
                f()

        @block.gpsimd
        def _(t):
            for f in q["pool"]:
                f()

        @block.sync
        def _(t):
            for f in q["sp"]:
                f()

    def mm(self, out, lhsT, rhs, start=True, stop=True):
        return self.op("pe", lambda t: t.matmul(out.ap, lhsT=lhsT.ap, rhs=rhs.ap, start=start, stop=stop),
                       reads=lhsT.b + rhs.b + ([] if start else out.b), writes=out.b)

    def tr(self, out, in_, ident):
        return self.op("pe", lambda t: t.transpose(out=out.ap, in_=in_.ap, identity=ident.ap),
                       reads=in_.b + ident.b, writes=out.b)

    def act(self, out, in_, func, bias=None, scale=1.0, accum=None, eng="act"):
        rd = list(in_.b)
        kw = {}
        if bias is not None:
            if isinstance(bias, T):
                rd += bias.b
                kw["bias"] = bias.ap
            else:
                kw["bias"] = bias
        if isinstance(scale, T):
            rd += scale.b
            kw["scale"] = scale.ap
        else:
            kw["scale"] = scale
        wr = list(out.b)
        if accum is not None:
            wr += accum.b
            kw["accum_out"] = accum.ap
        return self.op("act", lambda a: a.activation(out=out.ap, in_=in_.ap, func=func, **kw), reads=rd, writes=wr)

    def tt(self, eng, out, a, b, op):
        return self.op(eng, lambda v: v.tensor_tensor(out=out.ap, in0=a.ap, in1=b.ap, op=op),
                       reads=a.b + b.b, writes=out.b)

    def ts(self, eng, out, a, s1, op0, s2=None, op1=None):
        rd = list(a.b)
        v1 = s1
        if isinstance(s1, T):
            rd += s1.b
            v1 = s1.ap
        v2 = s2
        if isinstance(s2, T):
            rd += s2.b
            v2 = s2.ap
        if op1 is None:
            return self.op(eng, lambda v: v.tensor_scalar(out=out.ap, in0=a.ap, scalar1=v1, scalar2=None, op0=op0),
                           reads=rd, writes=out.b)
        return self.op(eng, lambda v: v.tensor_scalar(out=out.ap, in0=a.ap, scalar1=v1, scalar2=v2, op0=op0, op1=op1),
                       reads=rd, writes=out.b)

    def stt(self, eng, out, a, s, b, op0, op1):
        rd = a.b + b.b
        sv = s
        if isinstance(s, T):
            rd = rd + s.b
            sv = s.ap
        return self.op(eng, lambda v: v.scalar_tensor_tensor(out=out.ap, in0=a.ap, scalar=sv, in1=b.ap, op0=op0, op1=op1),
                       reads=rd, writes=out.b)

    def copy(self, eng, out, in_):
        if eng == "act":
            return self.op("act", lambda a: a.copy(out=out.ap, in_=in_.ap), reads=in_.b, writes=out.b)
        return self.op(eng, lambda v: v.tensor_copy(out=out.ap, in_=in_.ap), reads=in_.b, writes=out.b)

    def memset(self, eng, out, val):
        return self.op(eng, lambda v: v.memset(out.ap, val), writes=out.b)

    def ld(self, eng, dsem, out, in_ap):
        return self.dma(eng, dsem, out.ap, in_ap, writes=out.b)

    def st(self, eng, dsem, out_ap, in_):
        return self.dma(eng, dsem, out_ap, in_.ap, reads=in_.b)


class T:
    __slots__ = ("ap", "b")

    def __init__(self, ap, b):
        self.ap = ap
        self.b = b if isinstance(b, list) else [b]

    def __getitem__(self, k):
        return T(self.ap[k], self.b)

    def wb(self, b):
        return T(self.ap, b)


class Arena:
    def __init__(self, ap, nelem):
        self.ap = ap
        self.n = nelem
        self.off = 0

    def mark(self):
        return self.off

    def reset(self, m):
        self.off = m

    def alloc(self, shape, dt, name):
        n = int(np.prod(shape[1:]))
        units = n * (2 if dt == F32 else 1)
        if self.off % 2:
            self.off += 1
        a = self.ap[:, self.off:self.off + units]
        self.off += units
        assert self.off <= self.n, ("SBUF arena overflow", name, self.off, self.n)
        if dt == F32:
            a = a.bitcast(F32)
        if len(shape) == 3:
            a = a.rearrange("p (a b) -> p a b", a=shape[1])
        elif len(shape) == 4:
            a = a.rearrange("p (a b c) -> p a b c", a=shape[1], b=shape[2])
        return T(a, Buf(name))


S = 4096
D = 1024
NT = S // 128
NKT = D // 128
NH = 8
IN_DIM = 6688
OFF_P, OFF_Q, OFF_K, OFF_V, OFF_Z, OFF_BG, OFF_GP, OFF_GD = 0, 512, 1536, 2560, 3584, 4608, 4640, 5664
FFN = 2816
NFT = FFN // 128
POOL_WINDOWS = (2, 4, 8, 16)
EPS = 1e-6

MASKS = ("ident", "ones", "up_s", "lo_s", "up_i", "lo_i", "blk", "off_lo", "off_up")
BANDS = ("C", "L", "R", "CF", "CL")


def host_constants():
    p = np.arange(128)[:, None]
    f = np.arange(128)[None, :]
    m = {}
    m["ident"] = (p == f)
    m["ones"] = np.ones((128, 128), bool)
    m["up_s"] = f > p
    m["lo_s"] = f < p
    m["up_i"] = f >= p
    m["lo_i"] = f <= p
    m["blk"] = (p < 64) == (f < 64)
    m["off_lo"] = (p >= 64) & (f < 64)
    m["off_up"] = (p < 64) & (f >= 64)
    cols = [m[k].astype(np.float32) for k in MASKS]
    for w in POOL_WINDOWS:
        lo, hi = w // 2, w - 1 - w // 2
        mats = {k: np.zeros((128, 128), np.float32) for k in BANDS}
        for i in range(128):
            for tp in range(i - lo, i + hi + 1):
                if 0 <= tp < 128:
                    mats["C"][tp, i] += 1.0 / w
                elif tp < 0:
                    mats["L"][tp + 128, i] += 1.0 / w
                else:
                    mats["R"][tp - 128, i] += 1.0 / w
                a0 = max(i - lo, 0)
                cntf = (i + hi) - a0 + 1
                if 0 <= tp < 128:
                    mats["CF"][tp, i] += 1.0 / cntf
                b1 = min(i + hi, 127)
                cntl = b1 - (i - lo) + 1
                if 0 <= tp < 128:
                    mats["CL"][tp, i] += 1.0 / cntl
        for i in range(128):
            a0 = max(i - lo, 0)
            cnt = (i + hi) - a0 + 1
            for tp in range(a0, min(i + hi, 127) + 1):
                mats["CF"][tp, i] = 1.0 / cnt
            b1 = min(i + hi, 127)
            cnt = b1 - (i - lo) + 1
            for tp in range(max(i - lo, 0), b1 + 1):
                mats["CL"][tp, i] = 1.0 / cnt
        for k in ("C", "CF", "CL"):
            mats[k] -= np.eye(128, dtype=np.float32)
        cols += [mats[k] for k in BANDS]
    return np.ascontiguousarray(np.concatenate(cols, axis=1))


NCST = (len(MASKS) + 4 * len(BANDS)) * 128


class Ring:
    def __init__(self, items):
        self.items = items
        self.i = 0

    def next(self):
        it = self.items[self.i % len(self.items)]
        self.i += 1
        return it


def build_nc(dbg=(), nheads=NH, phase3=True, stop=None):
    nc = bass.Bass("TRN2", target_bir_lowering=False)

    def din(name, shape, dtype=F32):
        return nc.dram_tensor(name, list(shape), dtype, kind="ExternalInput").ap()

    x = din("x", [S, D])
    w_in = din("w_in", [D, IN_DIM])
    pool_w = din("pool_w", [4, 128, 128])
    pool_out = din("pool_out", [512, D])
    dn_out = din("dn_out", [D, D])
    w_o = din("w_o", [D, D])
    ffn_up = din("ffn_up", [D, 2 * FFN])
    ffn_down = din("ffn_down", [FFN, D])
    cw_d = din("cw", [128, 24 * 5])
    fcw_d = din("fcw", [128, 44 * 3])
    fcb_d = din("fcb", [128, 44])
    psc_d = din("psc", [128, 4])
    nw1_d = din("nw1", [128, D])
    nw2_d = din("nw2", [128, D])
    nwf_d = din("nwf", [128, D])
    dnw_d = din("dnw", [128, 128])
    alog_d = din("alog", [128, 16])
    dtb_d = din("dtb", [128, 16])
    cst_d = din("cst", [128, NCST])
    out = nc.dram_tensor("out", [S, D], F32, kind="ExternalOutput").ap()
    onT_d = nc.dram_tensor("onT_d", [NH, 128, S], BF16).ap()
    x1_d = nc.dram_tensor("x1_d", [S, D], F32).ap()
    dbg_out = {}
    for name, shape in dbg:
        dbg_out[name] = nc.dram_tensor("dbg_" + name, list(shape), F32, kind="ExternalOutput").ap()

    es = ExitStack()
    with es:
        E = es.enter_context
        P = Prog(nc, es)
        NU = 106400
        arena_t = E(nc.sbuf_tensor("arena", [128, NU], BF16))
        A = Arena(arena_t[:, :], NU)
        pbig_t = [E(nc.psum_tensor("pbig%d" % i, [128, 512], F32)) for i in range(2)]
        psm_t = [E(nc.psum_tensor("psm%d" % i, [128, 512], F32)) for i in range(4)]
        pbf_t = [E(nc.psum_tensor("pbf%d" % i, [128, 1024], BF16)) for i in range(2)]

        def psum_sets(tag, sm_on_big=False):
            bigb = [Buf("pbig%s%d" % (tag, i), True) for i in range(len(pbig_t))]
            big = Ring([T(t[:, :], bigb[i]) for i, t in enumerate(pbig_t)])
            smb = [Buf("psm%s%d" % (tag, i), True) for i in range(len(psm_t))]
            sm_t, sm_bufs = list(psm_t), list(smb)
            if sm_on_big:
                sm_t, sm_bufs = sm_t + list(pbig_t), sm_bufs + bigb
            sm = Ring([T(t[:, j * 128:(j + 1) * 128], sm_bufs[i]) for j in range(4) for i, t in enumerate(sm_t)])
            sm2 = Ring([(T(t[:, j * 256:(j + 1) * 256], sm_bufs[i]), T(t[:, j * 256:j * 256 + 128], sm_bufs[i]),
                         T(t[:, j * 256 + 128:(j + 1) * 256], sm_bufs[i])) for j in range(2) for i, t in enumerate(sm_t)])
            psum_sets.sm2 = sm2
            bfbuf = [Buf("pbf%s%d" % (tag, i), True) for i in range(len(pbf_t))]
            bfs = Ring([T(t[:, j * 128:(j + 1) * 128], bfbuf[i]) for j in range(8) for i, t in enumerate(pbf_t)])
            bfb = Ring([T(t[:, :], bfbuf[i]) for i, t in enumerate(pbf_t)])
            return big, sm, bfb, bfs

        dsem_i = [0]

        def dsem(name):
            dsem_i[0] += 1
            return P.new_dma_sem("%s_%d" % (name, dsem_i[0]))

        d_misc = dsem("misc")
        d_miscp = dsem("miscp")
        d_dbg = dsem("dbg")

        hT = A.alloc([128, NKT, S], BF16, "hT")
        hT_b = [Buf("hT%d" % i) for i in range(8)]

        def hT_blk(k, blk):
            return T(hT.ap[:, k, blk * 512:(blk + 1) * 512], hT_b[blk])

        def hT_tile(k, t):
            return T(hT.ap[:, k, t * 128:(t + 1) * 128], hT_b[t // 4])

        cstb = A.alloc([128, len(MASKS) * 128], BF16, "cstb")
        cstf = A.alloc([128, 4 * 128], F32, "cstf")
        P.ld("pool", d_miscp, cstb, cst_d[:, 0:len(MASKS) * 128])
        mb = {k: cstb[:, i * 128:(i + 1) * 128] for i, k in enumerate(MASKS)}
        for j, k in enumerate(("ident", "ones", "up_i", "lo_i")):
            i = MASKS.index(k)
            P.ld("sp", d_misc, cstf[:, j * 128:(j + 1) * 128], cst_d[:, i * 128:(i + 1) * 128])
        ident_f, ones_f, upi_f, loi_f = [cstf[:, j * 128:(j + 1) * 128] for j in range(4)]
        cw = A.alloc([128, 24, 5], F32, "cw")
        P.ld("sp", d_misc, cw, cw_d.rearrange("p (a b) -> p a b", a=24))
        fcw = A.alloc([128, 44, 3], F32, "fcw")
        P.ld("sp", d_misc, fcw, fcw_d.rearrange("p (a b) -> p a b", a=44))
        fcb = A.alloc([128, 44], F32, "fcb")
        P.ld("sp", d_misc, fcb, fcb_d[:, :])
        psc = A.alloc([128, 4], F32, "psc")
        P.ld("sp", d_misc, psc, psc_d[:, :])
        dnw = A.alloc([128, 128], F32, "dnw")
        P.ld("sp", d_misc, dnw, dnw_d[:, :])
        alog = A.alloc([128, 16], F32, "alog")
        P.ld("sp", d_misc, alog, alog_d[:, :])
        dtb = A.alloc([128, 16], F32, "dtb")
        P.ld("sp", d_misc, dtb, dtb_d[:, :])
        epsc = A.alloc([128, 1], F32, "epsc")
        P.memset("dve", epsc, EPS)
        mhalf = A.alloc([128, 1], F32, "mhalf")
        P.memset("dve", mhalf, -0.5)
        base_mark = A.mark()

        def dump(name, src):
            if name in dbg_out:
                P.st("sp", d_dbg, dbg_out[name], src)

        def rmsnorm_to_bf16(xt, nwB, hb, junk, st):
            P.act(junk, xt, AF.Square, accum=st[:, 0:1])
            P.act(st[:, 1:2], st[:, 0:1], AF.Sqrt, bias=epsc[:, 0:1], scale=1.0 / D)
            P.op("dve", lambda v: v.reciprocal(out=st.ap[:, 2:3], in_=st.ap[:, 1:2]), reads=st.b, writes=st.b)
            P.stt("dve", hb, xt, st[:, 2:3], nwB, ALU.mult, ALU.mult)

        def transpose_to_hT(hb, t, pbank, eng):
            pv = T(pbank.ap.rearrange("p (a b) -> p a b", a=8), pbank.b)
            for k in range(NKT):
                P.tr(pv[:, k, :], hb[:, k * 128:(k + 1) * 128], mb["ident"])
            dst = T(hT.ap[:, :, t * 128:(t + 1) * 128], hT_b[t // 4])
            P.copy(eng, dst, pv)

        big, sm, bfb, bfs = psum_sets("p1")
        nwB = A.alloc([128, D], F32, "nwB")
        P.ld("sp", d_misc, nwB, nw1_d[:, :])
        xts = Ring([A.alloc([128, D], F32, "xt%d" % i) for i in range(3)])
        xds = Ring([dsem("x") for i in range(3)])
        hbs = Ring([A.alloc([128, D], BF16, "hb%d" % i) for i in range(2)])
        junk = A.alloc([128, D], BF16, "junk")
        sts = Ring([A.alloc([128, 4], F32, "st%d" % i) for i in range(3)])
        def p1_norm(t):
            xt = xts.next()
            P.ld("sp", xds.next(), xt, x[t * 128:(t + 1) * 128, :])
            hb = hbs.next()
            rmsnorm_to_bf16(xt, nwB, hb, junk, sts.next())
            return hb

        hb_cur = p1_norm(0)
        for t in range(NT):
            hb_nxt = p1_norm(t + 1) if t + 1 < NT else None
            transpose_to_hT(hb_cur, t, bfb.next(), "act" if t % 2 else "dve")
            hb_cur = hb_nxt
        if "hT" in dbg_out:
            tmpf = A.alloc([128, S], F32, "dbg_hT")
            for k in range(NKT):
                P.copy("dve", tmpf, T(hT.ap[:, k, :], hT_b))
                P.st("sp", d_dbg, dbg_out["hT"][k], tmpf)
        if stop == 1:
            P.barrier()
            P.wait_all_dma("sp")
            P.flush()
            return nc
        P.barrier()
        A.reset(base_mark)

        big, sm, bfb, bfs = psum_sets("p20")
        GSH = [128, 2, NT, NH]
        beta = A.alloc(GSH, F32, "beta")
        Gc = A.alloc(GSH, F32, "Gc")
        GB = A.alloc(GSH, F32, "GB")
        eG = A.alloc(GSH, F32, "eG")
        bG = A.alloc(GSH, F32, "bG")
        eGd = A.alloc(GSH, F32, "eGd")
        gl = A.alloc(GSH, F32, "gl")
        invb = A.alloc(GSH, F32, "invb")
        p2_mark = A.mark()
        raw = A.alloc([128, NT, 32], F32, "raw")
        gg = A.alloc(GSH, F32, "gg")
        lnb = A.alloc(GSH, F32, "lnb")
        Gtot = A.alloc(GSH, F32, "Gtot")
        negA = A.alloc([128, 16], F32, "negA")
        wbg = A.alloc([128, NKT, 32], BF16, "wbg")
        P.ld("pool", d_miscp, wbg, w_in[:, OFF_BG:OFF_BG + 32].rearrange("(k p) c -> p k c", p=128))
        for t in range(NT):
            ps = sm.next()
            for k in range(NKT):
                P.mm(ps[:, 0:32], hT_tile(k, t), wbg[:, k, :], start=(k == 0), stop=(k == NKT - 1))
            P.copy("act" if t % 2 else "dve", raw[:, t, :], ps[:, 0:32])
        raw5 = T(raw.ap.rearrange("p t (k d h) -> p k d t h", k=2, d=2), raw.b)
        P.act(beta, raw5[:, 0], AF.Sigmoid)
        P.act(lnb, beta, AF.Ln)
        P.op("dve", lambda v: v.reciprocal(out=invb.ap, in_=beta.ap), reads=beta.b, writes=invb.b)
        P.act(negA, alog, AF.Exp)
        P.ts("dve", negA, negA, -1.0, ALU.mult)
        dtb4 = T(dtb.ap.rearrange("p (d h) -> p d h", d=2).unsqueeze(2).broadcast_to(GSH), dtb.b)
        negA4 = T(negA.ap.rearrange("p (d h) -> p d h", d=2).unsqueeze(2).broadcast_to(GSH), negA.b)
        P.tt("dve", gg, raw5[:, 1], dtb4, ALU.add)
        P.act(gg, gg, AF.Exp)
        P.act(gg, gg, AF.Ln, bias=1.0)
        P.tt("dve", gg, gg, negA4, ALU.mult)
        ggf = T(gg.ap.rearrange("p d t h -> p d (t h)"), gg.b)
        pc = big.next()
        P.mm(pc[:, 0:256], upi_f, ggf[:, 0], True, True)
        P.mm(pc[:, 256:512], loi_f, ggf[:, 1], True, True)
        P.copy("dve", T(Gc.ap.rearrange("p d t h -> p (d t h)"), Gc.b), pc)
        pt_ = big.next()
        P.mm(pt_, ones_f, T(gg.ap.rearrange("p d t h -> p (d t h)"), gg.b), True, True)
        P.copy("act", T(Gtot.ap.rearrange("p d t h -> p (d t h)"), Gtot.b), pt_)
        P.tt("dve", GB, Gc, lnb, ALU.add)
        P.act(eG, Gc, AF.Exp)
        P.tt("dve", bG, beta, eG, ALU.mult)
        P.tt("dve", eGd, Gtot, Gc, ALU.subtract)
        P.act(eGd, eGd, AF.Exp)
        P.act(gl, Gtot, AF.Exp)
        dump("raw", T(raw.ap.rearrange("p t c -> p (t c)"), raw.b))
        for nm, tt_ in (("beta", beta), ("Gc", Gc), ("gg", gg)):
            dump(nm, T(tt_.ap.rearrange("p d t h -> p (d t h)"), tt_.b))
        if stop == 2:
            P.barrier()
            P.wait_all_dma("sp")
            P.flush()
            return nc
        hT_d = nc.dram_tensor("hT_d", [8, 128, NKT, 512], BF16).ap()
        d_hst = dsem("hst")
        for blk in range(8):
            P.dma("sp", d_hst, hT_d[blk], hT.ap[:, :, blk * 512:(blk + 1) * 512], reads=[hT_b[blk]])
        P.barrier()
        A.reset(p2_mark)
        A2 = Arena(arena_t[:, 0:NKT * S], NKT * S)

        big, sm, bfb, bfs = psum_sets("p2", sm_on_big=True)
        PRE0 = 2
        NS = 4
        qkv_all = [A2.alloc([128, 3 * S], BF16, "qkv_all%d" % i) for i in range(2)]
        qkv_b = [[[Buf("%s%d_%d" % (nm, si, i)) for i in range(8)] for nm in ("qT", "kT", "vT")] for si in range(2)]
        zs_l = [A2.alloc([128, NT, 128], BF16, "zs%d" % i) for i in range(2)]
        zs_b = [[Buf("zs%d_%d" % (si, i)) for i in range(NT)] for si in range(2)]
        o_dir = [A.alloc([128, NT, 128], BF16, "o_f"), A.alloc([128, NT, 128], BF16, "o_b")]
        o_b = [[Buf("o%d_%d" % (d, i)) for i in range(NT)] for d in range(2)]
        d_w = [dsem("w%d" % i) for i in range(4)]
        d_on = dsem("on")
        Sst = [A.alloc([128, 128], F32, "S%d" % d) for d in range(2)]
        Sbf = [A.alloc([128, 128], BF16, "Sbf%d" % d) for d in range(2)]
        ssn = A.alloc([128, NT], F32, "ssn")
        junk128 = A.alloc([128, 128], BF16, "junk128")
        pre = A.alloc([128, S + 4], F32, "pre")
        pre_b = [Buf("pre%d" % i) for i in range(8)]
        accs = Ring([A.alloc([128, 512], F32, "acc%d" % i) for i in range(1)])
        sqs = Ring([A.alloc([128, 512], BF16, "sq%d" % i) for i in range(2)])
        rinvs = Ring([A.alloc([128, 512], F32, "rinv%d" % i) for i in range(1)])
        wqkvz = [A.alloc([128, NKT, 128], BF16, "w%s" % nm) for nm in "qkvz"]
        hring = Ring([(A.alloc([128, NKT, 512], BF16, "hblk%d" % i), dsem("hblk%d" % i)) for i in range(2)])

        def mk_shared(i):
            d = {}
            for nm in ("kbg", "kd", "vb", "KK", "QK"):
                d[nm] = A.alloc([128, 128], BF16, "%s_%d" % (nm, i))
            return d

        def mk_dir(i):
            d = {}
            for nm in ("u", "tmpo"):
                d[nm] = A.alloc([128, 128], F32, "%s_%d" % (nm, i))
            for nm in ("E2", "N", "Nd", "attnT", "Ld", "Lo", "MTa", "MTb", "Xd", "nT1",
                       "XT", "wT", "vnew"):
                d[nm] = A.alloc([128, 128], BF16, "%s_%d" % (nm, i))
            for ab in "ab":
                pair = A.alloc([128, 256], BF16, "MQ%s_%d" % (ab, i))
                d["M" + ab] = T(pair.ap[:, 0:128], Buf("M%s_%d" % (ab, i)))
                d["Q" + ab] = T(pair.ap[:, 128:256], Buf("Q%s_%d" % (ab, i)))
                d["MQ" + ab] = T(pair.ap, d["M" + ab].b + d["Q" + ab].b)
            return d

        shared_slots = [[mk_shared(d * NS + i) for i in range(NS)] for d in range(2)]
        dir_slots = [[mk_dir(d * NS + i) for i in range(NS)] for d in range(2)]
        cur = {"si": 0}

        def col(arr, d, t, h):
            return T(arr.ap[:, d, t, h:h + 1], arr.b)

        def qkv_T(si, Xi, lo, hi, bufs):
            return T(qkv_all[si].ap[:, Xi * S + lo: Xi * S + hi], bufs)

        def tile_of(Xi, t):
            si = cur["si"]
            return qkv_T(si, Xi, t * 128, (t + 1) * 128, [qkv_b[si][Xi][t // 4]])

        hstate = {"pending": None}

        def issue_hblk(blk):
            slot, ds_ = hring.next()
            P.dma("sp", ds_, slot.ap, hT_d[blk], writes=slot.b)
            hstate["pending"] = (blk, slot)

        def load_hblk(blk, nxt=None):
            if hstate["pending"] is None or hstate["pending"][0] != blk:
                issue_hblk(blk)
            slot = hstate["pending"][1]
            hstate["pending"] = None
            if nxt is not None:
                issue_hblk(nxt)
            return slot

        def conv_block(h, Xi, blk, si):
            acc = accs.next()
            rb = [pre_b[b_] for b_ in (blk - 1, blk, blk + 1) if 0 <= b_ < 8]
            for s_ in range(5):
                src = T(pre.ap[:, blk * 512 + s_: blk * 512 + s_ + 512], rb)
                wcol = T(cw.ap[:, Xi * 8 + h, s_:s_ + 1], cw.b)
                if s_ == 0:
                    P.ts("dve", acc, src, wcol, ALU.mult)
                else:
                    P.stt("dve", acc, src, wcol, acc, ALU.mult, ALU.add)
                if s_ % 2 == 1:
                    yield
            dst = qkv_T(si, Xi, blk * 512, (blk + 1) * 512, [qkv_b[si][Xi][blk]])
            rinv = rinvs.next()
            P.act(rinv, acc, AF.Exp, scale=-1.0)
            P.act(rinv, rinv, AF.Ln, bias=1.0)
            P.act(rinv, rinv, AF.Exp, scale=-1.0)
            yield
            if Xi == 2:
                P.tt("dve", dst, acc, rinv, ALU.mult)
                yield
            else:
                P.tt("dve", acc, acc, rinv, ALU.mult)
                yield
                sq = sqs.next()
                P.tt("pool", sq, acc, acc, ALU.mult)
                yield
                ps = big.next()
                P.mm(ps, mb["ones"], sq, True, True)
                P.act(rinv, ps, AF.Ln, bias=epsc[:, 0:1], scale=1.0)
                yield
                P.act(rinv, rinv, AF.Exp, scale=-0.5)
                yield
                if Xi == 0:
                    P.stt("dve", dst, acc, float(128 ** -0.5), rinv, ALU.mult, ALU.mult)
                else:
                    P.tt("dve", dst, acc, rinv, ALU.mult)
                yield

        def proj_gens(h, si):
            st = {"a": 0, "b": 0}

            def gen_a():
                for Xi, woff in enumerate((OFF_Q, OFF_K, OFF_V, OFF_Z)):
                    P.ld("pool", d_w[Xi], wqkvz[Xi],
                         w_in[:, woff + h * 128: woff + (h + 1) * 128].rearrange("(k p) c -> p k c", p=128))
                yield
                for Xi in range(3):
                    wX = wqkvz[Xi]
                    for blk in range(8):
                        i = 8 * Xi + blk
                        need = 0 if Xi == 0 else min(i - 6, 8 * Xi)
                        while st["b"] < need:
                            yield (lambda need=need: st["b"] >= need)
                        hb_ = load_hblk(blk, (blk + 1) % 8)
                        ps = big.next()
                        for k in range(NKT):
                            P.mm(ps, wX[:, k, :], hb_[:, k, :], start=(k == 0), stop=(k == NKT - 1))
                        P.copy("act", T(pre.ap[:, PRE0 + blk * 512: PRE0 + (blk + 1) * 512], [pre_b[blk]]), ps)
                        st["a"] = i + 1
                        yield
                while st["b"] < 24:
                    yield (lambda: st["b"] >= 24)
                wz = wqkvz[3]
                zflat = zs_l[si].ap.rearrange("p t e -> p (t e)")
                for blk in range(8):
                    hb_ = load_hblk(blk, blk + 1 if blk < 7 else None)
                    ps = big.next()
                    for k in range(NKT):
                        P.mm(ps, wz[:, k, :], hb_[:, k, :], start=(k == 0), stop=(k == NKT - 1))
                    acc = accs.next()
                    rinv = rinvs.next()
                    P.act(rinv, ps, AF.Exp, scale=-1.0)
                    P.act(acc, ps, AF.Copy)
                    yield
                    P.act(rinv, rinv, AF.Ln, bias=1.0)
                    P.act(rinv, rinv, AF.Exp, scale=-1.0)
                    yield
                    P.tt("dve", T(zflat[:, blk * 512:(blk + 1) * 512], zs_b[si][blk * 4:(blk + 1) * 4]), acc, rinv,
                         ALU.mult)
                    yield

            def gen_b():
                for Xi in range(3):
                    for blk in range(8):
                        i = 8 * Xi + blk
                        need = min(i + 2, 8 * Xi + 8)
                        while st["a"] < need:
                            yield (lambda need=need: st["a"] >= need)
                        yield from conv_block(h, Xi, blk, si)
                        st["b"] = i + 1

            return [gen_a(), gen_b()]

        def prep_tile(h, t, d, sh):
            ud = "up" if d == 0 else "lo"
            kt, qt, vt = tile_of(1, t), tile_of(0, t), tile_of(2, t)
            pk = bfs.next()
            P.tr(pk, kt, mb["ident"])
            P.act(sh["kbg"], pk, AF.Copy, scale=col(bG, d, t, h))
            P.ts("dve", sh["kd"], pk, col(eGd, d, t, h), ALU.mult)
            yield
            pv = bfs.next()
            P.tr(pv, vt, mb["ident"])
            P.act(sh["vb"], pv, AF.Copy, scale=col(beta, d, t, h))
            yield
            pboth, pqk, pkk = psum_sets.sm2.next()
            si_ = cur["si"]
            qk_rhs = T(qkv_all[si_].ap[:, 0:2 * S].rearrange("p (x s) -> p x s", x=2)[:, :, t * 128:(t + 1) * 128],
                       [qkv_b[si_][0][t // 4], qkv_b[si_][1][t // 4]])
            P.mm(T(pboth.ap.rearrange("p (x s) -> p x s", x=2), pboth.b), kt, qk_rhs, True, True)
            P.tt("dve", sh["QK"], pqk, mb[ud + "_i"], ALU.mult)
            P.tt("dve", sh["KK"], pkk, mb[ud + "_s"], ALU.mult)
            yield

        def local_gen(h, t, d, sh, ds):
            dg_, te_ = ds["tmpo"], ds["u"]
            P.ts("dve", dg_, ident_f, col(GB, d, t, h), ALU.mult)
            yield
            prow = sm.next()
            P.mm(prow, ones_f, dg_, True, True)
            P.ts("dve", te_, prow, col(Gc, d, t, h), ALU.subtract, 0.0, ALU.min)
            yield
            P.act(ds["E2"], te_, AF.Exp)
            yield
            P.tt("pool", ds["N"], sh["KK"], ds["E2"], ALU.mult)
            yield
            P.tt("pool", ds["attnT"], sh["QK"], ds["E2"], ALU.mult)
            P.tt("pool", ds["Nd"], ds["N"], mb["blk"], ALU.mult)
            yield
            pl = bfs.next()
            P.tr(pl, ds["N"], mb["ident"])
            P.tt("dve", ds["Ld"], pl, mb["blk"], ALU.mult)
            P.tt("dve", ds["Lo"], pl, mb["off_lo" if d == 0 else "off_up"], ALU.mult)
            yield
            P.tt("pool", ds["Qa"], mb["ident"], ds["Nd"], ALU.subtract)
            yield
            M, MT, Q = ds["Nd"], ds["Ld"], ds["Qa"]
            pa = sm.next()
            P.mm(pa, MT, M, True, True)
            P.copy("act", ds["Ma"], pa)
            yield
            pb = sm.next()
            P.mm(pb, M, MT, True, True)
            P.copy("act", ds["MTa"], pb)
            yield
            M, MT = ds["Ma"], ds["MTa"]
            for lvl in range(1, 6):
                nM, nMT = (ds["Mb"], ds["MTb"]) if lvl % 2 else (ds["Ma"], ds["MTa"])
                nQ = ds["Qb"] if lvl % 2 else ds["Qa"]
                if lvl < 5:
                    pboth, p_m, p_q = psum_sets.sm2.next()
                    rhs = ds["MQa"] if (lvl % 2 == 1) else ds["MQb"]
                    P.mm(pboth, MT, rhs, True, True)
                    P.copy("act", nM, p_m)
                    P.tt("dve", nQ, Q, p_q, ALU.add)
                    yield
                    pb = sm.next()
                    P.mm(pb, M, MT, True, True)
                    P.copy("act" if lvl % 2 else "dve", nMT, pb)
                    yield
                else:
                    pq = sm.next()
                    P.mm(pq, MT, Q, True, True)
                    P.tt("dve", nQ, Q, pq, ALU.add)
                    yield
                M, MT, Q = nM, nMT, nQ
            Y = Q
            px = bfs.next()
            P.tr(px, Y, mb["ident"])
            P.copy("act", ds["Xd"], px)
            yield
            p1 = sm.next()
            P.mm(p1, ds["Lo"], Y, True, True)
            P.act(ds["nT1"], p1, AF.Copy, scale=-1.0)
            yield
            p2 = sm.next()
            P.mm(p2, ds["Xd"], ds["nT1"], True, True)
            P.tt("dve", ds["XT"], Y, p2, ALU.add)
            yield
            pw = sm.next()
            P.mm(pw, sh["kbg"], ds["XT"], True, True)
            P.copy("act", ds["wT"], pw)
            yield
            pu = sm.next()
            P.mm(pu, ds["XT"], sh["vb"], True, True)
            P.copy("dve", ds["u"], pu)
            yield

        def scan_gen(h, t, d, sh, ds):
            qt = tile_of(0, t)
            pa = sm.next()
            P.mm(pa, ds["wT"], Sbf[d], True, True)
            po1 = sm.next()
            P.mm(po1, qt, Sbf[d], True, True)
            P.tt("dve", ds["vnew"], ds["u"], pa, ALU.subtract)
            P.act(ds["tmpo"], po1, AF.Copy, scale=col(eG, d, t, h))
            yield
            psn = sm.next()
            P.mm(psn, sh["kd"], ds["vnew"], True, True)
            P.stt("dve", Sst[d], Sst[d], T(gl.ap[:, d, t, h:h + 1], gl.b), psn, ALU.mult, ALU.add)
            yield
            P.copy("act", Sbf[d], Sst[d])
            yield
            po2 = sm.next()
            P.mm(po2, ds["attnT"], ds["vnew"], True, True)
            P.stt("dve", o_dir[d][:, t, :].wb(o_b[d][t]), po2, col(invb, d, t, h), ds["tmpo"], ALU.mult, ALU.add)
            yield

        def run_sched(gens):
            active = [(g, None) for g in gens]
            while active:
                progressed = False
                nxt = []
                for g, cond in active:
                    if cond is not None and not cond():
                        nxt.append((g, cond))
                        continue
                    try:
                        c = next(g)
                        nxt.append((g, c))
                    except StopIteration:
                        pass
                    progressed = True
                active = nxt
                assert progressed or not active, "scheduler deadlock"

        def chain_gens(h):
            gens = []
            for d in range(2):
                order = list(range(NT)) if d == 0 else list(range(NT - 1, -1, -1))
                state = {"local_done": [False] * NT, "scan_done": 0}

                def worker(w, d=d, order=order, state=state):
                    for idx in range(w, NT, NS):
                        t = order[idx]
                        while state["scan_done"] < idx - NS + 1:
                            yield (lambda idx=idx: state["scan_done"] >= idx - NS + 1)
                        sh = shared_slots[d][idx % NS]
                        ds = dir_slots[d][idx % NS]
                        yield from prep_tile(h, t, d, sh)
                        yield from local_gen(h, t, d, sh, ds)
                        state["local_done"][idx] = True

                def scanner(d=d, order=order, state=state):
                    for idx, t in enumerate(order):
                        while not state["local_done"][idx]:
                            yield (lambda idx=idx: state["local_done"][idx])
                        yield from scan_gen(h, t, d, shared_slots[d][idx % NS], dir_slots[d][idx % NS])
                        state["scan_done"] = idx + 1

                gens += [worker(w) for w in range(NS)] + [scanner()]
            return gens

        def norm_stage(h, si):
            kvb = qkv_b[si][1] + qkv_b[si][2]
            osum = T(qkv_all[si].ap[:, S:3 * S].bitcast(F32).rearrange("p (t e) -> p t e", t=NT), kvb)
            onT_sb = T(qkv_all[si].ap[:, 0:S], qkv_b[si][0])
            zst = T(zs_l[si].ap, zs_b[si])
            P.tt("dve", osum, o_dir[0].wb(o_b[0]), o_dir[1].wb(o_b[1]), ALU.add)
            for t in range(NT):
                P.act(junk128, osum[:, t, :], AF.Square, accum=ssn[:, t:t + 1])
            P.act(ssn, ssn, AF.Ln, bias=epsc[:, 0:1], scale=1.0 / 128)
            P.act(ssn, ssn, AF.Exp, scale=-0.5)
            P.tt("dve", osum, osum, T(ssn.ap.unsqueeze(2).broadcast_to([128, NT, 128]), ssn.b), ALU.mult)
            P.tt("pool", osum, osum, T(dnw.ap.unsqueeze(1).broadcast_to([128, NT, 128]), dnw.b), ALU.mult)
            if h == 0:
                dump("on0", T(osum.ap.rearrange("p t e -> p (t e)"), osum.b))
            on_tok = o_dir[0].wb(o_b[0])
            P.copy("act", on_tok, osum)
            zflat = T(zs_l[si].ap.rearrange("p t e -> p (t e)"), zs_b[si])
            for t8 in range(4):
                pbk = bfb.next()
                pv = T(pbk.ap.rearrange("p (a b) -> p a b", a=8), pbk.b)
                for j in range(8):
                    P.tr(pv[:, j, :], on_tok[:, t8 * 8 + j, :], mb["ident"])
                P.tt("dve", T(onT_sb.ap[:, t8 * 1024:(t8 + 1) * 1024], onT_sb.b), pbk,
                     zflat[:, t8 * 1024:(t8 + 1) * 1024], ALU.mult)
            P.st("sp", d_on, onT_d[h], onT_sb)

        P.memset("pool", T(pre.ap[:, 0:PRE0], [pre_b[0]]), 0.0)
        P.memset("pool", T(pre.ap[:, PRE0 + S:PRE0 + S + 2], [pre_b[7]]), 0.0)
        run_sched(proj_gens(0, 0))
        for h in range(nheads):
            si = h % 2
            cur["si"] = si
            for d in range(2):
                P.memset("dve", Sst[d], 0.0)
                P.memset("pool", Sbf[d], 0.0)
            gens = chain_gens(h)
            if h + 1 < nheads:
                gens += proj_gens(h + 1, 1 - si)
            run_sched(gens)
            norm_stage(h, si)
        P.barrier()
        d_hld = dsem("hld")
        for blk in range(8):
            P.dma("sp", d_hld, hT.ap[:, :, blk * 512:(blk + 1) * 512], hT_d[blk], writes=[hT_b[blk]])
        P.barrier()
        A.reset(base_mark)
        if phase3:
            build_phase3(nc, P, A, locals())
        P.barrier()
        P.wait_all_dma("sp")
        P.flush()
    return nc


def build_phase3(nc, P, A, L):
    hT, hT_b, hT_blk, hT_tile = L["hT"], L["hT_b"], L["hT_blk"], L["hT_tile"]
    mb, cst_d, epsc, dsem, psum_sets = L["mb"], L["cst_d"], L["epsc"], L["dsem"], L["psum_sets"]
    x, out, w_in, x1_d, onT_d = L["x"], L["out"], L["w_in"], L["x1_d"], L["onT_d"]
    psc, fcw, fcb = L["psc"], L["fcw"], L["fcb"]
    base_mark = L["base_mark"]
    rmsnorm_to_bf16, dump = L["rmsnorm_to_bf16"], L["dump"]
    NB = len(MASKS) * 128

    big, sm, bfb, bfs = psum_sets("p3a")
    d_c = dsem("c3a")
    nwB = A.alloc([128, D], F32, "nwB2")
    P.ld("sp", d_c, nwB, L["nw2_d"][:, :])
    bands = A.alloc([128, 20, 128], BF16, "bands")
    P.ld("pool", dsem("bands"), bands, cst_d[:, NB:NB + 20 * 128].rearrange("p (a b) -> p a b", a=20))
    wp = A.alloc([128, NKT, 512], BF16, "wp")
    P.ld("pool", dsem("wp"), wp, w_in[:, OFF_P:OFF_P + 512].rearrange("(k p) c -> p k c", p=128))
    poolw = A.alloc([128, 4, 128], BF16, "poolw")
    P.ld("pool", dsem("poolw"), poolw, L["pool_w"].rearrange("g c d -> c g d"))
    wo_sb = A.alloc([128, NKT, D], BF16, "wo_sb")
    d_wo = dsem("wo")
    for k in range(NKT):
        P.ld("pool", d_wo, wo_sb[:, k, :], L["w_o"][k * 128:(k + 1) * 128, :])
    onTb = A.alloc([128, NH, 512], BF16, "onTb")
    d_onb = dsem("onb")
    p_tok = A.alloc([128, 8, 512], BF16, "p_tok")
    ptok_b = [Buf("ptok%d" % i) for i in range(8)]
    pm0T = A.alloc([128, 4, 512], BF16, "pm0T")
    pmT = A.alloc([128, 4, 512], BF16, "pmT")
    W_gp = A.alloc([128, NKT, D], BF16, "W_gp")
    W_gd = A.alloc([128, NKT, D], BF16, "W_gd")
    W_dn = A.alloc([128, NKT, D], BF16, "W_dn")
    W_po = A.alloc([128, 4, D], BF16, "W_po")
    d_W = [dsem("W%d" % i) for i in range(4)]
    for k in range(NKT):
        P.ld("pool", d_W[0], W_gp[:, k, :], w_in[k * 128:(k + 1) * 128, OFF_GP:OFF_GP + D])
        P.ld("pool", d_W[1], W_gd[:, k, :], w_in[k * 128:(k + 1) * 128, OFF_GD:OFF_GD + D])
        P.ld("pool", d_W[2], W_dn[:, k, :], L["dn_out"][k * 128:(k + 1) * 128, :])
        if k < 4:
            P.ld("pool", d_W[3], W_po[:, k, :], L["pool_out"][k * 128:(k + 1) * 128, :])
    tmpa = Ring([A.alloc([128, 512], F32, "tmpa%d" % i) for i in range(1)])
    tmpb = Ring([A.alloc([128, 512], F32, "tmpb%d" % i) for i in range(1)])
    mT = A.alloc([128, NKT, 512], BF16, "mT")
    xts = Ring([(A.alloc([128, D], F32, "xt3_%d" % i), dsem("xt3_%d" % i)) for i in range(1)])
    x1s = Ring([(A.alloc([128, D], F32, "x1_%d" % i), dsem("x1s_%d" % i)) for i in range(1)])
    hbs = Ring([A.alloc([128, D], BF16, "hb3_%d" % i) for i in range(1)])
    junk = None
    sts = Ring([A.alloc([128, 4], F32, "st3_%d" % i) for i in range(2)])
    ptok_done = set()

    def ensure_ptok(t):
        if t in ptok_done:
            return
        ptok_done.add(t)
        ps = big.next()
        for k in range(NKT):
            P.mm(ps, hT_tile(k, t), wp[:, k, :], start=(k == 0), stop=(k == NKT - 1))
        P.copy("act", T(p_tok.ap[:, t % 8, :], [ptok_b[t % 8]]), ps)

    def ptok(t, g):
        return T(p_tok.ap[:, t % 8, g * 128:(g + 1) * 128], [ptok_b[t % 8]])

    def band(g, kind):
        return bands[:, g * 5 + BANDS.index(kind), :]

    for b in range(8):
        t0 = b * 4
        P.ld("sp", d_onb, onTb, onT_d[:, :, b * 512:(b + 1) * 512].rearrange("h p s -> p h s"))
        for t in range(max(t0 - 1, 0), min(t0 + 5, NT)):
            ensure_ptok(t)
        for tt in range(4):
            t = t0 + tt
            for g in range(4):
                ps = sm.next()
                ck = "CF" if t == 0 else ("CL" if t == NT - 1 else "C")
                terms = [(ptok(t, g), band(g, ck))]
                if t > 0:
                    terms.append((ptok(t - 1, g), band(g, "L")))
                if t < NT - 1:
                    terms.append((ptok(t + 1, g), band(g, "R")))
                for i, (l_, r_) in enumerate(terms):
                    P.mm(ps, l_, r_, start=(i == 0), stop=(i == len(terms) - 1))
                P.copy("act" if g % 2 else "dve", pm0T[:, g, tt * 128:(tt + 1) * 128], ps)
        for g in range(4):
            ps = big.next()
            P.mm(ps, poolw[:, g, :], pm0T[:, g, :], True, True)
            P.ts("dve", pmT[:, g, :], ps, T(psc.ap[:, g:g + 1], psc.b), ALU.mult)
        for n in range(NKT):
            nsl = slice(n * 128, (n + 1) * 128)
            wgp, wpo, wgd, wdn = W_gp[:, :, nsl], W_po[:, :, nsl], W_gd[:, :, nsl], W_dn[:, :, nsl]
            ta, tb = tmpa.next(), tmpb.next()
            ps = big.next()
            for k in range(NKT):
                P.mm(ps, wgp[:, k, :], hT_blk(k, b), start=(k == 0), stop=(k == NKT - 1))
            P.act(ta, ps, AF.Sigmoid)
            ps = big.next()
            for k in range(4):
                P.mm(ps, wpo[:, k, :], pmT[:, k, :], start=(k == 0), stop=(k == 3))
            P.tt("dve", ta, ta, ps, ALU.mult)
            ps = big.next()
            for k in range(NKT):
                P.mm(ps, wgd[:, k, :], hT_blk(k, b), start=(k == 0), stop=(k == NKT - 1))
            P.act(tb, ps, AF.Sigmoid)
            ps = big.next()
            for k in range(NKT):
                P.mm(ps, wdn[:, k, :], onTb[:, k, :], start=(k == 0), stop=(k == NKT - 1))
            P.tt("dve", tb, tb, ps, ALU.mult)
            P.tt("dve", mT[:, n, :], ta, tb, ALU.add)
        def wo_mm(tt):
            pss = []
            for half in range(2):
                ps = big.next()
                for k in range(NKT):
                    P.mm(ps, mT[:, k, tt * 128:(tt + 1) * 128], wo_sb[:, k, half * 512:(half + 1) * 512],
                         start=(k == 0), stop=(k == NKT - 1))
                pss.append(ps)
            return pss

        def wo_add(tt, pss):
            t = t0 + tt
            xt, dx = xts.next()
            P.ld("sp", dx, xt, x[t * 128:(t + 1) * 128, :])
            x1, dx1 = x1s.next()
            for half in range(2):
                P.tt("dve", x1[:, half * 512:(half + 1) * 512], xt[:, half * 512:(half + 1) * 512], pss[half], ALU.add)
            P.st("sp", dx1, x1_d[t * 128:(t + 1) * 128, :], x1)
            return x1

        pss = wo_mm(0)
        x1 = wo_add(0, pss)
        for tt in range(4):
            t = t0 + tt
            if tt + 1 < 4:
                pss = wo_mm(tt + 1)
            hb = hbs.next()
            rmsnorm_to_bf16(x1, nwB, hb, hb, sts.next())
            pbank = bfb.next()
            pv = T(pbank.ap.rearrange("p (a b) -> p a b", a=8), pbank.b)
            for k in range(NKT):
                P.tr(pv[:, k, :], hb[:, k * 128:(k + 1) * 128], mb["ident"])
            P.copy("act", T(hT.ap[:, :, t * 128:(t + 1) * 128], hT_b[t // 4]), pv)
            if tt + 1 < 4:
                x1 = wo_add(tt + 1, pss)
    P.barrier()
    A.reset(base_mark)

    big, sm, bfb, bfs = psum_sets("p3b")
    nwB = A.alloc([128, D], F32, "nwBf")
    P.ld("sp", dsem("nwf"), nwB, L["nwf_d"][:, :])
    actT = A.alloc([128, NFT, 1024], BF16, "actT")
    act_b = [Buf("act%d" % j) for j in range(NFT)]
    wdn_sb = A.alloc([128, NFT, D], BF16, "wdn_sb")
    d_wd = dsem("wdn")
    for j in range(NFT):
        P.ld("pool", d_wd, wdn_sb[:, j, :], L["ffn_down"][j * 128:(j + 1) * 128, :])
    ust = [[A.alloc([128, 1026], F32, "ust%d_%d" % (i, r)) for r in range(2)] for i in range(2)]
    wups = [(A.alloc([128, NKT, 128], BF16, "wup%d" % i), dsem("wup%d" % i)) for i in range(3)]
    chunk_list = [(B_, j_, gv_) for B_ in range(4) for j_ in range(NFT) for gv_ in range(2)]
    loaded = [0]

    def prefetch_upto(i):
        while loaded[0] <= min(i, len(chunk_list) - 1):
            c = loaded[0]
            _, j_, gv_ = chunk_list[c]
            colw_ = gv_ * FFN + j_ * 128
            wt_, ds__ = wups[c % 3]
            P.ld("pool", ds__, wt_, L["ffn_up"][:, colw_:colw_ + 128].rearrange("(k p) c -> p k c", p=128))
            loaded[0] += 1

    cvt = [[[A.alloc([128, 512], F32, "cvt%d_%d_%d" % (i, hf, r)) for r in range(2)] for hf in range(2)] for i in range(2)]
    x1ts = Ring([(A.alloc([128, D], F32, "x1t%d" % i), dsem("x1t%d" % i)) for i in range(1)])
    d_xo = dsem("xo")
    junk = T(ust[0][0].ap[:, 0:D], ust[0][0].b)
    st = A.alloc([128, 4], F32, "stf")
    def ffn_proj(B, j):
        b0 = B * 1024
        for gv in range(2):
            ci = (B * NFT + j) * 2 + gv
            prefetch_upto(ci + 2)
            wt, ds_ = wups[ci % 3]
            u = ust[gv][j % 2]
            fi = gv * NFT + j
            for half in range(2):
                ps = big.next()
                blk = B * 2 + half
                for k in range(NKT):
                    P.mm(ps, wt[:, k, :], hT_blk(k, blk), start=(k == 0), stop=(k == NKT - 1))
                P.copy("act", u[:, 1 + half * 512: 1 + (half + 1) * 512], ps)
                P.act(cvt[gv][half][j % 2], ps, AF.Identity, bias=T(fcb.ap[:, fi:fi + 1], fcb.b),
                      scale=T(fcw.ap[:, fi, 1:2], fcw.b))
            for side, tok, c in ((0, b0 - 1, 0), (1, b0 + 1024, 1025)):
                if tok < 0 or tok >= S:
                    P.memset("pool", u[:, c:c + 1], 0.0)
                else:
                    ps = sm.next()
                    for k in range(NKT):
                        P.mm(ps[:, 0:1], wt[:, k, :], T(hT.ap[:, k, tok:tok + 1], hT_b[tok // 512]),
                             start=(k == 0), stop=(k == NKT - 1))
                    P.copy("act", u[:, c:c + 1], ps[:, 0:1])

    def ffn_post(B, j):
        for half in range(2):
            c0 = 1 + half * 512
            for gv in range(2):
                u = ust[gv][j % 2]
                fi = gv * NFT + j
                wc = lambda s_: T(fcw.ap[:, fi, s_:s_ + 1], fcw.b)
                cv = cvt[gv][half][j % 2]
                P.stt("dve", cv, u[:, c0 - 1:c0 + 511], wc(0), cv, ALU.mult, ALU.add)
                P.stt("dve", cv, u[:, c0 + 1:c0 + 513], wc(2), cv, ALU.mult, ALU.add)
            g_, v_ = cvt[0][half][j % 2], cvt[1][half][j % 2]
            P.act(g_, g_, AF.Silu)
            P.tt("pool", T(actT.ap[:, j, half * 512:(half + 1) * 512], [act_b[j]]), g_, v_, ALU.mult)

    for B in range(4):
        ffn_proj(B, 0)
        for j in range(NFT):
            if j + 1 < NFT:
                ffn_proj(B, j + 1)
            ffn_post(B, j)
        for tt in range(8):
            t = B * 8 + tt
            x1t, d_x1t = x1ts.next()
            P.ld("sp", d_x1t, x1t, x1_d[t * 128:(t + 1) * 128, :])
            for half in range(2):
                ps = big.next()
                for j in range(NFT):
                    P.mm(ps, T(actT.ap[:, j, tt * 128:(tt + 1) * 128], [act_b[j]]), wdn_sb[:, j, half * 512:(half + 1) * 512],
                         start=(j == 0), stop=(j == NFT - 1))
                P.tt("dve", x1t[:, half * 512:(half + 1) * 512], x1t[:, half * 512:(half + 1) * 512], ps, ALU.add)
            P.act(junk, x1t, AF.Square, accum=st[:, 0:1])
            P.act(st[:, 1:2], st[:, 0:1], AF.Sqrt, bias=epsc[:, 0:1], scale=1.0 / D)
            P.op("dve", lambda v: v.reciprocal(out=st.ap[:, 2:3], in_=st.ap[:, 1:2]), reads=st.b, writes=st.b)
            xo = T(ust[1][tt % 2].ap[:, 0:D], ust[1][tt % 2].b)
            P.stt("dve", xo, x1t, st[:, 2:3], nwB, ALU.mult, ALU.mult)
            P.st("sp", d_xo, out[t * 128:(t + 1) * 128, :], xo)


_NC_CACHE = {}


def _host_inputs(inp, b, cst):
    f = lambda n: np.ascontiguousarray(np.asarray(inp[n], dtype=np.float32))
    d = {}
    d["x"] = f("x")[b]
    d["w_in"] = f("w_in")[0]
    d["pool_w"] = f("pool_w")[0]
    d["pool_out"] = f("pool_out")[0]
    d["dn_out"] = f("dn_out")[0]
    d["w_o"] = f("w_o")[0]
    d["ffn_up"] = f("ffn_up")[0]
    d["ffn_down"] = f("ffn_down")[0]
    d["cw"] = np.ascontiguousarray(f("qkv_conv_w")[0].reshape(5, 24, 128).transpose(2, 1, 0).reshape(128, 120))
    d["fcw"] = np.ascontiguousarray(f("ffn_conv_w")[0].reshape(3, 44, 128).transpose(2, 1, 0).reshape(128, 132))
    d["fcb"] = np.ascontiguousarray(f("ffn_conv_b")[0].reshape(44, 128).T)
    d["psc"] = np.ascontiguousarray(f("pool_scale")[0].reshape(4, 128).T)
    d["nw1"] = np.ascontiguousarray(np.broadcast_to(f("norm1_w")[0], (128, 1024)))
    d["nw2"] = np.ascontiguousarray(np.broadcast_to(f("norm2_w")[0], (128, 1024)))
    d["nwf"] = np.ascontiguousarray(np.broadcast_to(f("final_norm_w"), (128, 1024)))
    d["dnw"] = np.ascontiguousarray(np.broadcast_to(f("dn_norm_w")[0], (128, 128)))
    d["alog"] = np.ascontiguousarray(np.broadcast_to(f("a_log")[0].reshape(16), (128, 16)))
    d["dtb"] = np.ascontiguousarray(np.broadcast_to(f("dt_bias")[0].reshape(16), (128, 16)))
    d["cst"] = cst
    return d


def kernel(**inputs):
    if "nc" not in _NC_CACHE:
        _NC_CACHE["nc"] = build_nc()
    nc = _NC_CACHE["nc"]
    cst = host_constants()
    nb = np.asarray(inputs["x"]).shape[0]
    in_maps = [_host_inputs(inputs, b, cst) for b in range(nb)]
    res = run_bass_kernel_spmd(nc, in_maps, core_ids=list(range(nb)))
    return np.stack([np.asarray(r["out"], dtype=np.float32) for r in res.results], axis=0)
```

```python
import numpy as np
from contextlib import ExitStack
import concourse.bass as bass
import concourse.mybir as mybir
from concourse.bass_utils import run_bass_kernel_spmd

F32 = mybir.dt.float32
BF16 = mybir.dt.bfloat16
AF = mybir.ActivationFunctionType
ALU = mybir.AluOpType
AX = mybir.AxisListType


class Buf:
    __slots__ = ("name", "w", "r", "excl")

    def __init__(self, name, excl=False):
        self.name = name
        self.excl = excl
        self.w = None
        self.r = {}


class Prog:
    ENG = ("pe", "act", "dve", "pool", "sp")

    def __init__(self, nc, es):
        self.nc = nc
        self.es = es
        self.h = {"pe": nc.tensor, "act": nc.scalar, "dve": nc.vector, "pool": nc.gpsimd, "sp": nc.sync}
        self.sem = {e: es.enter_context(nc.semaphore("s_" + e)) for e in self.ENG}
        self.cnt = {e: 0 for e in self.ENG}
        self.waited = {e: {} for e in self.ENG}
        self.q = {e: [] for e in self.ENG}
        self.dma_sems = []
        self.n_inst = 0
        self.n_wait = 0

    def new_dma_sem(self, name):
        s = self.es.enter_context(self.nc.semaphore(name))
        d = {"sem": s, "val": 0, "name": name}
        self.dma_sems.append(d)
        return d

    def _need(self, eng, deps, key, sem, val):
        if eng == "pe" and key == "pe":
            return
        if val > self.waited[eng].get(key, 0):
            prev = deps.get(key)
            if prev is None or prev[1] < val:
                deps[key] = (sem, val)

    def _collect(self, eng, reads, writes):
        deps = {}
        for b in reads:
            if b.w is not None:
                self._need(eng, deps, *b.w)
            if b.excl:
                for k, (sem, v) in b.r.items():
                    if k != eng:
                        self._need(eng, deps, k, sem, v)
        for b in writes:
            if b.w is not None:
                self._need(eng, deps, *b.w)
            for k, (sem, v) in b.r.items():
                self._need(eng, deps, k, sem, v)
        return deps

    def _emit_waits(self, eng, deps):
        h = self.h[eng]
        for key, (sem, val) in deps.items():
            self.waited[eng][key] = val
            self.n_wait += 1
            self.q[eng].append(lambda h=h, sem=sem, val=val: h.wait_ge(sem, val))

    def _mark(self, key, sem, val, reads, writes):
        for b in reads:
            p = b.r.get(key)
            if p is None or p[1] < val:
                b.r[key] = (sem, val)
        for b in writes:
            b.w = (key, sem, val)
            b.r = {}

    def op(self, eng, fn, reads=(), writes=()):
        self._emit_waits(eng, self._collect(eng, reads, writes))
        self.cnt[eng] += 1
        idx = self.cnt[eng]
        sem = self.sem[eng]
        h = self.h[eng]
        self.q[eng].append(lambda: fn(h).then_inc(sem, 1))
        self._mark(eng, sem, idx, reads, writes)
        self.n_inst += 1
        return idx

    def dma(self, eng, dsem, out, in_, reads=(), writes=()):
        self._emit_waits(eng, self._collect(eng, reads, writes))
        dsem["val"] += 16
        val = dsem["val"]
        h = self.h[eng]
        s = dsem["sem"]
        self.q[eng].append(lambda: h.dma_start(out=out, in_=in_).then_inc(s, 16))
        self._mark("dma:" + dsem["name"], s, val, reads, writes)
        self.n_inst += 1

    def wait_all_dma(self, eng):
        h = self.h[eng]
        for d in self.dma_sems:
            if d["val"] > 0:
                self.q[eng].append(lambda h=h, s=d["sem"], v=d["val"]: h.wait_ge(s, v))

    def barrier(self):
        for e in self.ENG:
            h = self.h[e]
            for e2 in self.ENG:
                if e2 != e and self.cnt[e2] > self.waited[e].get(e2, 0):
                    v = self.cnt[e2]
                    self.waited[e][e2] = v
                    self.q[e].append(lambda h=h, s=self.sem[e2], v=v: h.wait_ge(s, v))
            if self.cnt[e] > self.waited[e].get(e, 0):
                v = self.cnt[e]
                self.waited[e][e] = v
                self.q[e].append(lambda h=h, s=self.sem[e], v=v: h.wait_ge(s, v))
            for d in self.dma_sems:
                k = "dma:" + d["name"]
                if d["val"] > self.waited[e].get(k, 0):
                    v = d["val"]
                    self.waited[e][k] = v
                    self.q[e].append(lambda h=h, s=d["sem"], v=v: h.wait_ge(s, v))

    def flush(self):
        block = self.es.enter_context(self.nc.Block())
        q = self.q

        @block.tensor
        def _(t):
            for f in q["pe"]:
                f()

        @block.scalar
        def _(t):
            for f in q["act"]:
                f()

        @block.vector
        def _(t):
            for f in q["dve"]:
                f()

        @block.gpsimd
        def _(t):
            for f in q["pool"]:
                f()

        @block.sync
        def _(t):
            for f in q["sp"]:
                f()

    def mm(self, out, lhsT, rhs, start=True, stop=True):
        return self.op("pe", lambda t: t.matmul(out.ap, lhsT=lhsT.ap, rhs=rhs.ap, start=start, stop=stop),
                       reads=lhsT.b + rhs.b + ([] if start else out.b), writes=out.b)

    def tr(self, out, in_, ident):
        return self.op("pe", lambda t: t.transpose(out=out.ap, in_=in_.ap, identity=ident.ap),
                       reads=in_.b + ident.b, writes=out.b)

    def act(self, out, in_, func, bias=None, scale=1.0, accum=None, eng="act"):
        rd = list(in_.b)
        kw = {}
        if bias is not None:
            if isinstance(bias, T):
                rd += bias.b
                kw["bias"] = bias.ap
            else:
                kw["bias"] = bias
        if isinstance(scale, T):
            rd += scale.b
            kw["scale"] = scale.ap
        else:
            kw["scale"] = scale
        wr = list(out.b)
        if accum is not None:
            wr += accum.b
            kw["accum_out"] = accum.ap
        return self.op("act", lambda a: a.activation(out=out.ap, in_=in_.ap, func=func, **kw), reads=rd, writes=wr)

    def tt(self, eng, out, a, b, op):
        return self.op(eng, lambda v: v.tensor_tensor(out=out.ap, in0=a.ap, in1=b.ap, op=op),
                       reads=a.b + b.b, writes=out.b)

    def ts(self, eng, out, a, s1, op0, s2=None, op1=None):
        rd = list(a.b)
        v1 = s1
        if isinstance(s1, T):
            rd += s1.b
            v1 = s1.ap
        v2 = s2
        if isinstance(s2, T):
            rd += s2.b
            v2 = s2.ap
        if op1 is None:
            return self.op(eng, lambda v: v.tensor_scalar(out=out.ap, in0=a.ap, scalar1=v1, scalar2=None, op0=op0),
                           reads=rd, writes=out.b)
        return self.op(eng, lambda v: v.tensor_scalar(out=out.ap, in0=a.ap, scalar1=v1, scalar2=v2, op0=op0, op1=op1),
                       reads=rd, writes=out.b)

    def stt(self, eng, out, a, s, b, op0, op1):
        rd = a.b + b.b
        sv = s
        if isinstance(s, T):
            rd = rd + s.b
            sv = s.ap
        return self.op(eng, lambda v: v.scalar_tensor_tensor(out=out.ap, in0=a.ap, scalar=sv, in1=b.ap, op0=op0, op1=op1),
                       reads=rd, writes=out.b)

    def copy(self, eng, out, in_):
        if eng == "act":
            return self.op("act", lambda a: a.copy(out=out.ap, in_=in_.ap), reads=in_.b, writes=out.b)
        return self.op(eng, lambda v: v.tensor_copy(out=out.ap, in_=in_.ap), reads=in_.b, writes=out.b)

    def memset(self, eng, out, val):
        return self.op(eng, lambda v: v.memset(out.ap, val), writes=out.b)

    def ld(self, eng, dsem, out, in_ap):
        return self.dma(eng, dsem, out.ap, in_ap, writes=out.b)

    def st(self, eng, dsem, out_ap, in_):
        return self.dma(eng, dsem, out_ap, in_.ap, reads=in_.b)


class T:
    __slots__ = ("ap", "b")

    def __init__(self, ap, b):
        self.ap = ap
        self.b = b if isinstance(b, list) else [b]

    def __getitem__(self, k):
        return T(self.ap[k], self.b)

    def wb(self, b):
        return T(self.ap, b)


class Arena:
    def __init__(self, ap, nelem):
        self.ap = ap
        self.n = nelem
        self.off = 0

    def mark(self):
        return self.off

    def reset(self, m):
        self.off = m

    def alloc(self, shape, dt, name):
        n = int(np.prod(shape[1:]))
        units = n * (2 if dt == F32 else 1)
        if self.off % 2:
            self.off += 1
        a = self.ap[:, self.off:self.off + units]
        self.off += units
        assert self.off <= self.n, ("SBUF arena overflow", name, self.off, self.n)
        if dt == F32:
            a = a.bitcast(F32)
        if len(shape) == 3:
            a = a.rearrange("p (a b) -> p a b", a=shape[1])
        elif len(shape) == 4:
            a = a.rearrange("p (a b c) -> p a b c", a=shape[1], b=shape[2])
        return T(a, Buf(name))


S = 4096
D = 1024
NT = S // 128
NKT = D // 128
NH = 8
IN_DIM = 6688
OFF_P, OFF_Q, OFF_K, OFF_V, OFF_Z, OFF_BG, OFF_GP, OFF_GD = 0, 512, 1536, 2560, 3584, 4608, 4640, 5664
FFN = 2816
NFT = FFN // 128
POOL_WINDOWS = (2, 4, 8, 16)
EPS = 1e-6

MASKS = ("ident", "ones", "up_i", "up_s", "lo_i", "lo_s", "blk", "off_lo", "blk2", "off_up")
BANDS = ("C", "L", "R", "CF", "CL")


def host_constants():
    p = np.arange(128)[:, None]
    f = np.arange(128)[None, :]
    m = {}
    m["ident"] = (p == f)
    m["ones"] = np.ones((128, 128), bool)
    m["up_s"] = f > p
    m["lo_s"] = f < p
    m["up_i"] = f >= p
    m["lo_i"] = f <= p
    m["blk"] = (p < 64) == (f < 64)
    m["off_lo"] = (p >= 64) & (f < 64)
    m["off_up"] = (p < 64) & (f >= 64)
    m["blk2"] = m["blk"]
    cols = [m[k].astype(np.float32) for k in MASKS]
    for w in POOL_WINDOWS:
        lo, hi = w // 2, w - 1 - w // 2
        mats = {k: np.zeros((128, 128), np.float32) for k in BANDS}
        for i in range(128):
            for tp in range(i - lo, i + hi + 1):
                if 0 <= tp < 128:
                    mats["C"][tp, i] += 1.0 / w
                elif tp < 0:
                    mats["L"][tp + 128, i] += 1.0 / w
                else:
                    mats["R"][tp - 128, i] += 1.0 / w
                a0 = max(i - lo, 0)
                cntf = (i + hi) - a0 + 1
                if 0 <= tp < 128:
                    mats["CF"][tp, i] += 1.0 / cntf
                b1 = min(i + hi, 127)
                cntl = b1 - (i - lo) + 1
                if 0 <= tp < 128:
                    mats["CL"][tp, i] += 1.0 / cntl
        for i in range(128):
            a0 = max(i - lo, 0)
            cnt = (i + hi) - a0 + 1
            for tp in range(a0, min(i + hi, 127) + 1):
                mats["CF"][tp, i] = 1.0 / cnt
            b1 = min(i + hi, 127)
            cnt = b1 - (i - lo) + 1
            for tp in range(max(i - lo, 0), b1 + 1):
                mats["CL"][tp, i] = 1.0 / cnt
        for k in ("C", "CF", "CL"):
            mats[k] -= np.eye(128, dtype=np.float32)
        cols += [mats[k] for k in BANDS]
    return np.ascontiguousarray(np.concatenate(cols, axis=1))


NCST = (len(MASKS) + 4 * len(BANDS)) * 128


class Ring:
    def __init__(self, items):
        self.items = items
        self.i = 0

    def next(self):
        it = self.items[self.i % len(self.items)]
        self.i += 1
        return it


def build_nc(dbg=(), nheads=NH, phase3=True, stop=None):
    nc = bass.Bass("TRN2", target_bir_lowering=False)

    def din(name, shape, dtype=F32):
        return nc.dram_tensor(name, list(shape), dtype, kind="ExternalInput").ap()

    x = din("x", [S, D])
    w_in = din("w_in", [D, IN_DIM])
    pool_w = din("pool_w", [4, 128, 128])
    pool_out = din("pool_out", [512, D])
    dn_out = din("dn_out", [D, D])
    w_o = din("w_o", [D, D])
    ffn_up = din("ffn_up", [D, 2 * FFN])
    ffn_down = din("ffn_down", [FFN, D])
    cw_d = din("cw", [128, 24 * 5])
    fcw_d = din("fcw", [128, 44 * 3])
    fcb_d = din("fcb", [128, 44])
    psc_d = din("psc", [128, 4])
    nw1_d = din("nw1", [128, D])
    nw2_d = din("nw2", [128, D])
    nwf_d = din("nwf", [128, D])
    dnw_d = din("dnw", [128, 128])
    alog_d = din("alog", [128, 16])
    dtb_d = din("dtb", [128, 16])
    cst_d = din("cst", [128, NCST])
    out = nc.dram_tensor("out", [S, D], F32, kind="ExternalOutput").ap()
    onT_d = nc.dram_tensor("onT_d", [NH, 128, S], BF16).ap()
    x1_d = nc.dram_tensor("x1_d", [S, D], F32).ap()
    dbg_out = {}
    for name, shape in dbg:
        dbg_out[name] = nc.dram_tensor("dbg_" + name, list(shape), F32, kind="ExternalOutput").ap()

    es = ExitStack()
    with es:
        E = es.enter_context
        P = Prog(nc, es)
        NU = 106400
        arena_t = E(nc.sbuf_tensor("arena", [128, NU], BF16))
        A = Arena(arena_t[:, :], NU)
        pbig_t = [E(nc.psum_tensor("pbig%d" % i, [128, 512], F32)) for i in range(2)]
        psm_t = [E(nc.psum_tensor("psm%d" % i, [128, 512], F32)) for i in range(4)]
        pbf_t = [E(nc.psum_tensor("pbf%d" % i, [128, 1024], BF16)) for i in range(2)]

        def psum_sets(tag, sm_on_big=False):
            bigb = [Buf("pbig%s%d" % (tag, i), True) for i in range(len(pbig_t))]
            big = Ring([T(t[:, :], bigb[i]) for i, t in enumerate(pbig_t)])
            smb = [Buf("psm%s%d" % (tag, i), True) for i in range(len(psm_t))]
            sm_t, sm_bufs = list(psm_t), list(smb)
            if sm_on_big:
                sm_t, sm_bufs = sm_t + list(pbig_t), sm_bufs + bigb
            sm = Ring([T(t[:, j * 128:(j + 1) * 128], sm_bufs[i]) for j in range(4) for i, t in enumerate(sm_t)])
            sm2 = Ring([(T(t[:, j * 256:(j + 1) * 256], sm_bufs[i]), T(t[:, j * 256:j * 256 + 128], sm_bufs[i]),
                         T(t[:, j * 256 + 128:(j + 1) * 256], sm_bufs[i])) for j in range(2) for i, t in enumerate(sm_t)])
            psum_sets.sm2 = sm2
            bfbuf = [Buf("pbf%s%d" % (tag, i), True) for i in range(len(pbf_t))]
            bfs = Ring([T(t[:, j * 128:(j + 1) * 128], bfbuf[i]) for j in range(8) for i, t in enumerate(pbf_t)])
            bfb = Ring([T(t[:, :], bfbuf[i]) for i, t in enumerate(pbf_t)])
            return big, sm, bfb, bfs

        dsem_i = [0]

        def dsem(name):
            dsem_i[0] += 1
            return P.new_dma_sem("%s_%d" % (name, dsem_i[0]))

        d_misc = dsem("misc")
        d_miscp = dsem("miscp")
        d_dbg = dsem("dbg")

        hT = A.alloc([128, NKT, S], BF16, "hT")
        hT_b = [Buf("hT%d" % i) for i in range(8)]

        def hT_blk(k, blk):
            return T(hT.ap[:, k, blk * 512:(blk + 1) * 512], hT_b[blk])

        def hT_tile(k, t):
            return T(hT.ap[:, k, t * 128:(t + 1) * 128], hT_b[t // 4])

        cstb = A.alloc([128, len(MASKS) * 128], BF16, "cstb")
        cstf = A.alloc([128, 4 * 128], F32, "cstf")
        P.ld("pool", d_miscp, cstb, cst_d[:, 0:len(MASKS) * 128])
        mb = {k: cstb[:, i * 128:(i + 1) * 128] for i, k in enumerate(MASKS)}
        for nm, first in (("pu", "up_i"), ("pl", "lo_i"), ("bo", "blk"), ("bu", "blk2")):
            i0 = MASKS.index(first)
            mb[nm] = cstb[:, i0 * 128:(i0 + 2) * 128]
        for j, k in enumerate(("ident", "ones", "up_i", "lo_i")):
            i = MASKS.index(k)
            P.ld("sp", d_misc, cstf[:, j * 128:(j + 1) * 128], cst_d[:, i * 128:(i + 1) * 128])
        ident_f, ones_f, upi_f, loi_f = [cstf[:, j * 128:(j + 1) * 128] for j in range(4)]
        cw = A.alloc([128, 24, 5], F32, "cw")
        P.ld("sp", d_misc, cw, cw_d.rearrange("p (a b) -> p a b", a=24))
        fcw = A.alloc([128, 44, 3], F32, "fcw")
        P.ld("sp", d_misc, fcw, fcw_d.rearrange("p (a b) -> p a b", a=44))
        fcb = A.alloc([128, 44], F32, "fcb")
        P.ld("sp", d_misc, fcb, fcb_d[:, :])
        psc = A.alloc([128, 4], F32, "psc")
        P.ld("sp", d_misc, psc, psc_d[:, :])
        dnw = A.alloc([128, 128], F32, "dnw")
        P.ld("sp", d_misc, dnw, dnw_d[:, :])
        alog = A.alloc([128, 16], F32, "alog")
        P.ld("sp", d_misc, alog, alog_d[:, :])
        dtb = A.alloc([128, 16], F32, "dtb")
        P.ld("sp", d_misc, dtb, dtb_d[:, :])
        epsc = A.alloc([128, 1], F32, "epsc")
        P.memset("dve", epsc, EPS)
        mhalf = A.alloc([128, 1], F32, "mhalf")
        P.memset("dve", mhalf, -0.5)
        base_mark = A.mark()

        def dump(name, src):
            if name in dbg_out:
                P.st("sp", d_dbg, dbg_out[name], src)

        def rmsnorm_to_bf16(xt, nwB, hb, junk, st):
            P.act(junk, xt, AF.Square, accum=st[:, 0:1])
            P.act(st[:, 1:2], st[:, 0:1], AF.Sqrt, bias=epsc[:, 0:1], scale=1.0 / D)
            P.op("dve", lambda v: v.reciprocal(out=st.ap[:, 2:3], in_=st.ap[:, 1:2]), reads=st.b, writes=st.b)
            P.stt("dve", hb, xt, st[:, 2:3], nwB, ALU.mult, ALU.mult)

        def transpose_to_hT(hb, t, pbank, eng):
            pv = T(pbank.ap.rearrange("p (a b) -> p a b", a=8), pbank.b)
            for k in range(NKT):
                P.tr(pv[:, k, :], hb[:, k * 128:(k + 1) * 128], mb["ident"])
            dst = T(hT.ap[:, :, t * 128:(t + 1) * 128], hT_b[t // 4])
            P.copy(eng, dst, pv)

        big, sm, bfb, bfs = psum_sets("p1")
        nwB = A.alloc([128, D], F32, "nwB")
        P.ld("sp", d_misc, nwB, nw1_d[:, :])
        xts = Ring([A.alloc([128, D], F32, "xt%d" % i) for i in range(3)])
        xds = Ring([dsem("x") for i in range(3)])
        hbs = Ring([A.alloc([128, D], BF16, "hb%d" % i) for i in range(2)])
        junk = A.alloc([128, D], BF16, "junk")
        sts = Ring([A.alloc([128, 4], F32, "st%d" % i) for i in range(3)])
        def p1_norm(t):
            xt = xts.next()
            P.ld("sp", xds.next(), xt, x[t * 128:(t + 1) * 128, :])
            hb = hbs.next()
            rmsnorm_to_bf16(xt, nwB, hb, junk, sts.next())
            return hb

        hb_cur = p1_norm(0)
        for t in range(NT):
            hb_nxt = p1_norm(t + 1) if t + 1 < NT else None
            transpose_to_hT(hb_cur, t, bfb.next(), "act" if t % 2 else "dve")
            hb_cur = hb_nxt
        if "hT" in dbg_out:
            tmpf = A.alloc([128, S], F32, "dbg_hT")
            for k in range(NKT):
                P.copy("dve", tmpf, T(hT.ap[:, k, :], hT_b))
                P.st("sp", d_dbg, dbg_out["hT"][k], tmpf)
        if stop == 1:
            P.barrier()
            P.wait_all_dma("sp")
            P.flush()
            return nc
        P.barrier()
        A.reset(base_mark)

        big, sm, bfb, bfs = psum_sets("p20")
        GSH = [128, 2, NT, NH]
        beta = A.alloc(GSH, F32, "beta")
        Gc = A.alloc(GSH, F32, "Gc")
        GB = A.alloc(GSH, F32, "GB")
        eG = A.alloc(GSH, F32, "eG")
        bG = A.alloc(GSH, F32, "bG")
        eGd = A.alloc(GSH, F32, "eGd")
        gl = A.alloc(GSH, F32, "gl")
        invb = A.alloc(GSH, F32, "invb")
        p2_mark = A.mark()
        raw = A.alloc([128, NT, 32], F32, "raw")
        gg = A.alloc(GSH, F32, "gg")
        lnb = A.alloc(GSH, F32, "lnb")
        Gtot = A.alloc(GSH, F32, "Gtot")
        negA = A.alloc([128, 16], F32, "negA")
        wbg = A.alloc([128, NKT, 32], BF16, "wbg")
        P.ld("pool", d_miscp, wbg, w_in[:, OFF_BG:OFF_BG + 32].rearrange("(k p) c -> p k c", p=128))
        for t in range(NT):
            ps = sm.next()
            for k in range(NKT):
                P.mm(ps[:, 0:32], hT_tile(k, t), wbg[:, k, :], start=(k == 0), stop=(k == NKT - 1))
            P.copy("act" if t % 2 else "dve", raw[:, t, :], ps[:, 0:32])
        raw5 = T(raw.ap.rearrange("p t (k d h) -> p k d t h", k=2, d=2), raw.b)
        P.act(beta, raw5[:, 0], AF.Sigmoid)
        P.act(lnb, beta, AF.Ln)
        P.op("dve", lambda v: v.reciprocal(out=invb.ap, in_=beta.ap), reads=beta.b, writes=invb.b)
        P.act(negA, alog, AF.Exp)
        P.ts("dve", negA, negA, -1.0, ALU.mult)
        dtb4 = T(dtb.ap.rearrange("p (d h) -> p d h", d=2).unsqueeze(2).broadcast_to(GSH), dtb.b)
        negA4 = T(negA.ap.rearrange("p (d h) -> p d h", d=2).unsqueeze(2).broadcast_to(GSH), negA.b)
        P.tt("dve", gg, raw5[:, 1], dtb4, ALU.add)
        P.act(gg, gg, AF.Exp)
        P.act(gg, gg, AF.Ln, bias=1.0)
        P.tt("dve", gg, gg, negA4, ALU.mult)
        ggf = T(gg.ap.rearrange("p d t h -> p d (t h)"), gg.b)
        pc = big.next()
        P.mm(pc[:, 0:256], upi_f, ggf[:, 0], True, True)
        P.mm(pc[:, 256:512], loi_f, ggf[:, 1], True, True)
        P.copy("dve", T(Gc.ap.rearrange("p d t h -> p (d t h)"), Gc.b), pc)
        pt_ = big.next()
        P.mm(pt_, ones_f, T(gg.ap.rearrange("p d t h -> p (d t h)"), gg.b), True, True)
        P.copy("act", T(Gtot.ap.rearrange("p d t h -> p (d t h)"), Gtot.b), pt_)
        P.tt("dve", GB, Gc, lnb, ALU.add)
        P.act(eG, Gc, AF.Exp)
        P.tt("dve", bG, beta, eG, ALU.mult)
        P.tt("dve", eGd, Gtot, Gc, ALU.subtract)
        P.act(eGd, eGd, AF.Exp)
        P.act(gl, Gtot, AF.Exp)
        dump("raw", T(raw.ap.rearrange("p t c -> p (t c)"), raw.b))
        for nm, tt_ in (("beta", beta), ("Gc", Gc), ("gg", gg)):
            dump(nm, T(tt_.ap.rearrange("p d t h -> p (d t h)"), tt_.b))
        if stop == 2:
            P.barrier()
            P.wait_all_dma("sp")
            P.flush()
            return nc
        hT_d = nc.dram_tensor("hT_d", [8, 128, NKT, 512], BF16).ap()
        d_hst = dsem("hst")
        for blk in range(8):
            P.dma("sp", d_hst, hT_d[blk], hT.ap[:, :, blk * 512:(blk + 1) * 512], reads=[hT_b[blk]])
        P.barrier()
        A.reset(p2_mark)
        A2 = Arena(arena_t[:, 0:NKT * S], NKT * S)

        big, sm, bfb, bfs = psum_sets("p2", sm_on_big=True)
        PRE0 = 2
        NS = 4
        qkv_all = [A2.alloc([128, 3 * S], BF16, "qkv_all%d" % i) for i in range(2)]
        qkv_b = [[[Buf("%s%d_%d" % (nm, si, i)) for i in range(8)] for nm in ("qT", "kT", "vT")] for si in range(2)]
        zs_l = [A2.alloc([128, NT, 128], BF16, "zs%d" % i) for i in range(2)]
        zs_b = [[Buf("zs%d_%d" % (si, i)) for i in range(NT)] for si in range(2)]
        o_dir = [A.alloc([128, NT, 128], BF16, "o_f"), A.alloc([128, NT, 128], BF16, "o_b")]
        o_b = [[Buf("o%d_%d" % (d, i)) for i in range(NT)] for d in range(2)]
        d_w = [dsem("w%d" % i) for i in range(4)]
        d_on = dsem("on")
        Sst = [A.alloc([128, 128], F32, "S%d" % d) for d in range(2)]
        Sbf = [A.alloc([128, 128], BF16, "Sbf%d" % d) for d in range(2)]
        ssn = A.alloc([128, NT], F32, "ssn")
        junk128 = A.alloc([128, 128], BF16, "junk128")
        pre = A.alloc([128, S + 4], F32, "pre")
        pre_b = [Buf("pre%d" % i) for i in range(8)]
        accs = Ring([A.alloc([128, 512], F32, "acc%d" % i) for i in range(1)])
        sqs = Ring([A.alloc([128, 512], BF16, "sq%d" % i) for i in range(2)])
        rinvs = Ring([A.alloc([128, 512], F32, "rinv%d" % i) for i in range(1)])
        wqkvz = [A.alloc([128, NKT, 128], BF16, "w%s" % nm) for nm in "qkvz"]
        hring = Ring([(A.alloc([128, NKT, 512], BF16, "hblk%d" % i), dsem("hblk%d" % i)) for i in range(2)])

        def mk_shared(i):
            d = {}
            for nm in ("kbg", "kd", "vb"):
                d[nm] = A.alloc([128, 128], BF16, "%s_%d" % (nm, i))
            pair = A.alloc([128, 256], BF16, "QKKK_%d" % i)
            d["QKKK"] = pair
            d["QK"] = T(pair.ap[:, 0:128], pair.b)
            d["KK"] = T(pair.ap[:, 128:256], pair.b)
            return d

        def mk_dir(i):
            d = {}
            for nm in ("u", "tmpo"):
                d[nm] = A.alloc([128, 128], F32, "%s_%d" % (nm, i))
            pairl = A.alloc([128, 256], BF16, "LdLo_%d" % i)
            d["LdLo"] = pairl
            d["Ld"] = T(pairl.ap[:, 0:128], pairl.b)
            d["Lo"] = T(pairl.ap[:, 128:256], pairl.b)
            for nm in ("E2", "N", "Nd", "attnT", "MTa", "MTb", "Xd", "nT1",
                       "XT", "wT", "vnew"):
                d[nm] = A.alloc([128, 128], BF16, "%s_%d" % (nm, i))
            for ab in "ab":
                pair = A.alloc([128, 256], BF16, "MQ%s_%d" % (ab, i))
                d["M" + ab] = T(pair.ap[:, 0:128], Buf("M%s_%d" % (ab, i)))
                d["Q" + ab] = T(pair.ap[:, 128:256], Buf("Q%s_%d" % (ab, i)))
                d["MQ" + ab] = T(pair.ap, d["M" + ab].b + d["Q" + ab].b)
            return d

        shared_slots = [[mk_shared(d * NS + i) for i in range(NS)] for d in range(2)]
        dir_slots = [[mk_dir(d * NS + i) for i in range(NS)] for d in range(2)]
        cur = {"si": 0}

        def col(arr, d, t, h):
            return T(arr.ap[:, d, t, h:h + 1], arr.b)

        def qkv_T(si, Xi, lo, hi, bufs):
            return T(qkv_all[si].ap[:, Xi * S + lo: Xi * S + hi], bufs)

        def tile_of(Xi, t):
            si = cur["si"]
            return qkv_T(si, Xi, t * 128, (t + 1) * 128, [qkv_b[si][Xi][t // 4]])

        hstate = {"pending": None}

        def issue_hblk(blk):
            slot, ds_ = hring.next()
            P.dma("sp", ds_, slot.ap, hT_d[blk], writes=slot.b)
            hstate["pending"] = (blk, slot)

        def load_hblk(blk, nxt=None):
            if hstate["pending"] is None or hstate["pending"][0] != blk:
                issue_hblk(blk)
            slot = hstate["pending"][1]
            hstate["pending"] = None
            if nxt is not None:
                issue_hblk(nxt)
            return slot

        def conv_block(h, Xi, blk, si):
            acc = accs.next()
            rb = [pre_b[b_] for b_ in (blk - 1, blk, blk + 1) if 0 <= b_ < 8]
            for s_ in range(5):
                src = T(pre.ap[:, blk * 512 + s_: blk * 512 + s_ + 512], rb)
                wcol = T(cw.ap[:, Xi * 8 + h, s_:s_ + 1], cw.b)
                if s_ == 0:
                    P.ts("dve", acc, src, wcol, ALU.mult)
                else:
                    P.stt("dve", acc, src, wcol, acc, ALU.mult, ALU.add)
                if s_ % 2 == 1:
                    yield
            dst = qkv_T(si, Xi, blk * 512, (blk + 1) * 512, [qkv_b[si][Xi][blk]])
            rinv = rinvs.next()
            P.act(rinv, acc, AF.Exp, scale=-1.0)
            P.act(rinv, rinv, AF.Ln, bias=1.0)
            P.act(rinv, rinv, AF.Exp, scale=-1.0)
            yield
            if Xi == 2:
                P.tt("dve", dst, acc, rinv, ALU.mult)
                yield
            else:
                P.tt("dve", acc, acc, rinv, ALU.mult)
                yield
                sq = sqs.next()
                P.tt("pool", sq, acc, acc, ALU.mult)
                yield
                ps = big.next()
                P.mm(ps, mb["ones"], sq, True, True)
                P.act(rinv, ps, AF.Ln, bias=epsc[:, 0:1], scale=1.0)
                yield
                P.act(rinv, rinv, AF.Exp, scale=-0.5)
                yield
                if Xi == 0:
                    P.stt("dve", dst, acc, float(128 ** -0.5), rinv, ALU.mult, ALU.mult)
                else:
                    P.tt("dve", dst, acc, rinv, ALU.mult)
                yield

        def proj_gens(h, si):
            st = {"a": 0, "b": 0}

            def gen_a():
                for Xi, woff in enumerate((OFF_Q, OFF_K, OFF_V, OFF_Z)):
                    P.ld("pool", d_w[Xi], wqkvz[Xi],
                         w_in[:, woff + h * 128: woff + (h + 1) * 128].rearrange("(k p) c -> p k c", p=128))
                yield
                for Xi in range(3):
                    wX = wqkvz[Xi]
                    for blk in range(8):
                        i = 8 * Xi + blk
                        need = 0 if Xi == 0 else min(i - 6, 8 * Xi)
                        while st["b"] < need:
                            yield (lambda need=need: st["b"] >= need)
                        hb_ = load_hblk(blk, (blk + 1) % 8)
                        ps = big.next()
                        for k in range(NKT):
                            P.mm(ps, wX[:, k, :], hb_[:, k, :], start=(k == 0), stop=(k == NKT - 1))
                        P.copy("act", T(pre.ap[:, PRE0 + blk * 512: PRE0 + (blk + 1) * 512], [pre_b[blk]]), ps)
                        st["a"] = i + 1
                        yield
                while st["b"] < 24:
                    yield (lambda: st["b"] >= 24)
                wz = wqkvz[3]
                zflat = zs_l[si].ap.rearrange("p t e -> p (t e)")
                for blk in range(8):
                    hb_ = load_hblk(blk, blk + 1 if blk < 7 else None)
                    ps = big.next()
                    for k in range(NKT):
                        P.mm(ps, wz[:, k, :], hb_[:, k, :], start=(k == 0), stop=(k == NKT - 1))
                    acc = accs.next()
                    rinv = rinvs.next()
                    P.act(rinv, ps, AF.Exp, scale=-1.0)
                    P.act(acc, ps, AF.Copy)
                    yield
                    P.act(rinv, rinv, AF.Ln, bias=1.0)
                    P.act(rinv, rinv, AF.Exp, scale=-1.0)
                    yield
                    P.tt("dve", T(zflat[:, blk * 512:(blk + 1) * 512], zs_b[si][blk * 4:(blk + 1) * 4]), acc, rinv,
                         ALU.mult)
                    yield

            def gen_b():
                for Xi in range(3):
                    for blk in range(8):
                        i = 8 * Xi + blk
                        need = min(i + 2, 8 * Xi + 8)
                        while st["a"] < need:
                            yield (lambda need=need: st["a"] >= need)
                        yield from conv_block(h, Xi, blk, si)
                        st["b"] = i + 1

            return [gen_a(), gen_b()]

        def prep_tile(h, t, d, sh):
            ud = "up" if d == 0 else "lo"
            kt, qt, vt = tile_of(1, t), tile_of(0, t), tile_of(2, t)
            pk = bfs.next()
            P.tr(pk, kt, mb["ident"])
            P.act(sh["kbg"], pk, AF.Copy, scale=col(bG, d, t, h))
            P.ts("dve", sh["kd"], pk, col(eGd, d, t, h), ALU.mult)
            yield
            pv = bfs.next()
            P.tr(pv, vt, mb["ident"])
            P.act(sh["vb"], pv, AF.Copy, scale=col(beta, d, t, h))
            yield
            pboth, pqk, pkk = psum_sets.sm2.next()
            si_ = cur["si"]
            qk_rhs = T(qkv_all[si_].ap[:, 0:2 * S].rearrange("p (x s) -> p x s", x=2)[:, :, t * 128:(t + 1) * 128],
                       [qkv_b[si_][0][t // 4], qkv_b[si_][1][t // 4]])
            P.mm(T(pboth.ap.rearrange("p (x s) -> p x s", x=2), pboth.b), kt, qk_rhs, True, True)
            P.tt("dve", sh["QKKK"], pboth, mb["pu" if d == 0 else "pl"], ALU.mult)
            yield

        def local_gen(h, t, d, sh, ds):
            dg_, te_ = ds["tmpo"], ds["u"]
            P.ts("dve", dg_, ident_f, col(GB, d, t, h), ALU.mult)
            yield
            prow = sm.next()
            P.mm(prow, ones_f, dg_, True, True)
            P.ts("dve", te_, prow, col(Gc, d, t, h), ALU.subtract, 0.0, ALU.min)
            yield
            P.act(ds["E2"], te_, AF.Exp)
            yield
            P.tt("pool", ds["N"], sh["KK"], ds["E2"], ALU.mult)
            yield
            P.tt("pool", ds["attnT"], sh["QK"], ds["E2"], ALU.mult)
            P.tt("pool", ds["Nd"], ds["N"], mb["blk"], ALU.mult)
            yield
            pl = bfs.next()
            P.tr(pl, ds["N"], mb["ident"])
            pl2 = T(pl.ap.unsqueeze(1).broadcast_to([128, 2, 128]), pl.b)
            P.tt("dve", T(ds["LdLo"].ap.rearrange("p (x s) -> p x s", x=2), ds["LdLo"].b), pl2,
                 T(mb["bo" if d == 0 else "bu"].ap.rearrange("p (x s) -> p x s", x=2), mb["bo"].b), ALU.mult)
            yield
            P.tt("pool", ds["Qa"], mb["ident"], ds["Nd"], ALU.subtract)
            yield
            M, MT, Q = ds["Nd"], ds["Ld"], ds["Qa"]
            pa = sm.next()
            P.mm(pa, MT, M, True, True)
            P.copy("act", ds["Ma"], pa)
            yield
            pb = sm.next()
            P.mm(pb, M, MT, True, True)
            P.copy("act", ds["MTa"], pb)
            yield
            M, MT = ds["Ma"], ds["MTa"]
            for lvl in range(1, 6):
                nM, nMT = (ds["Mb"], ds["MTb"]) if lvl % 2 else (ds["Ma"], ds["MTa"])
                nQ = ds["Qb"] if lvl % 2 else ds["Qa"]
                if lvl < 5:
                    pboth, p_m, p_q = psum_sets.sm2.next()
                    rhs = ds["MQa"] if (lvl % 2 == 1) else ds["MQb"]
                    P.mm(pboth, MT, rhs, True, True)
                    P.copy("act", nM, p_m)
                    P.tt("dve", nQ, Q, p_q, ALU.add)
                    yield
                    pb = sm.next()
                    P.mm(pb, M, MT, True, True)
                    P.copy("act" if lvl % 2 else "dve", nMT, pb)
                    yield
                else:
                    pq = sm.next()
                    P.mm(pq, MT, Q, True, True)
                    P.tt("dve", nQ, Q, pq, ALU.add)
                    yield
                M, MT, Q = nM, nMT, nQ
            Y = Q
            px = bfs.next()
            P.tr(px, Y, mb["ident"])
            P.copy("act", ds["Xd"], px)
            yield
            p1 = sm.next()
            P.mm(p1, ds["Lo"], Y, True, True)
            P.act(ds["nT1"], p1, AF.Copy, scale=-1.0)
            yield
            p2 = sm.next()
            P.mm(p2, ds["Xd"], ds["nT1"], True, True)
            P.tt("dve", ds["XT"], Y, p2, ALU.add)
            yield
            pw = sm.next()
            P.mm(pw, sh["kbg"], ds["XT"], True, True)
            P.copy("act", ds["wT"], pw)
            yield
            pu = sm.next()
            P.mm(pu, ds["XT"], sh["vb"], True, True)
            P.copy("dve", ds["u"], pu)
            yield

        def scan_gen(h, t, d, sh, ds):
            qt = tile_of(0, t)
            pa = sm.next()
            P.mm(pa, ds["wT"], Sbf[d], True, True)
            po1 = sm.next()
            P.mm(po1, qt, Sbf[d], True, True)
            P.tt("dve", ds["vnew"], ds["u"], pa, ALU.subtract)
            P.act(ds["tmpo"], po1, AF.Copy, scale=col(eG, d, t, h))
            yield
            psn = sm.next()
            P.mm(psn, sh["kd"], ds["vnew"], True, True)
            P.stt("dve", Sst[d], Sst[d], T(gl.ap[:, d, t, h:h + 1], gl.b), psn, ALU.mult, ALU.add)
            yield
            P.copy("act", Sbf[d], Sst[d])
            yield
            po2 = sm.next()
            P.mm(po2, ds["attnT"], ds["vnew"], True, True)
            P.stt("dve", o_dir[d][:, t, :].wb(o_b[d][t]), po2, col(invb, d, t, h), ds["tmpo"], ALU.mult, ALU.add)
            yield

        def run_sched(gens):
            active = [(g, None) for g in gens]
            while active:
                progressed = False
                nxt = []
                for g, cond in active:
                    if cond is not None and not cond():
                        nxt.append((g, cond))
                        continue
                    try:
                        c = next(g)
                        nxt.append((g, c))
                    except StopIteration:
                        pass
                    progressed = True
                active = nxt
                assert progressed or not active, "scheduler deadlock"

        def chain_gens(h):
            gens = []
            for d in range(2):
                order = list(range(NT)) if d == 0 else list(range(NT - 1, -1, -1))
                state = {"local_done": [False] * NT, "scan_done": 0}

                def worker(w, d=d, order=order, state=state):
                    for idx in range(w, NT, NS):
                        t = order[idx]
                        while state["scan_done"] < idx - NS + 1:
                            yield (lambda idx=idx: state["scan_done"] >= idx - NS + 1)
                        sh = shared_slots[d][idx % NS]
                        ds = dir_slots[d][idx % NS]
                        yield from prep_tile(h, t, d, sh)
                        yield from local_gen(h, t, d, sh, ds)
                        state["local_done"][idx] = True

                def scanner(d=d, order=order, state=state):
                    for idx, t in enumerate(order):
                        while not state["local_done"][idx]:
                            yield (lambda idx=idx: state["local_done"][idx])
                        yield from scan_gen(h, t, d, shared_slots[d][idx % NS], dir_slots[d][idx % NS])
                        state["scan_done"] = idx + 1

                gens += [worker(w) for w in range(NS)] + [scanner()]
            return gens

        def norm_stage(h, si):
            kvb = qkv_b[si][1] + qkv_b[si][2]
            osum = T(qkv_all[si].ap[:, S:3 * S].bitcast(F32).rearrange("p (t e) -> p t e", t=NT), kvb)
            onT_sb = T(qkv_all[si].ap[:, 0:S], qkv_b[si][0])
            zst = T(zs_l[si].ap, zs_b[si])
            P.tt("dve", osum, o_dir[0].wb(o_b[0]), o_dir[1].wb(o_b[1]), ALU.add)
            for t in range(NT):
                P.act(junk128, osum[:, t, :], AF.Square, accum=ssn[:, t:t + 1])
            P.act(ssn, ssn, AF.Ln, bias=epsc[:, 0:1], scale=1.0 / 128)
            P.act(ssn, ssn, AF.Exp, scale=-0.5)
            P.tt("dve", osum, osum, T(ssn.ap.unsqueeze(2).broadcast_to([128, NT, 128]), ssn.b), ALU.mult)
            P.tt("pool", osum, osum, T(dnw.ap.unsqueeze(1).broadcast_to([128, NT, 128]), dnw.b), ALU.mult)
            if h == 0:
                dump("on0", T(osum.ap.rearrange("p t e -> p (t e)"), osum.b))
            on_tok = o_dir[0].wb(o_b[0])
            P.copy("act", on_tok, osum)
            zflat = T(zs_l[si].ap.rearrange("p t e -> p (t e)"), zs_b[si])
            for t8 in range(4):
                pbk = bfb.next()
                pv = T(pbk.ap.rearrange("p (a b) -> p a b", a=8), pbk.b)
                for j in range(8):
                    P.tr(pv[:, j, :], on_tok[:, t8 * 8 + j, :], mb["ident"])
                P.tt("dve", T(onT_sb.ap[:, t8 * 1024:(t8 + 1) * 1024], onT_sb.b), pbk,
                     zflat[:, t8 * 1024:(t8 + 1) * 1024], ALU.mult)
            P.st("sp", d_on, onT_d[h], onT_sb)

        P.memset("pool", T(pre.ap[:, 0:PRE0], [pre_b[0]]), 0.0)
        P.memset("pool", T(pre.ap[:, PRE0 + S:PRE0 + S + 2], [pre_b[7]]), 0.0)
        run_sched(proj_gens(0, 0))
        for h in range(nheads):
            si = h % 2
            cur["si"] = si
            for d in range(2):
                P.memset("dve", Sst[d], 0.0)
                P.memset("pool", Sbf[d], 0.0)
            gens = chain_gens(h)
            if h + 1 < nheads:
                gens += proj_gens(h + 1, 1 - si)
            run_sched(gens)
            norm_stage(h, si)
        P.barrier()
        d_hld = dsem("hld")
        for blk in range(8):
            P.dma("sp", d_hld, hT.ap[:, :, blk * 512:(blk + 1) * 512], hT_d[blk], writes=[hT_b[blk]])
        P.barrier()
        A.reset(base_mark)
        if phase3:
            build_phase3(nc, P, A, locals())
        P.barrier()
        P.wait_all_dma("sp")
        P.flush()
    return nc


def build_phase3(nc, P, A, L):
    hT, hT_b, hT_blk, hT_tile = L["hT"], L["hT_b"], L["hT_blk"], L["hT_tile"]
    mb, cst_d, epsc, dsem, psum_sets = L["mb"], L["cst_d"], L["epsc"], L["dsem"], L["psum_sets"]
    x, out, w_in, x1_d, onT_d = L["x"], L["out"], L["w_in"], L["x1_d"], L["onT_d"]
    psc, fcw, fcb = L["psc"], L["fcw"], L["fcb"]
    base_mark = L["base_mark"]
    rmsnorm_to_bf16, dump = L["rmsnorm_to_bf16"], L["dump"]
    NB = len(MASKS) * 128

    big, sm, bfb, bfs = psum_sets("p3a")
    d_c = dsem("c3a")
    nwB = A.alloc([128, D], F32, "nwB2")
    P.ld("sp", d_c, nwB, L["nw2_d"][:, :])
    bands = A.alloc([128, 20, 128], BF16, "bands")
    P.ld("pool", dsem("bands"), bands, cst_d[:, NB:NB + 20 * 128].rearrange("p (a b) -> p a b", a=20))
    wp = A.alloc([128, NKT, 512], BF16, "wp")
    P.ld("pool", dsem("wp"), wp, w_in[:, OFF_P:OFF_P + 512].rearrange("(k p) c -> p k c", p=128))
    poolw = A.alloc([128, 4, 128], BF16, "poolw")
    P.ld("pool", dsem("poolw"), poolw, L["pool_w"].rearrange("g c d -> c g d"))
    wo_sb = A.alloc([128, NKT, D], BF16, "wo_sb")
    d_wo = dsem("wo")
    for k in range(NKT):
        P.ld("pool", d_wo, wo_sb[:, k, :], L["w_o"][k * 128:(k + 1) * 128, :])
    onTb = A.alloc([128, NH, 512], BF16, "onTb")
    d_onb = dsem("onb")
    p_tok = A.alloc([128, 8, 512], BF16, "p_tok")
    ptok_b = [Buf("ptok%d" % i) for i in range(8)]
    pm0T = A.alloc([128, 4, 512], BF16, "pm0T")
    pmT = A.alloc([128, 4, 512], BF16, "pmT")
    W_gp = A.alloc([128, NKT, D], BF16, "W_gp")
    W_gd = A.alloc([128, NKT, D], BF16, "W_gd")
    W_dn = A.alloc([128, NKT, D], BF16, "W_dn")
    W_po = A.alloc([128, 4, D], BF16, "W_po")
    d_W = [dsem("W%d" % i) for i in range(4)]
    for k in range(NKT):
        P.ld("pool", d_W[0], W_gp[:, k, :], w_in[k * 128:(k + 1) * 128, OFF_GP:OFF_GP + D])
        P.ld("pool", d_W[1], W_gd[:, k, :], w_in[k * 128:(k + 1) * 128, OFF_GD:OFF_GD + D])
        P.ld("pool", d_W[2], W_dn[:, k, :], L["dn_out"][k * 128:(k + 1) * 128, :])
        if k < 4:
            P.ld("pool", d_W[3], W_po[:, k, :], L["pool_out"][k * 128:(k + 1) * 128, :])
    tmpa = Ring([A.alloc([128, 512], F32, "tmpa%d" % i) for i in range(1)])
    tmpb = Ring([A.alloc([128, 512], F32, "tmpb%d" % i) for i in range(1)])
    mT = A.alloc([128, NKT, 512], BF16, "mT")
    xts = Ring([(A.alloc([128, D], F32, "xt3_%d" % i), dsem("xt3_%d" % i)) for i in range(1)])
    x1s = Ring([(A.alloc([128, D], F32, "x1_%d" % i), dsem("x1s_%d" % i)) for i in range(1)])
    hbs = Ring([A.alloc([128, D], BF16, "hb3_%d" % i) for i in range(1)])
    junk = None
    sts = Ring([A.alloc([128, 4], F32, "st3_%d" % i) for i in range(2)])
    ptok_done = set()

    def ensure_ptok(t):
        if t in ptok_done:
            return
        ptok_done.add(t)
        ps = big.next()
        for k in range(NKT):
            P.mm(ps, hT_tile(k, t), wp[:, k, :], start=(k == 0), stop=(k == NKT - 1))
        P.copy("act", T(p_tok.ap[:, t % 8, :], [ptok_b[t % 8]]), ps)

    def ptok(t, g):
        return T(p_tok.ap[:, t % 8, g * 128:(g + 1) * 128], [ptok_b[t % 8]])

    def band(g, kind):
        return bands[:, g * 5 + BANDS.index(kind), :]

    for b in range(8):
        t0 = b * 4
        P.ld("sp", d_onb, onTb, onT_d[:, :, b * 512:(b + 1) * 512].rearrange("h p s -> p h s"))
        for t in range(max(t0 - 1, 0), min(t0 + 5, NT)):
            ensure_ptok(t)
        for tt in range(4):
            t = t0 + tt
            for g in range(4):
                ps = sm.next()
                ck = "CF" if t == 0 else ("CL" if t == NT - 1 else "C")
                terms = [(ptok(t, g), band(g, ck))]
                if t > 0:
                    terms.append((ptok(t - 1, g), band(g, "L")))
                if t < NT - 1:
                    terms.append((ptok(t + 1, g), band(g, "R")))
                for i, (l_, r_) in enumerate(terms):
                    P.mm(ps, l_, r_, start=(i == 0), stop=(i == len(terms) - 1))
                P.copy("act" if g % 2 else "dve", pm0T[:, g, tt * 128:(tt + 1) * 128], ps)
        for g in range(4):
            ps = big.next()
            P.mm(ps, poolw[:, g, :], pm0T[:, g, :], True, True)
            P.ts("dve", pmT[:, g, :], ps, T(psc.ap[:, g:g + 1], psc.b), ALU.mult)
        for n in range(NKT):
            nsl = slice(n * 128, (n + 1) * 128)
            wgp, wpo, wgd, wdn = W_gp[:, :, nsl], W_po[:, :, nsl], W_gd[:, :, nsl], W_dn[:, :, nsl]
            ta, tb = tmpa.next(), tmpb.next()
            ps = big.next()
            for k in range(NKT):
                P.mm(ps, wgp[:, k, :], hT_blk(k, b), start=(k == 0), stop=(k == NKT - 1))
            P.act(ta, ps, AF.Sigmoid)
            ps = big.next()
            for k in range(4):
                P.mm(ps, wpo[:, k, :], pmT[:, k, :], start=(k == 0), stop=(k == 3))
            P.tt("dve", ta, ta, ps, ALU.mult)
            ps = big.next()
            for k in range(NKT):
                P.mm(ps, wgd[:, k, :], hT_blk(k, b), start=(k == 0), stop=(k == NKT - 1))
            P.act(tb, ps, AF.Sigmoid)
            ps = big.next()
            for k in range(NKT):
                P.mm(ps, wdn[:, k, :], onTb[:, k, :], start=(k == 0), stop=(k == NKT - 1))
            P.tt("dve", tb, tb, ps, ALU.mult)
            P.tt("dve", mT[:, n, :], ta, tb, ALU.add)
        def wo_mm(tt):
            pss = []
            for half in range(2):
                ps = big.next()
                for k in range(NKT):
                    P.mm(ps, mT[:, k, tt * 128:(tt + 1) * 128], wo_sb[:, k, half * 512:(half + 1) * 512],
                         start=(k == 0), stop=(k == NKT - 1))
                pss.append(ps)
            return pss

        def wo_add(tt, pss):
            t = t0 + tt
            xt, dx = xts.next()
            P.ld("sp", dx, xt, x[t * 128:(t + 1) * 128, :])
            x1, dx1 = x1s.next()
            for half in range(2):
                P.tt("dve", x1[:, half * 512:(half + 1) * 512], xt[:, half * 512:(half + 1) * 512], pss[half], ALU.add)
            P.st("sp", dx1, x1_d[t * 128:(t + 1) * 128, :], x1)
            return x1

        pss = wo_mm(0)
        x1 = wo_add(0, pss)
        for tt in range(4):
            t = t0 + tt
            if tt + 1 < 4:
                pss = wo_mm(tt + 1)
            hb = hbs.next()
            rmsnorm_to_bf16(x1, nwB, hb, hb, sts.next())
            pbank = bfb.next()
            pv = T(pbank.ap.rearrange("p (a b) -> p a b", a=8), pbank.b)
            for k in range(NKT):
                P.tr(pv[:, k, :], hb[:, k * 128:(k + 1) * 128], mb["ident"])
            P.copy("act", T(hT.ap[:, :, t * 128:(t + 1) * 128], hT_b[t // 4]), pv)
            if tt + 1 < 4:
                x1 = wo_add(tt + 1, pss)
    P.barrier()
    A.reset(base_mark)

    big, sm, bfb, bfs = psum_sets("p3b")
    nwB = A.alloc([128, D], F32, "nwBf")
    P.ld("sp", dsem("nwf"), nwB, L["nwf_d"][:, :])
    actT = A.alloc([128, NFT, 1024], BF16, "actT")
    act_b = [Buf("act%d" % j) for j in range(NFT)]
    wdn_sb = A.alloc([128, NFT, D], BF16, "wdn_sb")
    d_wd = dsem("wdn")
    for j in range(NFT):
        P.ld("pool", d_wd, wdn_sb[:, j, :], L["ffn_down"][j * 128:(j + 1) * 128, :])
    ust = [[A.alloc([128, 1026], F32, "ust%d_%d" % (i, r)) for r in range(2)] for i in range(2)]
    wups = [(A.alloc([128, NKT, 128], BF16, "wup%d" % i), dsem("wup%d" % i)) for i in range(3)]
    chunk_list = [(B_, j_, gv_) for B_ in range(4) for j_ in range(NFT) for gv_ in range(2)]
    loaded = [0]

    def prefetch_upto(i):
        while loaded[0] <= min(i, len(chunk_list) - 1):
            c = loaded[0]
            _, j_, gv_ = chunk_list[c]
            colw_ = gv_ * FFN + j_ * 128
            wt_, ds__ = wups[c % 3]
            P.ld("pool", ds__, wt_, L["ffn_up"][:, colw_:colw_ + 128].rearrange("(k p) c -> p k c", p=128))
            loaded[0] += 1

    cvt = [[[A.alloc([128, 512], F32, "cvt%d_%d_%d" % (i, hf, r)) for r in range(2)] for hf in range(2)] for i in range(2)]
    x1ts = Ring([(A.alloc([128, D], F32, "x1t%d" % i), dsem("x1t%d" % i)) for i in range(1)])
    d_xo = dsem("xo")
    junk = T(ust[0][0].ap[:, 0:D], ust[0][0].b)
    st = A.alloc([128, 4], F32, "stf")
    def ffn_proj(B, j):
        b0 = B * 1024
        for gv in range(2):
            ci = (B * NFT + j) * 2 + gv
            prefetch_upto(ci + 2)
            wt, ds_ = wups[ci % 3]
            u = ust[gv][j % 2]
            fi = gv * NFT + j
            for half in range(2):
                ps = big.next()
                blk = B * 2 + half
                for k in range(NKT):
                    P.mm(ps, wt[:, k, :], hT_blk(k, blk), start=(k == 0), stop=(k == NKT - 1))
                P.copy("act", u[:, 1 + half * 512: 1 + (half + 1) * 512], ps)
                P.act(cvt[gv][half][j % 2], ps, AF.Identity, bias=T(fcb.ap[:, fi:fi + 1], fcb.b),
                      scale=T(fcw.ap[:, fi, 1:2], fcw.b))
            for side, tok, c in ((0, b0 - 1, 0), (1, b0 + 1024, 1025)):
                if tok < 0 or tok >= S:
                    P.memset("pool", u[:, c:c + 1], 0.0)
                else:
                    ps = sm.next()
                    for k in range(NKT):
                        P.mm(ps[:, 0:1], wt[:, k, :], T(hT.ap[:, k, tok:tok + 1], hT_b[tok // 512]),
                             start=(k == 0), stop=(k == NKT - 1))
                    P.copy("act", u[:, c:c + 1], ps[:, 0:1])

    def ffn_post(B, j):
        for half in range(2):
            c0 = 1 + half * 512
            for gv in range(2):
                u = ust[gv][j % 2]
                fi = gv * NFT + j
                wc = lambda s_: T(fcw.ap[:, fi, s_:s_ + 1], fcw.b)
                cv = cvt[gv][half][j % 2]
                P.stt("dve", cv, u[:, c0 - 1:c0 + 511], wc(0), cv, ALU.mult, ALU.add)
                P.stt("dve", cv, u[:, c0 + 1:c0 + 513], wc(2), cv, ALU.mult, ALU.add)
            g_, v_ = cvt[0][half][j % 2], cvt[1][half][j % 2]
            P.act(g_, g_, AF.Silu)
            P.tt("pool", T(actT.ap[:, j, half * 512:(half + 1) * 512], [act_b[j]]), g_, v_, ALU.mult)

    for B in range(4):
        ffn_proj(B, 0)
        for j in range(NFT):
            if j + 1 < NFT:
                ffn_proj(B, j + 1)
            ffn_post(B, j)
        for tt in range(8):
            t = B * 8 + tt
            x1t, d_x1t = x1ts.next()
            P.ld("sp", d_x1t, x1t, x1_d[t * 128:(t + 1) * 128, :])
            for half in range(2):
                ps = big.next()
                for j in range(NFT):
                    P.mm(ps, T(actT.ap[:, j, tt * 128:(tt + 1) * 128], [act_b[j]]), wdn_sb[:, j, half * 512:(half + 1) * 512],
                         start=(j == 0), stop=(j == NFT - 1))
                P.tt("dve", x1t[:, half * 512:(half + 1) * 512], x1t[:, half * 512:(half + 1) * 512], ps, ALU.add)
            P.act(junk, x1t, AF.Square, accum=st[:, 0:1])
            P.act(st[:, 1:2], st[:, 0:1], AF.Sqrt, bias=epsc[:, 0:1], scale=1.0 / D)
            P.op("dve", lambda v: v.reciprocal(out=st.ap[:, 2:3], in_=st.ap[:, 1:2]), reads=st.b, writes=st.b)
            xo = T(ust[1][tt % 2].ap[:, 0:D], ust[1][tt % 2].b)
            P.stt("dve", xo, x1t, st[:, 2:3], nwB, ALU.mult, ALU.mult)
            P.st("sp", d_xo, out[t * 128:(t + 1) * 128, :], xo)


_NC_CACHE = {}


def _host_inputs(inp, b, cst):
    f = lambda n: np.ascontiguousarray(np.asarray(inp[n], dtype=np.float32))
    d = {}
    d["x"] = f("x")[b]
    d["w_in"] = f("w_in")[0]
    d["pool_w"] = f("pool_w")[0]
    d["pool_out"] = f("pool_out")[0]
    d["dn_out"] = f("dn_out")[0]
    d["w_o"] = f("w_o")[0]
    d["ffn_up"] = f("ffn_up")[0]
    d["ffn_down"] = f("ffn_down")[0]
    d["cw"] = np.ascontiguousarray(f("qkv_conv_w")[0].reshape(5, 24, 128).transpose(2, 1, 0).reshape(128, 120))
    d["fcw"] = np.ascontiguousarray(f("ffn_conv_w")[0].reshape(3, 44, 128).transpose(2, 1, 0).reshape(128, 132))
    d["fcb"] = np.ascontiguousarray(f("ffn_conv_b")[0].reshape(44, 128).T)
    d["psc"] = np.ascontiguousarray(f("pool_scale")[0].reshape(4, 128).T)
    d["nw1"] = np.ascontiguousarray(np.broadcast_to(f("norm1_w")[0], (128, 1024)))
    d["nw2"] = np.ascontiguousarray(np.broadcast_to(f("norm2_w")[0], (128, 1024)))
    d["nwf"] = np.ascontiguousarray(np.broadcast_to(f("final_norm_w"), (128, 1024)))
    d["dnw"] = np.ascontiguousarray(np.broadcast_to(f("dn_norm_w")[0], (128, 128)))
    d["alog"] = np.ascontiguousarray(np.broadcast_to(f("a_log")[0].reshape(16), (128, 16)))
    d["dtb"] = np.ascontiguousarray(np.broadcast_to(f("dt_bias")[0].reshape(16), (128, 16)))
    d["cst"] = cst
    return d


def kernel(**inputs):
    if "nc" not in _NC_CACHE:
        _NC_CACHE["nc"] = build_nc()
    nc = _NC_CACHE["nc"]
    cst = host_constants()
    nb = np.asarray(inputs["x"]).shape[0]
    in_maps = [_host_inputs(inputs, b, cst) for b in range(nb)]
    res = run_bass_kernel_spmd(nc, in_maps, core_ids=list(range(nb)))
    return np.stack([np.asarray(r["out"], dtype=np.float32) for r in res.results], axis=0)
```

```python
import numpy as np
from contextlib import ExitStack
import concourse.bass as bass
import concourse.mybir as mybir
from concourse.bass_utils import run_bass_kernel_spmd

F32 = mybir.dt.float32
BF16 = mybir.dt.bfloat16
AF = mybir.ActivationFunctionType
ALU = mybir.AluOpType
AX = mybir.AxisListType


class Buf:
    __slots__ = ("name", "w", "r", "excl")

    def __init__(self, name, excl=False):
        self.name = name
        self.excl = excl
        self.w = None
        self.r = {}


class Prog:
    ENG = ("pe", "act", "dve", "pool", "sp")

    def __init__(self, nc, es):
        self.nc = nc
        self.es = es
        self.h = {"pe": nc.tensor, "act": nc.scalar, "dve": nc.vector, "pool": nc.gpsimd, "sp": nc.sync}
        self.sem = {e: es.enter_context(nc.semaphore("s_" + e)) for e in self.ENG}
        self.cnt = {e: 0 for e in self.ENG}
        self.waited = {e: {} for e in self.ENG}
        self.q = {e: [] for e in self.ENG}
        self.dma_sems = []
        self.n_inst = 0
        self.n_wait = 0

    def new_dma_sem(self, name):
        s = self.es.enter_context(self.nc.semaphore(name))
        d = {"sem": s, "val": 0, "name": name}
        self.dma_sems.append(d)
        return d

    def _need(self, eng, deps, key, sem, val):
        if eng == "pe" and key == "pe":
            return
        if val > self.waited[eng].get(key, 0):
            prev = deps.get(key)
            if prev is None or prev[1] < val:
                deps[key] = (sem, val)

    def _collect(self, eng, reads, writes):
        deps = {}
        for b in reads:
            if b.w is not None:
                self._need(eng, deps, *b.w)
            if b.excl:
                for k, (sem, v) in b.r.items():
                    if k != eng:
                        self._need(eng, deps, k, sem, v)
        for b in writes:
            if b.w is not None:
                self._need(eng, deps, *b.w)
            for k, (sem, v) in b.r.items():
                self._need(eng, deps, k, sem, v)
        return deps

    def _emit_waits(self, eng, deps):
        h = self.h[eng]
        for key, (sem, val) in deps.items():
            self.waited[eng][key] = val
            self.n_wait += 1
            self.q[eng].append(lambda h=h, sem=sem, val=val: h.wait_ge(sem, val))

    def _mark(self, key, sem, val, reads, writes):
        for b in reads:
            p = b.r.get(key)
            if p is None or p[1] < val:
                b.r[key] = (sem, val)
        for b in writes:
            b.w = (key, sem, val)
            b.r = {}

    def op(self, eng, fn, reads=(), writes=()):
        self._emit_waits(eng, self._collect(eng, reads, writes))
        self.cnt[eng] += 1
        idx = self.cnt[eng]
        sem = self.sem[eng]
        h = self.h[eng]
        self.q[eng].append(lambda: fn(h).then_inc(sem, 1))
        self._mark(eng, sem, idx, reads, writes)
        self.n_inst += 1
        return idx

    def dma(self, eng, dsem, out, in_, reads=(), writes=()):
        self._emit_waits(eng, self._collect(eng, reads, writes))
        dsem["val"] += 16
        val = dsem["val"]
        h = self.h[eng]
        s = dsem["sem"]
        self.q[eng].append(lambda: h.dma_start(out=out, in_=in_).then_inc(s, 16))
        self._mark("dma:" + dsem["name"], s, val, reads, writes)
        self.n_inst += 1

    def wait_all_dma(self, eng):
        h = self.h[eng]
        for d in self.dma_sems:
            if d["val"] > 0:
                self.q[eng].append(lambda h=h, s=d["sem"], v=d["val"]: h.wait_ge(s, v))

    def barrier(self):
        for e in self.ENG:
            h = self.h[e]
            for e2 in self.ENG:
                if e2 != e and self.cnt[e2] > self.waited[e].get(e2, 0):
                    v = self.cnt[e2]
                    self.waited[e][e2] = v
                    self.q[e].append(lambda h=h, s=self.sem[e2], v=v: h.wait_ge(s, v))
            if self.cnt[e] > self.waited[e].get(e, 0):
                v = self.cnt[e]
                self.waited[e][e] = v
                self.q[e].append(lambda h=h, s=self.sem[e], v=v: h.wait_ge(s, v))
            for d in self.dma_sems:
                k = "dma:" + d["name"]
                if d["val"] > self.waited[e].get(k, 0):
                    v = d["val"]
                    self.waited[e][k] = v
                    self.q[e].append(lambda h=h, s=d["sem"], v=v: h.wait_ge(s, v))

    def flush(self):
        block = self.es.enter_context(self.nc.Block())
        q = self.q

        @block.tensor
        def _(t):
            for f in q["pe"]:
                f()

        @block.scalar
        def _(t):
            for f in q["act"]:
                f()

        @block.vector
        def _(t):
            for f in q["dve"]:
                f()

        @block.gpsimd
        def _(t):
            for f in q["pool"]:
                f()

        @block.sync
        def _(t):
            for f in q["sp"]:
                f()

    def mm(self, out, lhsT, rhs, start=True, stop=True):
        return self.op("pe", lambda t: t.matmul(out.ap, lhsT=lhsT.ap, rhs=rhs.ap, start=start, stop=stop),
                       reads=lhsT.b + rhs.b + ([] if start else out.b), writes=out.b)

    def tr(self, out, in_, ident):
        return self.op("pe", lambda t: t.transpose(out=out.ap, in_=in_.ap, identity=ident.ap),
                       reads=in_.b + ident.b, writes=out.b)

    def act(self, out, in_, func, bias=None, scale=1.0, accum=None, eng="act"):
        rd = list(in_.b)
        kw = {}
        if bias is not None:
            if isinstance(bias, T):
                rd += bias.b
                kw["bias"] = bias.ap
            else:
                kw["bias"] = bias
        if isinstance(scale, T):
            rd += scale.b
            kw["scale"] = scale.ap
        else:
            kw["scale"] = scale
        wr = list(out.b)
        if accum is not None:
            wr += accum.b
            kw["accum_out"] = accum.ap
        return self.op("act", lambda a: a.activation(out=out.ap, in_=in_.ap, func=func, **kw), reads=rd, writes=wr)

    def tt(self, eng, out, a, b, op):
        return self.op(eng, lambda v: v.tensor_tensor(out=out.ap, in0=a.ap, in1=b.ap, op=op),
                       reads=a.b + b.b, writes=out.b)

    def ts(self, eng, out, a, s1, op0, s2=None, op1=None):
        rd = list(a.b)
        v1 = s1
        if isinstance(s1, T):
            rd += s1.b
            v1 = s1.ap
        v2 = s2
        if isinstance(s2, T):
            rd += s2.b
            v2 = s2.ap
        if op1 is None:
            return self.op(eng, lambda v: v.tensor_scalar(out=out.ap, in0=a.ap, scalar1=v1, scalar2=None, op0=op0),
                           reads=rd, writes=out.b)
        return self.op(eng, lambda v: v.tensor_scalar(out=out.ap, in0=a.ap, scalar1=v1, scalar2=v2, op0=op0, op1=op1),
                       reads=rd, writes=out.b)

    def stt(self, eng, out, a, s, b, op0, op1):
        rd = a.b + b.b
        sv = s
        if isinstance(s, T):
            rd = rd + s.b
            sv = s.ap
        return self.op(eng, lambda v: v.scalar_tensor_tensor(out=out.ap, in0=a.ap, scalar=sv, in1=b.ap, op0=op0, op1=op1),
                       reads=rd, writes=out.b)

    def copy(self, eng, out, in_):
        if eng == "act":
            return self.op("act", lambda a: a.copy(out=out.ap, in_=in_.ap), reads=in_.b, writes=out.b)
        return self.op(eng, lambda v: v.tensor_copy(out=out.ap, in_=in_.ap), reads=in_.b, writes=out.b)

    def memset(self, eng, out, val):
        return self.op(eng, lambda v: v.memset(out.ap, val), writes=out.b)

    def ld(self, eng, dsem, out, in_ap):
        return self.dma(eng, dsem, out.ap, in_ap, writes=out.b)

    def st(self, eng, dsem, out_ap, in_):
        return self.dma(eng, dsem, out_ap, in_.ap, reads=in_.b)


class T:
    __slots__ = ("ap", "b")

    def __init__(self, ap, b):
        self.ap = ap
        self.b = b if isinstance(b, list) else [b]

    def __getitem__(self, k):
        return T(self.ap[k], self.b)

    def wb(self, b):
        return T(self.ap, b)


class Arena:
    def __init__(self, ap, nelem):
        self.ap = ap
        self.n = nelem
        self.off = 0

    def mark(self):
        return self.off

    def reset(self, m):
        self.off = m

    def alloc(self, shape, dt, name):
        n = int(np.prod(shape[1:]))
        units = n * (2 if dt == F32 else 1)
        if self.off % 2:
            self.off += 1
        a = self.ap[:, self.off:self.off + units]
        self.off += units
        assert self.off <= self.n, ("SBUF arena overflow", name, self.off, self.n)
        if dt == F32:
            a = a.bitcast(F32)
        if len(shape) == 3:
            a = a.rearrange("p (a b) -> p a b", a=shape[1])
        elif len(shape) == 4:
            a = a.rearrange("p (a b c) -> p a b c", a=shape[1], b=shape[2])
        return T(a, Buf(name))


S = 4096
D = 1024
NT = S // 128
NKT = D // 128
NH = 8
IN_DIM = 6688
OFF_P, OFF_Q, OFF_K, OFF_V, OFF_Z, OFF_BG, OFF_GP, OFF_GD = 0, 512, 1536, 2560, 3584, 4608, 4640, 5664
FFN = 2816
NFT = FFN // 128
POOL_WINDOWS = (2, 4, 8, 16)
EPS = 1e-6

MASKS = ("ident", "ones", "up_s", "lo_s", "up_i", "lo_i", "blk", "off_lo", "off_up")
BANDS = ("C", "L", "R", "CF", "CL")


def host_constants():
    p = np.arange(128)[:, None]
    f = np.arange(128)[None, :]
    m = {}
    m["ident"] = (p == f)
    m["ones"] = np.ones((128, 128), bool)
    m["up_s"] = f > p
    m["lo_s"] = f < p
    m["up_i"] = f >= p
    m["lo_i"] = f <= p
    m["blk"] = (p < 64) == (f < 64)
    m["off_lo"] = (p >= 64) & (f < 64)
    m["off_up"] = (p < 64) & (f >= 64)
    cols = [m[k].astype(np.float32) for k in MASKS]
    for w in POOL_WINDOWS:
        lo, hi = w // 2, w - 1 - w // 2
        mats = {k: np.zeros((128, 128), np.float32) for k in BANDS}
        for i in range(128):
            for tp in range(i - lo, i + hi + 1):
                if 0 <= tp < 128:
                    mats["C"][tp, i] += 1.0 / w
                elif tp < 0:
                    mats["L"][tp + 128, i] += 1.0 / w
                else:
                    mats["R"][tp - 128, i] += 1.0 / w
                a0 = max(i - lo, 0)
                cntf = (i + hi) - a0 + 1
                if 0 <= tp < 128:
                    mats["CF"][tp, i] += 1.0 / cntf
                b1 = min(i + hi, 127)
                cntl = b1 - (i - lo) + 1
                if 0 <= tp < 128:
                    mats["CL"][tp, i] += 1.0 / cntl
        for i in range(128):
            a0 = max(i - lo, 0)
            cnt = (i + hi) - a0 + 1
            for tp in range(a0, min(i + hi, 127) + 1):
                mats["CF"][tp, i] = 1.0 / cnt
            b1 = min(i + hi, 127)
            cnt = b1 - (i - lo) + 1
            for tp in range(max(i - lo, 0), b1 + 1):
                mats["CL"][tp, i] = 1.0 / cnt
        for k in ("C", "CF", "CL"):
            mats[k] -= np.eye(128, dtype=np.float32)
        cols += [mats[k] for k in BANDS]
    return np.ascontiguousarray(np.concatenate(cols, axis=1))


NCST = (len(MASKS) + 4 * len(BANDS)) * 128


class Ring:
    def __init__(self, items):
        self.items = items
        self.i = 0

    def next(self):
        it = self.items[self.i % len(self.items)]
        self.i += 1
        return it


def build_nc(dbg=(), nheads=NH, phase3=True, stop=None):
    nc = bass.Bass("TRN2", target_bir_lowering=False)

    def din(name, shape, dtype=F32):
        return nc.dram_tensor(name, list(shape), dtype, kind="ExternalInput").ap()

    x = din("x", [S, D])
    w_in = din("w_in", [D, IN_DIM])
    pool_w = din("pool_w", [4, 128, 128])
    pool_out = din("pool_out", [512, D])
    dn_out = din("dn_out", [D, D])
    w_o = din("w_o", [D, D])
    ffn_up = din("ffn_up", [D, 2 * FFN])
    ffn_down = din("ffn_down", [FFN, D])
    cw_d = din("cw", [128, 24 * 5])
    fcw_d = din("fcw", [128, 44 * 3])
    fcb_d = din("fcb", [128, 44])
    psc_d = din("psc", [128, 4])
    nw1_d = din("nw1", [128, D])
    nw2_d = din("nw2", [128, D])
    nwf_d = din("nwf", [128, D])
    dnw_d = din("dnw", [128, 128])
    alog_d = din("alog", [128, 16])
    dtb_d = din("dtb", [128, 16])
    cst_d = din("cst", [128, NCST])
    out = nc.dram_tensor("out", [S, D], F32, kind="ExternalOutput").ap()
    onT_d = nc.dram_tensor("onT_d", [NH, 128, S], BF16).ap()
    x1_d = nc.dram_tensor("x1_d", [S, D], F32).ap()
    dbg_out = {}
    for name, shape in dbg:
        dbg_out[name] = nc.dram_tensor("dbg_" + name, list(shape), F32, kind="ExternalOutput").ap()

    es = ExitStack()
    with es:
        E = es.enter_context
        P = Prog(nc, es)
        NU = 106400
        arena_t = E(nc.sbuf_tensor("arena", [128, NU], BF16))
        A = Arena(arena_t[:, :], NU)
        pbig_t = [E(nc.psum_tensor("pbig%d" % i, [128, 512], F32)) for i in range(2)]
        psm_t = [E(nc.psum_tensor("psm%d" % i, [128, 512], F32)) for i in range(4)]
        pbf_t = [E(nc.psum_tensor("pbf%d" % i, [128, 1024], BF16)) for i in range(2)]

        def psum_sets(tag, sm_on_big=False):
            bigb = [Buf("pbig%s%d" % (tag, i), True) for i in range(len(pbig_t))]
            big = Ring([T(t[:, :], bigb[i]) for i, t in enumerate(pbig_t)])
            smb = [Buf("psm%s%d" % (tag, i), True) for i in range(len(psm_t))]
            sm_t, sm_bufs = list(psm_t), list(smb)
            if sm_on_big:
                sm_t, sm_bufs = sm_t + list(pbig_t), sm_bufs + bigb
            sm = Ring([T(t[:, j * 128:(j + 1) * 128], sm_bufs[i]) for j in range(4) for i, t in enumerate(sm_t)])
            sm2 = Ring([(T(t[:, j * 256:(j + 1) * 256], sm_bufs[i]), T(t[:, j * 256:j * 256 + 128], sm_bufs[i]),
                         T(t[:, j * 256 + 128:(j + 1) * 256], sm_bufs[i])) for j in range(2) for i, t in enumerate(sm_t)])
            psum_sets.sm2 = sm2
            bfbuf = [Buf("pbf%s%d" % (tag, i), True) for i in range(len(pbf_t))]
            bfs = Ring([T(t[:, j * 128:(j + 1) * 128], bfbuf[i]) for j in range(8) for i, t in enumerate(pbf_t)])
            bfb = Ring([T(t[:, :], bfbuf[i]) for i, t in enumerate(pbf_t)])
            return big, sm, bfb, bfs

        dsem_i = [0]

        def dsem(name):
            dsem_i[0] += 1
            return P.new_dma_sem("%s_%d" % (name, dsem_i[0]))

        d_misc = dsem("misc")
        d_miscp = dsem("miscp")
        d_dbg = dsem("dbg")

        hT = A.alloc([128, NKT, S], BF16, "hT")
        hT_b = [Buf("hT%d" % i) for i in range(8)]

        def hT_blk(k, blk):
            return T(hT.ap[:, k, blk * 512:(blk + 1) * 512], hT_b[blk])

        def hT_tile(k, t):
            return T(hT.ap[:, k, t * 128:(t + 1) * 128], hT_b[t // 4])

        cstb = A.alloc([128, len(MASKS) * 128], BF16, "cstb")
        cstf = A.alloc([128, 4 * 128], F32, "cstf")
        P.ld("pool", d_miscp, cstb, cst_d[:, 0:len(MASKS) * 128])
        mb = {k: cstb[:, i * 128:(i + 1) * 128] for i, k in enumerate(MASKS)}
        for j, k in enumerate(("ident", "ones", "up_i", "lo_i")):
            i = MASKS.index(k)
            P.ld("sp", d_misc, cstf[:, j * 128:(j + 1) * 128], cst_d[:, i * 128:(i + 1) * 128])
        ident_f, ones_f, upi_f, loi_f = [cstf[:, j * 128:(j + 1) * 128] for j in range(4)]
        cw = A.alloc([128, 24, 5], F32, "cw")
        P.ld("sp", d_misc, cw, cw_d.rearrange("p (a b) -> p a b", a=24))
        fcw = A.alloc([128, 44, 3], F32, "fcw")
        P.ld("sp", d_misc, fcw, fcw_d.rearrange("p (a b) -> p a b", a=44))
        fcb = A.alloc([128, 44], F32, "fcb")
        P.ld("sp", d_misc, fcb, fcb_d[:, :])
        psc = A.alloc([128, 4], F32, "psc")
        P.ld("sp", d_misc, psc, psc_d[:, :])
        dnw = A.alloc([128, 128], F32, "dnw")
        P.ld("sp", d_misc, dnw, dnw_d[:, :])
        alog = A.alloc([128, 16], F32, "alog")
        P.ld("sp", d_misc, alog, alog_d[:, :])
        dtb = A.alloc([128, 16], F32, "dtb")
        P.ld("sp", d_misc, dtb, dtb_d[:, :])
        epsc = A.alloc([128, 1], F32, "epsc")
        P.memset("dve", epsc, EPS)
        mhalf = A.alloc([128, 1], F32, "mhalf")
        P.memset("dve", mhalf, -0.5)
        base_mark = A.mark()

        def dump(name, src):
            if name in dbg_out:
                P.st("sp", d_dbg, dbg_out[name], src)

        def rmsnorm_to_bf16(xt, nwB, hb, junk, st):
            P.act(junk, xt, AF.Square, accum=st[:, 0:1])
            P.act(st[:, 1:2], st[:, 0:1], AF.Sqrt, bias=epsc[:, 0:1], scale=1.0 / D)
            P.op("dve", lambda v: v.reciprocal(out=st.ap[:, 2:3], in_=st.ap[:, 1:2]), reads=st.b, writes=st.b)
            P.stt("dve", hb, xt, st[:, 2:3], nwB, ALU.mult, ALU.mult)

        def transpose_to_hT(hb, t, pbank, eng):
            pv = T(pbank.ap.rearrange("p (a b) -> p a b", a=8), pbank.b)
            for k in range(NKT):
                P.tr(pv[:, k, :], hb[:, k * 128:(k + 1) * 128], mb["ident"])
            dst = T(hT.ap[:, :, t * 128:(t + 1) * 128], hT_b[t // 4])
            P.copy(eng, dst, pv)

        big, sm, bfb, bfs = psum_sets("p1")
        nwB = A.alloc([128, D], F32, "nwB")
        P.ld("sp", d_misc, nwB, nw1_d[:, :])
        xts = Ring([A.alloc([128, D], F32, "xt%d" % i) for i in range(3)])
        xds = Ring([dsem("x") for i in range(3)])
        hbs = Ring([A.alloc([128, D], BF16, "hb%d" % i) for i in range(2)])
        junk = A.alloc([128, D], BF16, "junk")
        sts = Ring([A.alloc([128, 4], F32, "st%d" % i) for i in range(3)])
        def p1_norm(t):
            xt = xts.next()
            P.ld("sp", xds.next(), xt, x[t * 128:(t + 1) * 128, :])
            hb = hbs.next()
            rmsnorm_to_bf16(xt, nwB, hb, junk, sts.next())
            return hb

        hb_cur = p1_norm(0)
        for t in range(NT):
            hb_nxt = p1_norm(t + 1) if t + 1 < NT else None
            transpose_to_hT(hb_cur, t, bfb.next(), "act" if t % 2 else "dve")
            hb_cur = hb_nxt
        if "hT" in dbg_out:
            tmpf = A.alloc([128, S], F32, "dbg_hT")
            for k in range(NKT):
                P.copy("dve", tmpf, T(hT.ap[:, k, :], hT_b))
                P.st("sp", d_dbg, dbg_out["hT"][k], tmpf)
        if stop == 1:
            P.barrier()
            P.wait_all_dma("sp")
            P.flush()
            return nc
        P.barrier()
        A.reset(base_mark)

        big, sm, bfb, bfs = psum_sets("p20")
        GSH = [128, 2, NT, NH]
        beta = A.alloc(GSH, F32, "beta")
        Gc = A.alloc(GSH, F32, "Gc")
        GB = A.alloc(GSH, F32, "GB")
        eG = A.alloc(GSH, F32, "eG")
        bG = A.alloc(GSH, F32, "bG")
        eGd = A.alloc(GSH, F32, "eGd")
        gl = A.alloc(GSH, F32, "gl")
        invb = A.alloc(GSH, F32, "invb")
        p2_mark = A.mark()
        raw = A.alloc([128, NT, 32], F32, "raw")
        gg = A.alloc(GSH, F32, "gg")
        lnb = A.alloc(GSH, F32, "lnb")
        Gtot = A.alloc(GSH, F32, "Gtot")
        negA = A.alloc([128, 16], F32, "negA")
        wbg = A.alloc([128, NKT, 32], BF16, "wbg")
        P.ld("pool", d_miscp, wbg, w_in[:, OFF_BG:OFF_BG + 32].rearrange("(k p) c -> p k c", p=128))
        for t in range(NT):
            ps = sm.next()
            for k in range(NKT):
                P.mm(ps[:, 0:32], hT_tile(k, t), wbg[:, k, :], start=(k == 0), stop=(k == NKT - 1))
            P.copy("act" if t % 2 else "dve", raw[:, t, :], ps[:, 0:32])
        raw5 = T(raw.ap.rearrange("p t (k d h) -> p k d t h", k=2, d=2), raw.b)
        P.act(beta, raw5[:, 0], AF.Sigmoid)
        P.act(lnb, beta, AF.Ln)
        P.op("dve", lambda v: v.reciprocal(out=invb.ap, in_=beta.ap), reads=beta.b, writes=invb.b)
        P.act(negA, alog, AF.Exp)
        P.ts("dve", negA, negA, -1.0, ALU.mult)
        dtb4 = T(dtb.ap.rearrange("p (d h) -> p d h", d=2).unsqueeze(2).broadcast_to(GSH), dtb.b)
        negA4 = T(negA.ap.rearrange("p (d h) -> p d h", d=2).unsqueeze(2).broadcast_to(GSH), negA.b)
        P.tt("dve", gg, raw5[:, 1], dtb4, ALU.add)
        P.act(gg, gg, AF.Exp)
        P.act(gg, gg, AF.Ln, bias=1.0)
        P.tt("dve", gg, gg, negA4, ALU.mult)
        ggf = T(gg.ap.rearrange("p d t h -> p d (t h)"), gg.b)
        pc = big.next()
        P.mm(pc[:, 0:256], upi_f, ggf[:, 0], True, True)
        P.mm(pc[:, 256:512], loi_f, ggf[:, 1], True, True)
        P.copy("dve", T(Gc.ap.rearrange("p d t h -> p (d t h)"), Gc.b), pc)
        pt_ = big.next()
        P.mm(pt_, ones_f, T(gg.ap.rearrange("p d t h -> p (d t h)"), gg.b), True, True)
        P.copy("act", T(Gtot.ap.rearrange("p d t h -> p (d t h)"), Gtot.b), pt_)
        P.tt("dve", GB, Gc, lnb, ALU.add)
        P.act(eG, Gc, AF.Exp)
        P.tt("dve", bG, beta, eG, ALU.mult)
        P.tt("dve", eGd, Gtot, Gc, ALU.subtract)
        P.act(eGd, eGd, AF.Exp)
        P.act(gl, Gtot, AF.Exp)
        dump("raw", T(raw.ap.rearrange("p t c -> p (t c)"), raw.b))
        for nm, tt_ in (("beta", beta), ("Gc", Gc), ("gg", gg)):
            dump(nm, T(tt_.ap.rearrange("p d t h -> p (d t h)"), tt_.b))
        if stop == 2:
            P.barrier()
            P.wait_all_dma("sp")
            P.flush()
            return nc
        hT_d = nc.dram_tensor("hT_d", [8, 128, NKT, 512], BF16).ap()
        d_hst = dsem("hst")
        for blk in range(8):
            P.dma("sp", d_hst, hT_d[blk], hT.ap[:, :, blk * 512:(blk + 1) * 512], reads=[hT_b[blk]])
        P.barrier()
        A.reset(p2_mark)
        A2 = Arena(arena_t[:, 0:NKT * S], NKT * S)

        big, sm, bfb, bfs = psum_sets("p2", sm_on_big=True)
        PRE0 = 2
        NS = 4
        qkv_all = [A2.alloc([128, 3 * S], BF16, "qkv_all%d" % i) for i in range(2)]
        qkv_b = [[[Buf("%s%d_%d" % (nm, si, i)) for i in range(8)] for nm in ("qT", "kT", "vT")] for si in range(2)]
        zs_l = [A2.alloc([128, NT, 128], BF16, "zs%d" % i) for i in range(2)]
        zs_b = [[Buf("zs%d_%d" % (si, i)) for i in range(NT)] for si in range(2)]
        o_dir = [A.alloc([128, NT, 128], BF16, "o_f"), A.alloc([128, NT, 128], BF16, "o_b")]
        o_b = [[Buf("o%d_%d" % (d, i)) for i in range(NT)] for d in range(2)]
        d_w = [dsem("w%d" % i) for i in range(4)]
        d_on = dsem("on")
        Sst = [A.alloc([128, 128], F32, "S%d" % d) for d in range(2)]
        Sbf = [A.alloc([128, 128], BF16, "Sbf%d" % d) for d in range(2)]
        ssn = A.alloc([128, NT], F32, "ssn")
        junk128 = A.alloc([128, 128], BF16, "junk128")
        pre = A.alloc([128, S + 4], F32, "pre")
        pre_b = [Buf("pre%d" % i) for i in range(8)]
        accs = Ring([A.alloc([128, 512], F32, "acc%d" % i) for i in range(1)])
        sqs = Ring([A.alloc([128, 512], BF16, "sq%d" % i) for i in range(2)])
        rinvs = Ring([A.alloc([128, 512], F32, "rinv%d" % i) for i in range(1)])
        wqkvz = [A.alloc([128, NKT, 128], BF16, "w%s" % nm) for nm in "qkvz"]
        hring = Ring([(A.alloc([128, NKT, 512], BF16, "hblk%d" % i), dsem("hblk%d" % i)) for i in range(2)])

        def mk_shared(i):
            d = {}
            for nm in ("kbg", "kd", "vb", "KK", "QK"):
                d[nm] = A.alloc([128, 128], BF16, "%s_%d" % (nm, i))
            return d

        def mk_dir(i):
            d = {}
            for nm in ("u", "tmpo"):
                d[nm] = A.alloc([128, 128], F32, "%s_%d" % (nm, i))
            for nm in ("E2", "N", "Nd", "attnT", "Ld", "Lo", "MTa", "MTb", "Xd", "nT1",
                       "XT", "wT", "vnew"):
                d[nm] = A.alloc([128, 128], BF16, "%s_%d" % (nm, i))
            for ab in "ab":
                pair = A.alloc([128, 256], BF16, "MQ%s_%d" % (ab, i))
                d["M" + ab] = T(pair.ap[:, 0:128], Buf("M%s_%d" % (ab, i)))
                d["Q" + ab] = T(pair.ap[:, 128:256], Buf("Q%s_%d" % (ab, i)))
                d["MQ" + ab] = T(pair.ap, d["M" + ab].b + d["Q" + ab].b)
            return d

        shared_slots = [[mk_shared(d * NS + i) for i in range(NS)] for d in range(2)]
        dir_slots = [[mk_dir(d * NS + i) for i in range(NS)] for d in range(2)]
        cur = {"si": 0}

        def col(arr, d, t, h):
            return T(arr.ap[:, d, t, h:h + 1], arr.b)

        def qkv_T(si, Xi, lo, hi, bufs):
            return T(qkv_all[si].ap[:, Xi * S + lo: Xi * S + hi], bufs)

        def tile_of(Xi, t):
            si = cur["si"]
            return qkv_T(si, Xi, t * 128, (t + 1) * 128, [qkv_b[si][Xi][t // 4]])

        hstate = {"pending": None}

        def issue_hblk(blk):
            slot, ds_ = hring.next()
            P.dma("sp", ds_, slot.ap, hT_d[blk], writes=slot.b)
            hstate["pending"] = (blk, slot)

        def load_hblk(blk, nxt=None):
            if hstate["pending"] is None or hstate["pending"][0] != blk:
                issue_hblk(blk)
            slot = hstate["pending"][1]
            hstate["pending"] = None
            if nxt is not None:
                issue_hblk(nxt)
            return slot

        def conv_block(h, Xi, blk, si):
            acc = accs.next()
            rb = [pre_b[b_] for b_ in (blk - 1, blk, blk + 1) if 0 <= b_ < 8]
            for s_ in range(5):
                src = T(pre.ap[:, blk * 512 + s_: blk * 512 + s_ + 512], rb)
                wcol = T(cw.ap[:, Xi * 8 + h, s_:s_ + 1], cw.b)
                if s_ == 0:
                    P.ts("dve", acc, src, wcol, ALU.mult)
                else:
                    P.stt("dve", acc, src, wcol, acc, ALU.mult, ALU.add)
                if s_ % 2 == 1:
                    yield
            dst = qkv_T(si, Xi, blk * 512, (blk + 1) * 512, [qkv_b[si][Xi][blk]])
            rinv = rinvs.next()
            P.act(rinv, acc, AF.Exp, scale=-1.0)
            P.act(rinv, rinv, AF.Ln, bias=1.0)
            P.act(rinv, rinv, AF.Exp, scale=-1.0)
            yield
            if Xi == 2:
                P.tt("dve", dst, acc, rinv, ALU.mult)
                yield
            else:
                P.tt("dve", acc, acc, rinv, ALU.mult)
                yield
                sq = sqs.next()
                P.tt("pool", sq, acc, acc, ALU.mult)
                yield
                ps = big.next()
                P.mm(ps, mb["ones"], sq, True, True)
                P.act(rinv, ps, AF.Ln, bias=epsc[:, 0:1], scale=1.0)
                yield
                P.act(rinv, rinv, AF.Exp, scale=-0.5)
                yield
                if Xi == 0:
                    P.stt("dve", dst, acc, float(128 ** -0.5), rinv, ALU.mult, ALU.mult)
                else:
                    P.tt("dve", dst, acc, rinv, ALU.mult)
                yield

        def proj_gens(h, si):
            st = {"a": 0, "b": 0}

            def gen_a():
                for Xi, woff in enumerate((OFF_Q, OFF_K, OFF_V, OFF_Z)):
                    P.ld("pool", d_w[Xi], wqkvz[Xi],
                         w_in[:, woff + h * 128: woff + (h + 1) * 128].rearrange("(k p) c -> p k c", p=128))
                yield
                for Xi in range(3):
                    wX = wqkvz[Xi]
                    for blk in range(8):
                        i = 8 * Xi + blk
                        need = 0 if Xi == 0 else min(i - 6, 8 * Xi)
                        while st["b"] < need:
                            yield (lambda need=need: st["b"] >= need)
                        hb_ = load_hblk(blk, (blk + 1) % 8)
                        ps = big.next()
                        for k in range(NKT):
                            P.mm(ps, wX[:, k, :], hb_[:, k, :], start=(k == 0), stop=(k == NKT - 1))
                        P.copy("act", T(pre.ap[:, PRE0 + blk * 512: PRE0 + (blk + 1) * 512], [pre_b[blk]]), ps)
                        st["a"] = i + 1
                        yield
                while st["b"] < 24:
                    yield (lambda: st["b"] >= 24)
                wz = wqkvz[3]
                zflat = zs_l[si].ap.rearrange("p t e -> p (t e)")
                for blk in range(8):
                    hb_ = load_hblk(blk, blk + 1 if blk < 7 else None)
                    ps = big.next()
                    for k in range(NKT):
                        P.mm(ps, wz[:, k, :], hb_[:, k, :], start=(k == 0), stop=(k == NKT - 1))
                    acc = accs.next()
                    rinv = rinvs.next()
                    P.act(rinv, ps, AF.Exp, scale=-1.0)
                    P.act(acc, ps, AF.Copy)
                    yield
                    P.act(rinv, rinv, AF.Ln, bias=1.0)
                    P.act(rinv, rinv, AF.Exp, scale=-1.0)
                    yield
                    P.tt("dve", T(zflat[:, blk * 512:(blk + 1) * 512], zs_b[si][blk * 4:(blk + 1) * 4]), acc, rinv,
                         ALU.mult)
                    yield

            def gen_b():
                for Xi in range(3):
                    for blk in range(8):
                        i = 8 * Xi + blk
                        need = min(i + 2, 8 * Xi + 8)
                        while st["a"] < need:
                            yield (lambda need=need: st["a"] >= need)
                        yield from conv_block(h, Xi, blk, si)
                        st["b"] = i + 1

            return [gen_a(), gen_b()]

        def prep_tile(h, t, d, sh):
            ud = "up" if d == 0 else "lo"
            kt, qt, vt = tile_of(1, t), tile_of(0, t), tile_of(2, t)
            pk = bfs.next()
            P.tr(pk, kt, mb["ident"])
            P.act(sh["kbg"], pk, AF.Copy, scale=col(bG, d, t, h))
            P.ts("dve", sh["kd"], pk, col(eGd, d, t, h), ALU.mult)
            yield
            pv = bfs.next()
            P.tr(pv, vt, mb["ident"])
            P.act(sh["vb"], pv, AF.Copy, scale=col(beta, d, t, h))
            yield
            pboth, pqk, pkk = psum_sets.sm2.next()
            si_ = cur["si"]
            qk_rhs = T(qkv_all[si_].ap[:, 0:2 * S].rearrange("p (x s) -> p x s", x=2)[:, :, t * 128:(t + 1) * 128],
                       [qkv_b[si_][0][t // 4], qkv_b[si_][1][t // 4]])
            P.mm(T(pboth.ap.rearrange("p (x s) -> p x s", x=2), pboth.b), kt, qk_rhs, True, True)
            P.tt("dve", sh["QK"], pqk, mb[ud + "_i"], ALU.mult)
            P.tt("dve", sh["KK"], pkk, mb[ud + "_s"], ALU.mult)
            yield

        def local_gen(h, t, d, sh, ds):
            dg_, te_ = ds["tmpo"], ds["u"]
            P.ts("dve", dg_, ident_f, col(GB, d, t, h), ALU.mult)
            yield
            prow = sm.next()
            P.mm(prow, ones_f, dg_, True, True)
            P.ts("dve", te_, prow, col(Gc, d, t, h), ALU.subtract, 0.0, ALU.min)
            yield
            P.act(ds["E2"], te_, AF.Exp)
            yield
            P.tt("pool", ds["N"], sh["KK"], ds["E2"], ALU.mult)
            yield
            P.tt("pool", ds["attnT"], sh["QK"], ds["E2"], ALU.mult)
            P.tt("pool", ds["Nd"], ds["N"], mb["blk"], ALU.mult)
            yield
            pl = bfs.next()
            P.tr(pl, ds["N"], mb["ident"])
            P.tt("dve", ds["Ld"], pl, mb["blk"], ALU.mult)
            P.tt("dve", ds["Lo"], pl, mb["off_lo" if d == 0 else "off_up"], ALU.mult)
            yield
            P.tt("pool", ds["Qa"], mb["ident"], ds["Nd"], ALU.subtract)
            yield
            M, MT, Q = ds["Nd"], ds["Ld"], ds["Qa"]
            pa = sm.next()
            P.mm(pa, MT, M, True, True)
            P.copy("act", ds["Ma"], pa)
            yield
            pb = sm.next()
            P.mm(pb, M, MT, True, True)
            P.copy("act", ds["MTa"], pb)
            yield
            M, MT = ds["Ma"], ds["MTa"]
            for lvl in range(1, 6):
                nM, nMT = (ds["Mb"], ds["MTb"]) if lvl % 2 else (ds["Ma"], ds["MTa"])
                nQ = ds["Qb"] if lvl % 2 else ds["Qa"]
                if lvl < 5:
                    pboth, p_m, p_q = psum_sets.sm2.next()
                    rhs = ds["MQa"] if (lvl % 2 == 1) else ds["MQb"]
                    P.mm(pboth, MT, rhs, True, True)
                    P.copy("act", nM, p_m)
                    P.tt("dve", nQ, Q, p_q, ALU.add)
                    yield
                    pb = sm.next()
                    P.mm(pb, M, MT, True, True)
                    P.copy("act" if lvl % 2 else "dve", nMT, pb)
                    yield
                else:
                    pq = sm.next()
                    P.mm(pq, MT, Q, True, True)
                    P.tt("dve", nQ, Q, pq, ALU.add)
                    yield
                M, MT, Q = nM, nMT, nQ
            Y = Q
            px = bfs.next()
            P.tr(px, Y, mb["ident"])
            P.copy("act", ds["Xd"], px)
            yield
            p1 = sm.next()
            P.mm(p1, ds["Lo"], Y, True, True)
            P.act(ds["nT1"], p1, AF.Copy, scale=-1.0)
            yield
            p2 = sm.next()
            P.mm(p2, ds["Xd"], ds["nT1"], True, True)
            P.tt("dve", ds["XT"], Y, p2, ALU.add)
            yield
            pw = sm.next()
            P.mm(pw, sh["kbg"], ds["XT"], True, True)
            P.copy("act", ds["wT"], pw)
            yield
            pu = sm.next()
            P.mm(pu, ds["XT"], sh["vb"], True, True)
            P.copy("dve", ds["u"], pu)
            yield

        def scan_gen(h, t, d, sh, ds):
            qt = tile_of(0, t)
            pa = sm.next()
            P.mm(pa, ds["wT"], Sbf[d], True, True)
            po1 = sm.next()
            P.mm(po1, qt, Sbf[d], True, True)
            P.tt("dve", ds["vnew"], ds["u"], pa, ALU.subtract)
            P.act(ds["tmpo"], po1, AF.Copy, scale=col(eG, d, t, h))
            yield
            psn = sm.next()
            P.mm(psn, sh["kd"], ds["vnew"], True, True)
            P.stt("dve", Sst[d], Sst[d], T(gl.ap[:, d, t, h:h + 1], gl.b), psn, ALU.mult, ALU.add)
            yield
            P.copy("act", Sbf[d], Sst[d])
            yield
            po2 = sm.next()
            P.mm(po2, ds["attnT"], ds["vnew"], True, True)
            P.stt("dve", o_dir[d][:, t, :].wb(o_b[d][t]), po2, col(invb, d, t, h), ds["tmpo"], ALU.mult, ALU.add)
            yield

        def run_sched(gens):
            active = [(g, None) for g in gens]
            while active:
                progressed = False
                nxt = []
                for g, cond in active:
                    if cond is not None and not cond():
                        nxt.append((g, cond))
                        continue
                    try:
                        c = next(g)
                        nxt.append((g, c))
                    except StopIteration:
                        pass
                    progressed = True
                active = nxt
                assert progressed or not active, "scheduler deadlock"

        def chain_gens(h):
            gens = []
            for d in range(2):
                order = list(range(NT)) if d == 0 else list(range(NT - 1, -1, -1))
                state = {"local_done": [False] * NT, "scan_done": 0}

                def worker(w, d=d, order=order, state=state):
                    for idx in range(w, NT, NS):
                        t = order[idx]
                        while state["scan_done"] < idx - NS + 1:
                            yield (lambda idx=idx: state["scan_done"] >= idx - NS + 1)
                        sh = shared_slots[d][idx % NS]
                        ds = dir_slots[d][idx % NS]
                        yield from prep_tile(h, t, d, sh)
                        yield from local_gen(h, t, d, sh, ds)
                        state["local_done"][idx] = True

                def scanner(d=d, order=order, state=state):
                    for idx, t in enumerate(order):
                        while not state["local_done"][idx]:
                            yield (lambda idx=idx: state["local_done"][idx])
                        yield from scan_gen(h, t, d, shared_slots[d][idx % NS], dir_slots[d][idx % NS])
                        state["scan_done"] = idx + 1

                gens += [worker(w) for w in range(NS)] + [scanner()]
            return gens

        def norm_stage(h, si):
            kvb = qkv_b[si][1] + qkv_b[si][2]
            osum = T(qkv_all[si].ap[:, S:3 * S].bitcast(F32).rearrange("p (t e) -> p t e", t=NT), kvb)
            onT_sb = T(qkv_all[si].ap[:, 0:S], qkv_b[si][0])
            zst = T(zs_l[si].ap, zs_b[si])
            P.tt("dve", osum, o_dir[0].wb(o_b[0]), o_dir[1].wb(o_b[1]), ALU.add)
            for t in range(NT):
                P.act(junk128, osum[:, t, :], AF.Square, accum=ssn[:, t:t + 1])
            P.act(ssn, ssn, AF.Ln, bias=epsc[:, 0:1], scale=1.0 / 128)
            P.act(ssn, ssn, AF.Exp, scale=-0.5)
            P.tt("dve", osum, osum, T(ssn.ap.unsqueeze(2).broadcast_to([128, NT, 128]), ssn.b), ALU.mult)
            P.tt("pool", osum, osum, T(dnw.ap.unsqueeze(1).broadcast_to([128, NT, 128]), dnw.b), ALU.mult)
            if h == 0:
                dump("on0", T(osum.ap.rearrange("p t e -> p (t e)"), osum.b))
            on_tok = o_dir[0].wb(o_b[0])
            P.copy("act", on_tok, osum)
            zflat = T(zs_l[si].ap.rearrange("p t e -> p (t e)"), zs_b[si])
            for t8 in range(4):
                pbk = bfb.next()
                pv = T(pbk.ap.rearrange("p (a b) -> p a b", a=8), pbk.b)
                for j in range(8):
                    P.tr(pv[:, j, :], on_tok[:, t8 * 8 + j, :], mb["ident"])
                P.tt("dve", T(onT_sb.ap[:, t8 * 1024:(t8 + 1) * 1024], onT_sb.b), pbk,
                     zflat[:, t8 * 1024:(t8 + 1) * 1024], ALU.mult)
            P.st("sp", d_on, onT_d[h], onT_sb)

        P.memset("pool", T(pre.ap[:, 0:PRE0], [pre_b[0]]), 0.0)
        P.memset("pool", T(pre.ap[:, PRE0 + S:PRE0 + S + 2], [pre_b[7]]), 0.0)
        run_sched(proj_gens(0, 0))
        for h in range(nheads):
            si = h % 2
            cur["si"] = si
            for d in range(2):
                P.memset("dve", Sst[d], 0.0)
                P.memset("pool", Sbf[d], 0.0)
            gens = chain_gens(h)
            if h + 1 < nheads:
                gens += proj_gens(h + 1, 1 - si)
            run_sched(gens)
            norm_stage(h, si)
        P.barrier()
        d_hld = dsem("hld")
        for blk in range(8):
            P.dma("sp", d_hld, hT.ap[:, :, blk * 512:(blk + 1) * 512], hT_d[blk], writes=[hT_b[blk]])
        P.barrier()
        A.reset(base_mark)
        if phase3:
            build_phase3(nc, P, A, locals())
        P.barrier()
        P.wait_all_dma("sp")
        P.flush()
    return nc


def build_phase3(nc, P, A, L):
    hT, hT_b, hT_blk, hT_tile = L["hT"], L["hT_b"], L["hT_blk"], L["hT_tile"]
    mb, cst_d, epsc, dsem, psum_sets = L["mb"], L["cst_d"], L["epsc"], L["dsem"], L["psum_sets"]
    x, out, w_in, x1_d, onT_d = L["x"], L["out"], L["w_in"], L["x1_d"], L["onT_d"]
    psc, fcw, fcb = L["psc"], L["fcw"], L["fcb"]
    base_mark = L["base_mark"]
    rmsnorm_to_bf16, dump = L["rmsnorm_to_bf16"], L["dump"]
    NB = len(MASKS) * 128

    big, sm, bfb, bfs = psum_sets("p3a")
    d_c = dsem("c3a")
    nwB = A.alloc([128, D], F32, "nwB2")
    P.ld("sp", d_c, nwB, L["nw2_d"][:, :])
    bands = A.alloc([128, 20, 128], BF16, "bands")
    P.ld("pool", dsem("bands"), bands, cst_d[:, NB:NB + 20 * 128].rearrange("p (a b) -> p a b", a=20))
    wp = A.alloc([128, NKT, 512], BF16, "wp")
    P.ld("pool", dsem("wp"), wp, w_in[:, OFF_P:OFF_P + 512].rearrange("(k p) c -> p k c", p=128))
    poolw = A.alloc([128, 4, 128], BF16, "poolw")
    P.ld("pool", dsem("poolw"), poolw, L["pool_w"].rearrange("g c d -> c g d"))
    wo_sb = A.alloc([128, NKT, D], BF16, "wo_sb")
    d_wo = dsem("wo")
    for k in range(NKT):
        P.ld("pool", d_wo, wo_sb[:, k, :], L["w_o"][k * 128:(k + 1) * 128, :])
    onTb = A.alloc([128, NH, 512], BF16, "onTb")
    d_onb = dsem("onb")
    p_tok = A.alloc([128, 8, 512], BF16, "p_tok")
    ptok_b = [Buf("ptok%d" % i) for i in range(8)]
    pm0T = A.alloc([128, 4, 512], BF16, "pm0T")
    pmT = A.alloc([128, 4, 512], BF16, "pmT")
    W_gp = A.alloc([128, NKT, D], BF16, "W_gp")
    W_gd = A.alloc([128, NKT, D], BF16, "W_gd")
    W_dn = A.alloc([128, NKT, D], BF16, "W_dn")
    W_po = A.alloc([128, 4, D], BF16, "W_po")
    d_W = [dsem("W%d" % i) for i in range(4)]
    for k in range(NKT):
        P.ld("pool", d_W[0], W_gp[:, k, :], w_in[k * 128:(k + 1) * 128, OFF_GP:OFF_GP + D])
        P.ld("pool", d_W[1], W_gd[:, k, :], w_in[k * 128:(k + 1) * 128, OFF_GD:OFF_GD + D])
        P.ld("pool", d_W[2], W_dn[:, k, :], L["dn_out"][k * 128:(k + 1) * 128, :])
        if k < 4:
            P.ld("pool", d_W[3], W_po[:, k, :], L["pool_out"][k * 128:(k + 1) * 128, :])
    tmpa = Ring([A.alloc([128, 512], F32, "tmpa%d" % i) for i in range(1)])
    tmpb = Ring([A.alloc([128, 512], F32, "tmpb%d" % i) for i in range(1)])
    mT = A.alloc([128, NKT, 512], BF16, "mT")
    xts = Ring([(A.alloc([128, D], F32, "xt3_%d" % i), dsem("xt3_%d" % i)) for i in range(1)])
    x1s = Ring([(A.alloc([128, D], F32, "x1_%d" % i), dsem("x1s_%d" % i)) for i in range(1)])
    hbs = Ring([A.alloc([128, D], BF16, "hb3_%d" % i) for i in range(1)])
    junk = None
    sts = Ring([A.alloc([128, 4], F32, "st3_%d" % i) for i in range(2)])
    ptok_done = set()

    def ensure_ptok(t):
        if t in ptok_done:
            return
        ptok_done.add(t)
        ps = big.next()
        for k in range(NKT):
            P.mm(ps, hT_tile(k, t), wp[:, k, :], start=(k == 0), stop=(k == NKT - 1))
        P.copy("act", T(p_tok.ap[:, t % 8, :], [ptok_b[t % 8]]), ps)

    def ptok(t, g):
        return T(p_tok.ap[:, t % 8, g * 128:(g + 1) * 128], [ptok_b[t % 8]])

    def band(g, kind):
        return bands[:, g * 5 + BANDS.index(kind), :]

    for b in range(8):
        t0 = b * 4
        P.ld("sp", d_onb, onTb, onT_d[:, :, b * 512:(b + 1) * 512].rearrange("h p s -> p h s"))
        for t in range(max(t0 - 1, 0), min(t0 + 5, NT)):
            ensure_ptok(t)
        for tt in range(4):
            t = t0 + tt
            for g in range(4):
                ps = sm.next()
                ck = "CF" if t == 0 else ("CL" if t == NT - 1 else "C")
                terms = [(ptok(t, g), band(g, ck))]
                if t > 0:
                    terms.append((ptok(t - 1, g), band(g, "L")))
                if t < NT - 1:
                    terms.append((ptok(t + 1, g), band(g, "R")))
                for i, (l_, r_) in enumerate(terms):
                    P.mm(ps, l_, r_, start=(i == 0), stop=(i == len(terms) - 1))
                P.copy("act" if g % 2 else "dve", pm0T[:, g, tt * 128:(tt + 1) * 128], ps)
        for g in range(4):
            ps = big.next()
            P.mm(ps, poolw[:, g, :], pm0T[:, g, :], True, True)
            P.ts("dve", pmT[:, g, :], ps, T(psc.ap[:, g:g + 1], psc.b), ALU.mult)
        for n in range(NKT):
            nsl = slice(n * 128, (n + 1) * 128)
            wgp, wpo, wgd, wdn = W_gp[:, :, nsl], W_po[:, :, nsl], W_gd[:, :, nsl], W_dn[:, :, nsl]
            ta, tb = tmpa.next(), tmpb.next()
            ps = big.next()
            for k in range(NKT):
                P.mm(ps, wgp[:, k, :], hT_blk(k, b), start=(k == 0), stop=(k == NKT - 1))
            P.act(ta, ps, AF.Sigmoid)
            ps = big.next()
            for k in range(4):
                P.mm(ps, wpo[:, k, :], pmT[:, k, :], start=(k == 0), stop=(k == 3))
            P.tt("dve", ta, ta, ps, ALU.mult)
            ps = big.next()
            for k in range(NKT):
                P.mm(ps, wgd[:, k, :], hT_blk(k, b), start=(k == 0), stop=(k == NKT - 1))
            P.act(tb, ps, AF.Sigmoid)
            ps = big.next()
            for k in range(NKT):
                P.mm(ps, wdn[:, k, :], onTb[:, k, :], start=(k == 0), stop=(k == NKT - 1))
            P.tt("dve", tb, tb, ps, ALU.mult)
            P.tt("dve", mT[:, n, :], ta, tb, ALU.add)
        def wo_mm(tt):
            pss = []
            for half in range(2):
                ps = big.next()
                for k in range(NKT):
                    P.mm(ps, mT[:, k, tt * 128:(tt + 1) * 128], wo_sb[:, k, half * 512:(half + 1) * 512],
                         start=(k == 0), stop=(k == NKT - 1))
                pss.append(ps)
            return pss

        def wo_add(tt, pss):
            t = t0 + tt
            xt, dx = xts.next()
            P.ld("sp", dx, xt, x[t * 128:(t + 1) * 128, :])
            x1, dx1 = x1s.next()
            for half in range(2):
                P.tt("dve", x1[:, half * 512:(half + 1) * 512], xt[:, half * 512:(half + 1) * 512], pss[half], ALU.add)
            P.st("sp", dx1, x1_d[t * 128:(t + 1) * 128, :], x1)
            return x1

        pss = wo_mm(0)
        x1 = wo_add(0, pss)
        for tt in range(4):
            t = t0 + tt
            if tt + 1 < 4:
                pss = wo_mm(tt + 1)
            hb = hbs.next()
            rmsnorm_to_bf16(x1, nwB, hb, hb, sts.next())
            pbank = bfb.next()
            pv = T(pbank.ap.rearrange("p (a b) -> p a b", a=8), pbank.b)
            for k in range(NKT):
                P.tr(pv[:, k, :], hb[:, k * 128:(k + 1) * 128], mb["ident"])
            P.copy("act", T(hT.ap[:, :, t * 128:(t + 1) * 128], hT_b[t // 4]), pv)
            if tt + 1 < 4:
                x1 = wo_add(tt + 1, pss)
    P.barrier()
    A.reset(base_mark)

    big, sm, bfb, bfs = psum_sets("p3b")
    nwB = A.alloc([128, D], F32, "nwBf")
    P.ld("sp", dsem("nwf"), nwB, L["nwf_d"][:, :])
    actT = A.alloc([128, NFT, 1024], BF16, "actT")
    act_b = [Buf("act%d" % j) for j in range(NFT)]
    wdn_sb = A.alloc([128, NFT, D], BF16, "wdn_sb")
    d_wd = dsem("wdn")
    for j in range(NFT):
        P.ld("pool", d_wd, wdn_sb[:, j, :], L["ffn_down"][j * 128:(j + 1) * 128, :])
    ust = [[A.alloc([128, 1026], F32, "ust%d_%d" % (i, r)) for r in range(2)] for i in range(2)]
    wups = [(A.alloc([128, NKT, 128], BF16, "wup%d" % i), dsem("wup%d" % i)) for i in range(3)]
    chunk_list = [(B_, j_, gv_) for B_ in range(4) for j_ in range(NFT) for gv_ in range(2)]
    loaded = [0]

    def prefetch_upto(i):
        while loaded[0] <= min(i, len(chunk_list) - 1):
            c = loaded[0]
            _, j_, gv_ = chunk_list[c]
            colw_ = gv_ * FFN + j_ * 128
            wt_, ds__ = wups[c % 3]
            P.ld("pool", ds__, wt_, L["ffn_up"][:, colw_:colw_ + 128].rearrange("(k p) c -> p k c", p=128))
            loaded[0] += 1

    cvt = [[[A.alloc([128, 512], F32, "cvt%d_%d_%d" % (i, hf, r)) for r in range(2)] for hf in range(2)] for i in range(2)]
    x1ts = Ring([(A.alloc([128, D], F32, "x1t%d" % i), dsem("x1t%d" % i)) for i in range(1)])
    d_xo = dsem("xo")
    junk = T(ust[0][0].ap[:, 0:D], ust[0][0].b)
    st = A.alloc([128, 4], F32, "stf")
    def ffn_proj(B, j):
        b0 = B * 1024
        for gv in range(2):
            ci = (B * NFT + j) * 2 + gv
            prefetch_upto(ci + 2)
            wt, ds_ = wups[ci % 3]
            u = ust[gv][j % 2]
            fi = gv * NFT + j
            for half in range(2):
                ps = big.next()
                blk = B * 2 + half
                for k in range(NKT):
                    P.mm(ps, wt[:, k, :], hT_blk(k, blk), start=(k == 0), stop=(k == NKT - 1))
                P.copy("act", u[:, 1 + half * 512: 1 + (half + 1) * 512], ps)
                P.act(cvt[gv][half][j % 2], ps, AF.Identity, bias=T(fcb.ap[:, fi:fi + 1], fcb.b),
                      scale=T(fcw.ap[:, fi, 1:2], fcw.b))
            if b0 > 0 and b0 + 1024 < S:
                ps = sm.next()
                hb2 = [hT_b[(b0 - 1) // 512], hT_b[(b0 + 1024) // 512]]
                for k in range(NKT):
                    P.mm(ps[:, 0:2], wt[:, k, :], T(hT.ap[:, k, b0 - 1:b0 + 1025:1025], hb2),
                         start=(k == 0), stop=(k == NKT - 1))
                P.copy("act", T(u.ap[:, 0:1026:1025], u.b), ps[:, 0:2])
            else:
                for side, tok, c in ((0, b0 - 1, 0), (1, b0 + 1024, 1025)):
                    if tok < 0 or tok >= S:
                        P.memset("pool", u[:, c:c + 1], 0.0)
                    else:
                        ps = sm.next()
                        for k in range(NKT):
                            P.mm(ps[:, 0:1], wt[:, k, :], T(hT.ap[:, k, tok:tok + 1], hT_b[tok // 512]),
                                 start=(k == 0), stop=(k == NKT - 1))
                        P.copy("act", u[:, c:c + 1], ps[:, 0:1])

    def ffn_post(B, j):
        for half in range(2):
            c0 = 1 + half * 512
            for gv in range(2):
                u = ust[gv][j % 2]
                fi = gv * NFT + j
                wc = lambda s_: T(fcw.ap[:, fi, s_:s_ + 1], fcw.b)
                cv = cvt[gv][half][j % 2]
                P.stt("dve", cv, u[:, c0 - 1:c0 + 511], wc(0), cv, ALU.mult, ALU.add)
                P.stt("dve", cv, u[:, c0 + 1:c0 + 513], wc(2), cv, ALU.mult, ALU.add)
            g_, v_ = cvt[0][half][j % 2], cvt[1][half][j % 2]
            P.act(g_, g_, AF.Silu)
            P.tt("pool", T(actT.ap[:, j, half * 512:(half + 1) * 512], [act_b[j]]), g_, v_, ALU.mult)

    for B in range(4):
        ffn_proj(B, 0)
        for j in range(NFT):
            if j + 1 < NFT:
                ffn_proj(B, j + 1)
            ffn_post(B, j)
        for tt in range(8):
            t = B * 8 + tt
            x1t, d_x1t = x1ts.next()
            P.ld("sp", d_x1t, x1t, x1_d[t * 128:(t + 1) * 128, :])
            for half in range(2):
                ps = big.next()
                for j in range(NFT):
                    P.mm(ps, T(actT.ap[:, j, tt * 128:(tt + 1) * 128], [act_b[j]]), wdn_sb[:, j, half * 512:(half + 1) * 512],
                         start=(j == 0), stop=(j == NFT - 1))
                P.tt("dve", x1t[:, half * 512:(half + 1) * 512], x1t[:, half * 512:(half + 1) * 512], ps, ALU.add)
            P.act(junk, x1t, AF.Square, accum=st[:, 0:1])
            P.act(st[:, 1:2], st[:, 0:1], AF.Sqrt, bias=epsc[:, 0:1], scale=1.0 / D)
            P.op("dve", lambda v: v.reciprocal(out=st.ap[:, 2:3], in_=st.ap[:, 1:2]), reads=st.b, writes=st.b)
            xo = T(ust[1][tt % 2].ap[:, 0:D], ust[1][tt % 2].b)
            P.stt("dve", xo, x1t, st[:, 2:3], nwB, ALU.mult, ALU.mult)
            P.st("sp", d_xo, out[t * 128:(t + 1) * 128, :], xo)


_NC_CACHE = {}


def _host_inputs(inp, b, cst):
    f = lambda n: np.ascontiguousarray(np.asarray(inp[n], dtype=np.float32))
    d = {}
    d["x"] = f("x")[b]
    d["w_in"] = f("w_in")[0]
    d["pool_w"] = f("pool_w")[0]
    d["pool_out"] = f("pool_out")[0]
    d["dn_out"] = f("dn_out")[0]
    d["w_o"] = f("w_o")[0]
    d["ffn_up"] = f("ffn_up")[0]
    d["ffn_down"] = f("ffn_down")[0]
    d["cw"] = np.ascontiguousarray(f("qkv_conv_w")[0].reshape(5, 24, 128).transpose(2, 1, 0).reshape(128, 120))
    d["fcw"] = np.ascontiguousarray(f("ffn_conv_w")[0].reshape(3, 44, 128).transpose(2, 1, 0).reshape(128, 132))
    d["fcb"] = np.ascontiguousarray(f("ffn_conv_b")[0].reshape(44, 128).T)
    d["psc"] = np.ascontiguousarray(f("pool_scale")[0].reshape(4, 128).T)
    d["nw1"] = np.ascontiguousarray(np.broadcast_to(f("norm1_w")[0], (128, 1024)))
    d["nw2"] = np.ascontiguousarray(np.broadcast_to(f("norm2_w")[0], (128, 1024)))
    d["nwf"] = np.ascontiguousarray(np.broadcast_to(f("final_norm_w"), (128, 1024)))
    d["dnw"] = np.ascontiguousarray(np.broadcast_to(f("dn_norm_w")[0], (128, 128)))
    d["alog"] = np.ascontiguousarray(np.broadcast_to(f("a_log")[0].reshape(16), (128, 16)))
    d["dtb"] = np.ascontiguousarray(np.broadcast_to(f("dt_bias")[0].reshape(16), (128, 16)))
    d["cst"] = cst
    return d


def kernel(**inputs):
    if "nc" not in _NC_CACHE:
        _NC_CACHE["nc"] = build_nc()
    nc = _NC_CACHE["nc"]
    cst = host_constants()
    nb = np.asarray(inputs["x"]).shape[0]
    in_maps = [_host_inputs(inputs, b, cst) for b in range(nb)]
    res = run_bass_kernel_spmd(nc, in_maps, core_ids=list(range(nb)))
    return np.stack([np.asarray(r["out"], dtype=np.float32) for r in res.results], axis=0)
```
